# Optimizing a Trainium2 kernel written in Bass

```python
import jax, jax.numpy as jnp
from jax import lax

D_MODEL = 2048
BATCH = 2
SEQ = 4096
DEPTH = 2
DEC_BATCH = 32
DEC_SEQ = 16
PAST_LEN = 4096

CHUNK = 64
Q_BLOCK = 2 * CHUNK
N_A = DEPTH // 2
N_B = DEPTH - N_A
D_RNN = 2 * D_MODEL
N_GATE_BLOCKS = 16
GATE_BLOCK = D_RNN // N_GATE_BLOCKS
CONV_W = 4
LRU_C = 8.0
N_HEADS = 16
HEAD_DIM = D_MODEL // N_HEADS
D_ATT = N_HEADS * HEAD_DIM
EPS = 1e-6

kernel_name = 'hawk_stickbreak_yoco_stream_step'


def rms_norm(x, g):
    xf = x.astype(jnp.float32)
    y = xf * lax.rsqrt(jnp.mean(xf * xf, axis=-1, keepdims=True) + EPS)
    return (y * g.astype(jnp.float32)).astype(x.dtype)


def ada_mod(c, w, b):
    m = jax.nn.silu(c) @ w + b
    shift, scale, gate = jnp.split(m, 3, axis=-1)
    return shift[:, None, :], scale[:, None, :], gate[:, None, :]


def causal_conv(xb, buf, w, b):
    T = xb.shape[1]
    xp = jnp.concatenate([buf.astype(xb.dtype), xb], axis=1)
    y = b + sum(xp[:, i:i + T] * w[i] for i in range(CONV_W))
    return y, xp[:, -(CONV_W - 1):]


def block_diag(x, w, b):
    xb = x.reshape(x.shape[:-1] + (N_GATE_BLOCKS, GATE_BLOCK))
    return jnp.einsum('btnk,nkj->btnj', xb, w).reshape(x.shape) + b


def rg_lru(x, h0, w_r, b_r, w_i, b_i, lam):
    xf = x.astype(jnp.float32)
    r = jax.nn.sigmoid(block_diag(x, w_r, b_r).astype(jnp.float32))
    i = jax.nn.sigmoid(block_diag(x, w_i, b_i).astype(jnp.float32))
    log_a = -LRU_C * r * jax.nn.softplus(-lam.astype(jnp.float32))
    a = jnp.exp(log_a)
    u = jnp.sqrt(-jnp.expm1(2.0 * log_a)) * (i * xf)

    def step(h, au):
        a_t, u_t = au
        h = a_t * h + u_t
        return h, h

    hT, hs = lax.scan(step, h0.astype(jnp.float32), (jnp.swapaxes(a, 0, 1), jnp.swapaxes(u, 0, 1)))
    return jnp.swapaxes(hs, 0, 1).astype(x.dtype), hT


def recurrent_mixer(hn, h0, conv0, w_in, conv_w, conv_b, w_r, b_r, w_i, b_i, lam, w_out):
    xb, zg = jnp.split(hn @ w_in, 2, axis=-1)
    xb, conv_new = causal_conv(xb, conv0, conv_w, conv_b)
    y, hT = rg_lru(xb, h0, w_r, b_r, w_i, b_i, lam)
    return (y * jax.nn.silu(zg)) @ w_out, hT, conv_new


def sb_block(q, q_pos, k, v):
    z = jnp.einsum('bqhd,bshd->bhqs', q, k).astype(jnp.float32) * (HEAD_DIM ** -0.5)
    mask = jnp.arange(k.shape[1])[None, :] < q_pos[:, None]
    log_keep = jnp.where(mask, jax.nn.log_sigmoid(-z), 0.0)
    log_after = lax.cumsum(log_keep, axis=3, reverse=True) - log_keep
    attn = jnp.where(mask, jnp.exp(jax.nn.log_sigmoid(z) + log_after), 0.0)
    o = jnp.einsum('bhqs,bshd->bqhd', attn, v.astype(jnp.float32))
    return o.astype(q.dtype)


def stick_breaking(q, k, v):
    B, T = q.shape[0], q.shape[1]
    q_pos = (k.shape[1] - T) + jnp.arange(T)
    if T <= Q_BLOCK:
        return sb_block(q, q_pos, k, v)
    nb = T // Q_BLOCK
    qb = q.reshape(B, nb, Q_BLOCK, N_HEADS, HEAD_DIM).transpose(1, 0, 2, 3, 4)
    pb = q_pos.reshape(nb, Q_BLOCK)
    ob = lax.map(lambda a: sb_block(a[0], a[1], k, v), (qb, pb))
    return ob.transpose(1, 0, 2, 3, 4).reshape(B, T, N_HEADS, HEAD_DIM)


def stick_breaking_mixer(hn, k_all, v_all, w_in, w_out):
    B, T = hn.shape[0], hn.shape[1]
    q, zg = jnp.split(hn @ w_in, 2, axis=-1)
    o = stick_breaking(q.reshape(B, T, N_HEADS, HEAD_DIM), k_all, v_all)
    return (o.reshape(B, T, D_ATT) * jax.nn.silu(zg)) @ w_out


def shared_kv(x, g_kv, w_kv):
    B, T = x.shape[0], x.shape[1]
    k, v = jnp.split(rms_norm(x, g_kv) @ w_kv, 2, axis=-1)
    return k.reshape(B, T, N_HEADS, HEAD_DIM), v.reshape(B, T, N_HEADS, HEAD_DIM)


def run_group(x, c, h0, conv0, past_k, past_v, g_pre, g_post, w_ada, b_ada, w_in_a, conv_w, conv_b,
              w_rgate, b_rgate, w_igate, b_igate, lru_lambda, w_out_a, g_kv, w_kv, w_in_b, w_out_b):
    hs, convs = [], []
    k_new = v_new = k_all = v_all = None
    for l in range(DEPTH):
        shift, scale, gate = ada_mod(c, w_ada[l], b_ada[l])
        hn = rms_norm(x, g_pre[l]) * (1.0 + scale) + shift
        if l < N_A:
            out, hT, cb = recurrent_mixer(hn, h0[l], conv0[l], w_in_a[l], conv_w[l], conv_b[l],
                                          w_rgate[l], b_rgate[l], w_igate[l], b_igate[l],
                                          lru_lambda[l], w_out_a[l])
            hs.append(hT)
            convs.append(cb)
        else:
            if l == N_A:
                k_new, v_new = shared_kv(x, g_kv, w_kv)
                k_all = jnp.concatenate([past_k.astype(k_new.dtype), k_new], axis=1)
                v_all = jnp.concatenate([past_v.astype(v_new.dtype), v_new], axis=1)
            j = l - N_A
            out = stick_breaking_mixer(hn, k_all, v_all, w_in_b[j], w_out_b[j])
        x = x + gate * rms_norm(out, g_post[l])
    return x, k_new, v_new, jnp.stack(hs), jnp.stack(convs)


def setup_inputs(seed: int = 0) -> dict:
    key = jax.random.key(seed)
    ks = jax.random.split(key, 32)

    def nrm(k, shape, s):
        return jax.random.normal(k, shape, jnp.float32) * s

    u = jax.random.uniform(ks[19], (N_A, D_RNN), jnp.float32, 0.9, 0.999)
    a0 = u ** (1.0 / LRU_C)
    return {
        'x_prompt': nrm(ks[0], (BATCH, SEQ, D_MODEL), 1.0),
        'x_sample': nrm(ks[1], (DEC_BATCH, DEC_SEQ, D_MODEL), 1.0),
        'c_prompt': nrm(ks[2], (BATCH, D_MODEL), 1.0),
        'c_sample': nrm(ks[3], (DEC_BATCH, D_MODEL), 1.0),
        'cache_k': nrm(ks[4], (DEC_BATCH, PAST_LEN, N_HEADS, HEAD_DIM), 1.0),
        'cache_v': nrm(ks[5], (DEC_BATCH, PAST_LEN, N_HEADS, HEAD_DIM), 1.0),
        'state_lru': nrm(ks[6], (N_A, DEC_BATCH, D_RNN), 0.5),
        'state_conv': nrm(ks[7], (N_A, DEC_BATCH, CONV_W - 1, D_RNN), 1.0),
        'g_pre': 1.0 + nrm(ks[8], (DEPTH, D_MODEL), 0.05),
        'g_post': 1.0 + nrm(ks[9], (DEPTH, D_MODEL), 0.05),
        'w_ada': nrm(ks[10], (DEPTH, D_MODEL, 3 * D_MODEL), 0.5 * D_MODEL ** -0.5),
        'b_ada': nrm(ks[11], (DEPTH, 3 * D_MODEL), 0.01),
        'w_in_a': nrm(ks[12], (N_A, D_MODEL, 2 * D_RNN), D_MODEL ** -0.5),
        'conv_w': nrm(ks[13], (N_A, CONV_W, D_RNN), CONV_W ** -0.5),
        'conv_b': nrm(ks[14], (N_A, D_RNN), 0.01),
        'w_rgate': nrm(ks[15], (N_A, N_GATE_BLOCKS, GATE_BLOCK, GATE_BLOCK), GATE_BLOCK ** -0.5),
        'b_rgate': nrm(ks[16], (N_A, D_RNN), 0.01),
        'w_igate': nrm(ks[17], (N_A, N_GATE_BLOCKS, GATE_BLOCK, GATE_BLOCK), GATE_BLOCK ** -0.5),
        'b_igate': nrm(ks[18], (N_A, D_RNN), 0.01),
        'lru_lambda': jnp.log(a0) - jnp.log1p(-a0),
        'w_out_a': nrm(ks[20], (N_A, D_RNN, D_MODEL), D_RNN ** -0.5),
        'g_kv': 1.0 + nrm(ks[21], (D_MODEL,), 0.05),
        'w_kv': nrm(ks[22], (D_MODEL, 2 * D_ATT), D_MODEL ** -0.5),
        'w_in_b': nrm(ks[23], (N_B, D_MODEL, 2 * D_ATT), D_MODEL ** -0.5),
        'w_out_b': nrm(ks[24], (N_B, D_ATT, D_MODEL), D_ATT ** -0.5),
    }


def reference(x_prompt, x_sample, c_prompt, c_sample, cache_k, cache_v, state_lru, state_conv,
              g_pre, g_post, w_ada, b_ada, w_in_a, conv_w, conv_b, w_rgate, b_rgate, w_igate, b_igate,
              lru_lambda, w_out_a, g_kv, w_kv, w_in_b, w_out_b):
    B = x_prompt.shape[0]
    h0_p = jnp.zeros((N_A, B, D_RNN), jnp.float32)
    conv0_p = jnp.zeros((N_A, B, CONV_W - 1, D_RNN), x_prompt.dtype)
    past0 = jnp.zeros((B, 0, N_HEADS, HEAD_DIM), x_prompt.dtype)
    y_prompt, k_prompt, v_prompt, lru_prompt, conv_prompt = run_group(
        x_prompt, c_prompt, h0_p, conv0_p, past0, past0, g_pre, g_post, w_ada, b_ada, w_in_a,
        conv_w, conv_b, w_rgate, b_rgate, w_igate, b_igate, lru_lambda, w_out_a, g_kv, w_kv,
        w_in_b, w_out_b)
    y_sample, k_sample, v_sample, lru_sample, conv_sample = run_group(
        x_sample, c_sample, state_lru, state_conv, cache_k, cache_v, g_pre, g_post, w_ada, b_ada,
        w_in_a, conv_w, conv_b, w_rgate, b_rgate, w_igate, b_igate, lru_lambda, w_out_a, g_kv,
        w_kv, w_in_b, w_out_b)
    return (y_prompt, y_sample, k_prompt, v_prompt, k_sample, v_sample,
            lru_prompt, lru_sample, conv_prompt, conv_sample)
```

```python
import numpy as np
import concourse.bass as bass
import concourse.mybir as mybir
from concourse.bass_utils import run_bass_kernel_spmd

F32 = mybir.dt.float32
BF16 = mybir.dt.bfloat16
AF = mybir.ActivationFunctionType
ALU = mybir.AluOpType

D = 2048
NPT = 4096
NST = 256
NT = NPT + NST
NTT = NT // 128
EPS = 1e-6
GROUPS = [[0, 1, 2, 3], [4, 5, 6, 7]]
SCALE = 128 ** -0.5


class Buf:
    __slots__ = ("name", "w", "r", "excl")

    def __init__(self, name="", excl=False):
        self.name = name
        self.w = None
        self.r = {}
        self.excl = excl


class _Eng:
    def __init__(self, fw, name, eng, same_sync):
        self.fw = fw
        self.name = name
        self.eng = eng
        self.epoch = 0
        self.count = 0
        self.sem = fw.nc.alloc_semaphore(name=f"es_{name}_0")
        self.key = (name, 0)
        fw.sems[self.key] = self.sem
        self.seen = {}
        self.same_sync = same_sync
        self.dslots = []
        self.dnext = 0

    def tick(self, inst):
        if self.count >= 30000:
            self.epoch += 1
            self.count = 0
            self.sem = self.fw.nc.alloc_semaphore(name=f"es_{self.name}_{self.epoch}")
            self.key = (self.name, self.epoch)
            self.fw.sems[self.key] = self.sem
        self.count += 1
        inst.then_inc(self.sem, 1)
        return (self.key, self.count)


class FW:
    def __init__(self, nc, same_sync=True):
        self.nc = nc
        self.sems = {}
        self.pe = _Eng(self, "pe", nc.tensor, False)
        self.act = _Eng(self, "act", nc.scalar, same_sync)
        self.dve = _Eng(self, "dve", nc.vector, same_sync)
        self.pool = _Eng(self, "pool", nc.gpsimd, same_sync)
        self.sp = _Eng(self, "sp", nc.sync, False)
        self.engs = [self.pe, self.act, self.dve, self.pool, self.sp]
        for q, n in ((self.sp, 16), (self.pool, 12)):
            for i in range(n):
                key = ("d" + q.name, i)
                self.sems[key] = nc.alloc_semaphore(name=f"ds_{q.name}_{i}")
                q.dslots.append([key, 0, None])
        self.n_cc = 0
        self.cc_tickets = []

    def _wait(self, E, ticket):
        key, val = ticket
        if key[0] == E.name and not E.same_sync:
            return
        if E.seen.get(key, 0) >= val:
            return
        E.eng.wait_ge(self.sems[key], val)
        E.seen[key] = val

    def _deps(self, E, reads, writes):
        deps = {}

        def add(t):
            if t is None:
                return
            k, v = t
            if deps.get(k, 0) < v:
                deps[k] = v
        for b in reads:
            add(b.w)
            if b.excl:
                for k, v in b.r.items():
                    if k[0] != E.name:
                        add((k, v))
        for b in writes:
            add(b.w)
            for k, v in b.r.items():
                add((k, v))
        for k, v in deps.items():
            self._wait(E, (k, v))

    def _commit(self, ticket, reads, writes):
        k, v = ticket
        for b in reads:
            if b.r.get(k, 0) < v:
                b.r[k] = v
        for b in writes:
            b.w = ticket
            b.r = {}

    def op(self, E, emit, reads=(), writes=()):
        self._deps(E, reads, writes)
        inst = emit(E.eng)
        t = E.tick(inst)
        self._commit(t, reads, writes)
        return t

    def dma(self, Q, out, in_, reads=(), writes=(), **kw):
        slot = Q.dslots[Q.dnext % len(Q.dslots)]
        Q.dnext += 1
        if slot[2] is not None:
            self._wait(Q, slot[2])
        self._deps(Q, reads, writes)
        inst = Q.eng.dma_start(out=out, in_=in_, **kw)
        slot[1] += 1
        t = (slot[0], 16 * slot[1])
        inst.then_inc(self.sems[slot[0]], 16)
        slot[2] = t
        self._commit(t, reads, writes)
        return t

    def collective(self, kind, in_ap, out_ap, reads=(), writes=()):
        Q = self.pool
        self._deps(Q, reads, writes)
        key = ("cc", self.n_cc)
        self.n_cc += 1
        self.sems[key] = self.nc.alloc_semaphore(name=f"cc_{key[1]}")
        inst = Q.eng.collective_compute(kind, ALU.bypass, replica_groups=GROUPS,
                                        ins=[in_ap], outs=[out_ap])
        inst.then_inc(self.sems[key], 1)
        t = (key, 1)
        self.cc_tickets.append(t)
        self._commit(t, reads, writes)
        return t

    def barrier(self):
        tickets = []
        for E in self.engs:
            if E.count > 0:
                tickets.append((E.key, E.count))
            for s in E.dslots:
                if s[2] is not None:
                    tickets.append(s[2])
        tickets += self.cc_tickets
        for E in self.engs:
            for t in tickets:
                if t[0][0] == E.name:
                    continue
                self._wait(E, t)


class Pool:
    def __init__(self, es, nc, name, shape, dtype, n):
        self.tiles = []
        for i in range(n):
            t = es.enter_context(nc.sbuf_tensor(f"p_{name}_{i}", list(shape), dtype))
            self.tiles.append((t, Buf(f"{name}_{i}")))
        self.i = 0

    def next(self):
        t = self.tiles[self.i % len(self.tiles)]
        self.i += 1
        return t


from contextlib import ExitStack


def build_nc(stages=9):
    nc = bass.Bass("TRN2", target_bir_lowering=False)

    in_names = []
    _NC_CACHE["in_names"] = in_names

    def din(name, shape, dt=F32):
        in_names.append(name)
        return nc.dram_tensor(name, list(shape), dt, kind="ExternalInput").ap()

    def dout(name, shape, dt=F32):
        return nc.dram_tensor(name, list(shape), dt, kind="ExternalOutput").ap()

    def dscr(name, shape, dt, local=False):
        if local:
            return nc.dram_tensor(name, list(shape), dt, kind="Internal", addr_space="Local").ap()
        return nc.dram_tensor(name, list(shape), dt, kind="Internal").ap()

    xg = din("xg", [NT, D])
    xg_sl = din("xg_sl", [NT, 512])
    cg = din("cg", [17, D])
    w_ada0 = din("w_ada0", [D, 4096])
    b_ada0 = din("b_ada0", [4096])
    w_adas = din("w_adas", [D, 2048])
    b_adas = din("b_adas", [2048])
    g_pre0 = din("g_pre0", [D])
    g_sl = din("g_sl", [4, 512])
    w_in_a = din("w_in_a", [D, 2048])
    conv_w = din("conv_w", [4, 1024])
    chp = din("chp", [4, 1024])
    w_rg = din("w_rg", [4, 256, 256])
    w_ig = din("w_ig", [4, 256, 256])
    st_in = din("st_in", [64, 1024])
    w_out_a = din("w_out_a", [4096, 512])
    w_kv = din("w_kv", [D, 1024])
    w_in_b = din("w_in_b", [D, 1024])
    w_out_b = din("w_out_b", [D, 512])
    if stages >= 7:
        cache_k = din("cache_k", [16, 4096, 512])
        cache_v = din("cache_v", [16, 4096, 512])
    ident_in = din("ident", [128, 128])
    tri_in = din("tri", [128, 128])
    m16_in = din("m16", [128, 16])
    sel_in = din("sel", [17, 3, 128])

    y_sl = dout("y_sl", [NT, 512])
    k_sl = dout("k_sl", [NT, 512])
    v_sl = dout("v_sl", [NT, 512])
    lru_sl = dout("lru_sl", [17, 1024])
    conv_sl = dout("conv_sl", [51, 1024])

    hn0_scr = dscr("hn0_scr", [D, NT], BF16)
    TBN = [512] * 8 + [256]
    y0_src = [dscr(f"y0_src{tb}", [1024, TBN[tb]], BF16) for tb in range(9)]
    y0_all = [dscr(f"y0_all{tb}", [4096, TBN[tb]], BF16, local=True) for tb in range(9)]
    st_src = [dscr(f"st_src{i}", [128, NTT], F32) for i in range(3)]
    st_all = [dscr(f"st_all{i}", [512, NTT], F32, local=True) for i in range(3)]
    x1_scr = dscr("x1_scr", [NT, 512], F32)
    hn_src = [dscr(f"hn_src{tb}", [1024, TBN[tb]], BF16) for tb in range(9)]
    hn_all = [dscr(f"hn_all{tb}", [4096, TBN[tb]], BF16, local=True) for tb in range(9)]
    qT_scr = dscr("qT_scr", [4, 128, NT], BF16)
    kT_scr = dscr("kT_scr", [4, 128, NT], BF16)
    sz_scr = dscr("sz_scr", [4, 128, NT], BF16)
    v_scr = dscr("v_scr", [NT, 512], BF16)
    y1_src = [dscr(f"y1_src{tb}", [512, TBN[tb]], BF16) for tb in range(9)]
    y1_all = [dscr(f"y1_all{tb}", [2048, TBN[tb]], BF16, local=True) for tb in range(9)]

    fw = FW(nc)
    pe, act, dve, pool, sp = fw.pe, fw.act, fw.dve, fw.pool, fw.sp
    d_hn0 = Buf()
    d_y0s = [Buf() for _ in range(9)]
    d_y0a = [Buf() for _ in range(9)]
    d_sts = [Buf() for _ in range(3)]
    d_sta = [Buf() for _ in range(3)]
    d_x1 = Buf()
    d_hns = [Buf() for _ in range(9)]
    d_hna = [Buf() for _ in range(9)]
    d_y1s = [Buf() for _ in range(9)]
    d_y1a = [Buf() for _ in range(9)]
    d_qT, d_kT, d_sz, d_v = Buf(), Buf(), Buf(), Buf()
    d_out = [Buf() for _ in range(5)]

    glob = ExitStack()

    def sb(es, name, shape, dt):
        return es.enter_context(nc.sbuf_tensor("s_" + name, list(shape), dt))

    psb = [(nc.alloc_psum_tensor(f"ps{i}", [128, 512], F32), Buf(f"ps{i}", excl=True)) for i in range(8)]
    ps_i = [0]

    ps_banks = [list(range(8))]

    def ps_next():
        bl = ps_banks[0]
        t = psb[bl[ps_i[0] % len(bl)]]
        ps_i[0] += 1
        return t

    ident_f = sb(glob, "ident_f", [128, 128], F32); b_idf = Buf()
    ident_b = sb(glob, "ident_b", [128, 128], BF16); b_idb = Buf()
    tri = sb(glob, "tri", [128, 128], F32); b_tri = Buf()
    m16 = sb(glob, "m16", [128, 16], F32); b_m16 = Buf()
    sel = sb(glob, "sel", [17, 3, 128], F32); b_sel = Buf()
    ones_c = sb(glob, "ones_c", [128, 1], F32); b_ones = Buf()
    GG0bc = sb(glob, "GG0bc", [128, 3, 512], F32); b_GG0 = Buf()
    A1bc = sb(glob, "A1bc", [128, 3, 512], F32); b_A1 = Buf()
    B1bc = sb(glob, "B1bc", [128, 3, 512], F32); b_B1 = Buf()
    GG1bc = sb(glob, "GG1bc", [128, 3, 512], F32); b_GG1 = Buf()
    GKVbc = sb(glob, "GKVbc", [128, 512], F32); b_GKV = Buf()
    stat = sb(glob, "stat", [128, 8, NTT], F32)
    b_stat = [Buf() for _ in range(8)]

    fw.dma(sp, ident_f[:], ident_in[:, :], writes=[b_idf])
    fw.dma(sp, tri[:], tri_in[:, :], writes=[b_tri])
    fw.dma(sp, m16[:], m16_in[:, :], writes=[b_m16])
    fw.dma(sp, sel[:], sel_in[:, :, :], writes=[b_sel])
    fw.dma(sp, GKVbc[:], g_sl[3, :].partition_broadcast(128), writes=[b_GKV])
    fw.op(act, lambda e: e.activation(out=ident_b[:], in_=ident_f[:], func=AF.Copy), reads=[b_idf], writes=[b_idb])
    fw.op(pool, lambda e: e.memset(ones_c[:], 1.0), writes=[b_ones])

    def rg_of(tile):
        return 0 if tile < 32 else tile - 31

    def rstd_from(ss_i, r_i, ncols=NTT):
        fw.op(dve, lambda e: e.tensor_scalar(out=stat[:, r_i, 0:ncols], in0=stat[:, ss_i, 0:ncols], scalar1=1.0 / D,
                                             scalar2=EPS, op0=ALU.mult, op1=ALU.add),
              reads=[b_stat[ss_i]], writes=[b_stat[r_i]])
        fw.op(act, lambda e: e.activation(out=stat[:, r_i, 0:ncols], in_=stat[:, r_i, 0:ncols], func=AF.Sqrt),
              reads=[b_stat[r_i]], writes=[b_stat[r_i]])
        fw.op(dve, lambda e: e.reciprocal(out=stat[:, r_i, 0:ncols], in_=stat[:, r_i, 0:ncols]),
              reads=[b_stat[r_i]], writes=[b_stat[r_i]])

    with ExitStack() as es:
        A0bc = sb(es, "A0bc", [128, 3, D], F32); b_A0 = Buf()
        B0bc = sb(es, "B0bc", [128, 3, D], F32); b_B0 = Buf()
        with ExitStack() as es0:
            cgt = sb(es0, "cgt", [17, D], F32); b_cgt = Buf()
            scT = sb(es0, "scT", [128, 16, 17], BF16); b_scT = Buf()
            modrow = sb(es0, "modrow", [17, 6144], F32); b_mod = Buf()
            bbp = Pool(es0, nc, "bbc", [17, 512], F32, 2)
            g0bc = sb(es0, "g0bc", [17, D], F32); b_g0 = Buf()
            gslbc = sb(es0, "gslbc", [17, 3, 512], F32); b_gsl = Buf()
            rows = sb(es0, "rows", [17, 4, 512], F32); b_rows = Buf()
            A0row = sb(es0, "A0row", [17, D], F32); b_A0r = Buf()
            wap = Pool(es0, nc, "wap", [128, 16, 512], BF16, 2)

            fw.dma(sp, cgt[:], cg[:, :], writes=[b_cgt])
            fw.dma(sp, g0bc[:], g_pre0.partition_broadcast(17), writes=[b_g0])
            for i in range(3):
                fw.dma(sp, gslbc[:, i, :], g_sl[i, :].partition_broadcast(17), writes=[b_gsl])
            fw.op(act, lambda e: e.activation(out=cgt[:], in_=cgt[:], func=AF.Silu), reads=[b_cgt], writes=[b_cgt])
            pt, pb = ps_next()
            for kt in range(16):
                fw.op(pe, lambda e: e.transpose(pt[:, kt * 17:(kt + 1) * 17], cgt[0:17, kt * 128:(kt + 1) * 128],
                                                ident_f[0:17, 0:17]),
                      reads=[b_cgt, b_idf], writes=[pb])
            fw.op(act, lambda e: e.activation(out=scT[:].rearrange("p a b -> p (a b)"), in_=pt[:, 0:272], func=AF.Copy),
                  reads=[pb], writes=[b_scT])
            for cc in range(12):
                wt, wb = wap.next()
                src = (w_ada0 if cc < 8 else w_adas).rearrange("(kt p) c -> p kt c", p=128)
                c0 = (cc % 8) * 512 if cc < 8 else (cc - 8) * 512
                fw.dma(pool, wt[:], src[:, :, c0:c0 + 512], writes=[wb])
                bbc, b_bbc = bbp.next()
                fw.dma(sp, bbc[:], (b_ada0 if cc < 8 else b_adas)[c0:c0 + 512].partition_broadcast(17), writes=[b_bbc])
                pt, pb = ps_next()
                for kt in range(16):
                    fw.op(pe, lambda e: e.matmul(pt[0:17, :], lhsT=scT[:, kt, :], rhs=wt[:, kt, :],
                                                 start=(kt == 0), stop=(kt == 15)),
                          reads=[b_scT, wb], writes=[pb])
                fw.op(dve, lambda e: e.tensor_tensor(out=modrow[:, cc * 512:(cc + 1) * 512], in0=pt[0:17, :],
                                                     in1=bbc[:, :], op=ALU.add),
                      reads=[pb, b_bbc], writes=[b_mod])
            fw.op(dve, lambda e: e.scalar_tensor_tensor(out=A0row[:], in0=modrow[:, 2048:4096], scalar=1.0, in1=g0bc[:],
                                                        op0=ALU.add, op1=ALU.mult),
                  reads=[b_mod, b_g0], writes=[b_A0r])
            fw.op(dve, lambda e: e.tensor_tensor(out=rows[:, 0, :], in0=modrow[:, 4096:4608], in1=gslbc[:, 0, :], op=ALU.mult),
                  reads=[b_mod, b_gsl], writes=[b_rows])
            fw.op(dve, lambda e: e.scalar_tensor_tensor(out=rows[:, 1, :], in0=modrow[:, 5120:5632], scalar=1.0,
                                                        in1=gslbc[:, 1, :], op0=ALU.add, op1=ALU.mult),
                  reads=[b_mod, b_gsl], writes=[b_rows])
            fw.op(dve, lambda e: e.tensor_tensor(out=rows[:, 2, :], in0=modrow[:, 5632:6144], in1=gslbc[:, 2, :], op=ALU.mult),
                  reads=[b_mod, b_gsl], writes=[b_rows])
            jobs = []
            for rg in range(3):
                for c4 in range(4):
                    jobs.append((A0row[:, c4 * 512:(c4 + 1) * 512], b_A0r, A0bc[:, rg, c4 * 512:(c4 + 1) * 512], b_A0, rg))
                    jobs.append((modrow[:, c4 * 512:(c4 + 1) * 512], b_mod, B0bc[:, rg, c4 * 512:(c4 + 1) * 512], b_B0, rg))
                jobs.append((rows[:, 0, :], b_rows, GG0bc[:, rg, :], b_GG0, rg))
                jobs.append((rows[:, 1, :], b_rows, A1bc[:, rg, :], b_A1, rg))
                jobs.append((modrow[:, 4608:5120], b_mod, B1bc[:, rg, :], b_B1, rg))
                jobs.append((rows[:, 2, :], b_rows, GG1bc[:, rg, :], b_GG1, rg))
            for ji, (src_ap, src_b, dst_ap, dst_b, rg) in enumerate(jobs):
                pt, pb = ps_next()
                fw.op(pe, lambda e: e.matmul(pt[:, :], lhsT=sel[:, rg, :], rhs=src_ap, start=True, stop=True),
                      reads=[b_sel, src_b], writes=[pb])
                E = act if ji % 2 == 0 else dve
                if E is act:
                    fw.op(act, lambda e: e.activation(out=dst_ap, in_=pt[:, :], func=AF.Copy), reads=[pb], writes=[dst_b])
                else:
                    fw.op(dve, lambda e: e.tensor_copy(out=dst_ap, in_=pt[:, :]), reads=[pb], writes=[dst_b])
            fw.barrier()

        with ExitStack() as es1:
            xtp = Pool(es1, nc, "xt", [128, D], F32, 3)
            junk = sb(es1, "junk", [128, D], BF16); b_junk = Buf()
            hnp = Pool(es1, nc, "hn", [128, D], BF16, 2)
            stg = Pool(es1, nc, "hstg", [128, 16, 512], BF16, 2)
            tmpp = Pool(es1, nc, "tmp1", [128, 4], F32, 4)
            for tb in range(9):
                n128 = 4 if tb < 8 else 2
                st_t, st_b = stg.next()
                for j in range(n128):
                    tile = tb * 4 + j
                    rg = rg_of(tile)
                    xt, xb_ = xtp.next()
                    tmp1, b_tmp1 = tmpp.next()
                    fw.dma(sp, xt[:], xg[tile * 128:(tile + 1) * 128, :], writes=[xb_])
                    fw.op(act, lambda e: e.activation(out=junk[:], in_=xt[:], func=AF.Square, accum_out=tmp1[:, 0:1]),
                          reads=[xb_], writes=[b_junk, b_tmp1])
                    fw.op(dve, lambda e: e.tensor_scalar(out=tmp1[:, 1:2], in0=tmp1[:, 0:1], scalar1=1.0 / D, scalar2=EPS,
                                                         op0=ALU.mult, op1=ALU.add), reads=[b_tmp1], writes=[b_tmp1])
                    fw.op(act, lambda e: e.activation(out=tmp1[:, 2:3], in_=tmp1[:, 1:2], func=AF.Sqrt),
                          reads=[b_tmp1], writes=[b_tmp1])
                    fw.op(dve, lambda e: e.reciprocal(out=tmp1[:, 3:4], in_=tmp1[:, 2:3]), reads=[b_tmp1], writes=[b_tmp1])
                    fw.op(dve, lambda e: e.scalar_tensor_tensor(out=xt[:], in0=xt[:], scalar=tmp1[:, 3:4], in1=A0bc[:, rg, :],
                                                                op0=ALU.mult, op1=ALU.mult),
                          reads=[xb_, b_tmp1, b_A0], writes=[xb_])
                    hn, hb = hnp.next()
                    fw.op(pool, lambda e: e.tensor_tensor(out=hn[:], in0=xt[:], in1=B0bc[:, rg, :], op=ALU.add),
                          reads=[xb_, b_B0], writes=[hb])
                    for half in range(2):
                        pt, pb = ps_next()
                        ptb = pt.bitcast(BF16)
                        for k8 in range(8):
                            kt = half * 8 + k8
                            fw.op(pe, lambda e: e.transpose(ptb[:, k8 * 128:(k8 + 1) * 128], hn[:, kt * 128:(kt + 1) * 128],
                                                            ident_b[:]),
                                  reads=[hb, b_idb], writes=[pb])
                        dst = st_t[:, half * 8:(half + 1) * 8, j * 128:(j + 1) * 128]
                        srcp = ptb[:, :].rearrange("p (a b) -> p a b", a=8)
                        if half == 0:
                            fw.op(act, lambda e: e.activation(out=dst, in_=srcp, func=AF.Copy), reads=[pb], writes=[st_b])
                        else:
                            fw.op(dve, lambda e: e.tensor_copy(out=dst, in_=srcp), reads=[pb], writes=[st_b])
                n = n128 * 128
                fw.dma(pool, hn0_scr.rearrange("(kt p) t -> p kt t", p=128)[:, :, tb * 512:tb * 512 + n], st_t[:, :, 0:n],
                       reads=[st_b], writes=[d_hn0])
            fw.barrier()
    if stages <= 1:
        fw.barrier()
        return nc

    with ExitStack() as es:
        wA = sb(es, "wA", [128, 16, 2048], BF16); b_wA = Buf()
        wR = sb(es, "wR", [128, 8, 256], BF16); b_wR = Buf()
        wI = sb(es, "wI", [128, 8, 256], BF16); b_wI = Buf()
        cw = sb(es, "cw", [128, 4, 8], F32); b_cw = Buf()
        cp = sb(es, "cp", [128, 4, 8], F32); b_cp = Buf()
        negc = sb(es, "negc", [128, 2, 8], F32); b_negc = Buf()
        strow = sb(es, "strow", [64, 1024], F32); b_strow = Buf()
        stT = sb(es, "stT", [128, 8, 64], F32); b_stT = Buf()
        xe = sb(es, "xe", [128, 8, 515], F32); b_xe = [Buf() for _ in range(8)]
        b_xes = b_xe

        def xesv(ct_):
            return xe[:, ct_, 0:304].rearrange("p (s j) -> p s j", j=19)
        hc = sb(es, "hc", [128, 8], F32); b_hc = [Buf() for _ in range(8)]
        hfin = sb(es, "hfin", [128, 8, 17], F32); b_hfin = Buf()
        xlast = sb(es, "xlast", [128, 8, 17, 3], F32); b_xlast = Buf()
        hnT = Pool(es, nc, "hnT", [128, 16, 512], BF16, 2)
        xcp = Pool(es, nc, "xc", [128, 512], F32, 4)
        xcbp = Pool(es, nc, "xcb", [128, 512], BF16, 4)
        szp = Pool(es, nc, "sz", [128, 512], F32, 4)
        rp = Pool(es, nc, "r", [128, 512], F32, 2)
        ip = Pool(es, nc, "ig", [128, 512], F32, 2)
        ap_ = Pool(es, nc, "a", [128, 512], F32, 2)
        a2p = Pool(es, nc, "a2", [128, 512], F32, 2)
        up = Pool(es, nc, "u", [128, 512], F32, 2)
        hp = Pool(es, nc, "h", [128, 512], F32, 2)
        yp = Pool(es, nc, "y", [128, 512], BF16, 3)

        for kq in range(4):
            fw.dma(pool, wA[:, kq * 4:(kq + 1) * 4, :],
                   w_in_a.rearrange("(kt p) c -> p kt c", p=128)[:, kq * 4:(kq + 1) * 4, :], writes=[b_wA])
        fw.dma(pool, wR[:], w_rg.rearrange("b (kh p) j -> p (b kh) j", p=128), writes=[b_wR])
        fw.dma(pool, wI[:], w_ig.rearrange("b (kh p) j -> p (b kh) j", p=128), writes=[b_wI])
        with nc.allow_non_contiguous_dma(reason="tiny per-channel params"):
            fw.dma(sp, cw[:], conv_w.rearrange("i (ct p) -> p i ct", p=128), writes=[b_cw])
            fw.dma(sp, cp[:], chp.rearrange("i (ct p) -> p i ct", p=128), writes=[b_cp])
        fw.dma(sp, strow[:], st_in[:, :], writes=[b_strow])
        fw.op(act, lambda e: e.activation(out=negc[:, 0, :], in_=cp[:, 3, :], func=AF.Exp, scale=-1.0), reads=[b_cp], writes=[b_negc])
        fw.op(act, lambda e: e.activation(out=negc[:, 0, :], in_=negc[:, 0, :], func=AF.Ln, bias=1.0), reads=[b_negc], writes=[b_negc])
        fw.op(dve, lambda e: e.tensor_scalar(out=negc[:, 1, :], in0=negc[:, 0, :], scalar1=-16.0, scalar2=None, op0=ALU.mult),
              reads=[b_negc], writes=[b_negc])
        fw.op(dve, lambda e: e.tensor_scalar(out=negc[:, 0, :], in0=negc[:, 0, :], scalar1=-8.0, scalar2=None, op0=ALU.mult),
              reads=[b_negc], writes=[b_negc])
        pt, pb = ps_next()
        for ct in range(8):
            fw.op(pe, lambda e: e.transpose(pt[:, ct * 64:(ct + 1) * 64], strow[0:64, ct * 128:(ct + 1) * 128], ident_f[0:64, 0:64]),
                  reads=[b_strow, b_idf], writes=[pb])
        fw.op(act, lambda e: e.activation(out=stT[:].rearrange("p a b -> p (a b)"), in_=pt[:, :], func=AF.Copy), reads=[pb], writes=[b_stT])
        for ct in range(8):
            fw.op(pool, lambda e: e.memset(xe[:, ct, 0:3], 0.0), writes=[b_xe[ct]])
            fw.op(pool, lambda e: e.memset(hc[:, ct:ct + 1], 0.0), writes=[b_hc[ct]])

        for tb in range(9):
            samp = tb == 8
            n = 256 if samp else 512
            ht, hb_ = hnT.next()
            fw.dma(sp, ht[:, :, 0:n], hn0_scr.rearrange("(kt p) t -> p kt t", p=128)[:, :, tb * 512:tb * 512 + n],
                   reads=[d_hn0], writes=[hb_])
            if samp:
                for ct in range(8):
                    fw.op(pool, lambda e: e.tensor_copy(out=xesv(ct)[:, :, 0:3],
                                                        in_=stT[:, ct, 16:64].rearrange("p (s i) -> p s i", i=3)),
                          reads=[b_stT], writes=[b_xes[ct]])
            for blk in range(4):
                xcs, xcbs, szs = [], [], []
                for c2 in range(2):
                    ct = blk * 2 + c2
                    pt, pb = ps_next()
                    for kt in range(16):
                        fw.op(pe, lambda e: e.matmul(pt[:, 0:n], lhsT=wA[:, kt, ct * 128:(ct + 1) * 128], rhs=ht[:, kt, 0:n],
                                                     start=(kt == 0), stop=(kt == 15)), reads=[b_wA, hb_], writes=[pb])
                    if samp:
                        xv = xesv(ct)[:, :, 3:19]
                        bx = b_xes[ct]
                        fw.op(act, lambda e: e.activation(out=xv, in_=pt[:, 0:n].rearrange("p (s t) -> p s t", t=16), func=AF.Copy),
                              reads=[pb], writes=[bx])
                    else:
                        bx = b_xe[ct]
                        fw.op(act, lambda e: e.activation(out=xe[:, ct, 3:515], in_=pt[:, 0:n], func=AF.Copy), reads=[pb], writes=[bx])
                    pt2, pb2 = ps_next()
                    for kt in range(16):
                        fw.op(pe, lambda e: e.matmul(pt2[:, 0:n], lhsT=wA[:, kt, 1024 + ct * 128:1024 + (ct + 1) * 128],
                                                     rhs=ht[:, kt, 0:n], start=(kt == 0), stop=(kt == 15)),
                              reads=[b_wA, hb_], writes=[pb2])
                    sz, szb = szp.next()
                    fw.op(act, lambda e: e.activation(out=sz[:, 0:n], in_=pt2[:, 0:n], func=AF.Silu), reads=[pb2], writes=[szb])
                    szs.append((sz, szb))
                    xc, xcb_ = xcp.next()

                    def xin(i):
                        if samp:
                            return xesv(ct)[:, :, i:i + 16]
                        return xe[:, ct, i:i + 512]

                    def xo(t_):
                        if samp:
                            return t_[:, 0:n].rearrange("p (s t) -> p s t", t=16)
                        return t_[:, 0:n]
                    fw.op(dve, lambda e: e.tensor_scalar(out=xo(xc), in0=xin(3), scalar1=cw[:, 3, ct:ct + 1], scalar2=cp[:, 0, ct:ct + 1],
                                                         op0=ALU.mult, op1=ALU.add), reads=[bx, b_cw, b_cp], writes=[xcb_])
                    for i in (2, 1, 0):
                        fw.op(dve, lambda e: e.scalar_tensor_tensor(out=xo(xc), in0=xin(i), scalar=cw[:, i, ct:ct + 1], in1=xo(xc),
                                                                    op0=ALU.mult, op1=ALU.add), reads=[bx, b_cw, xcb_], writes=[xcb_])
                    xcb, xcbb = xcbp.next()
                    fw.op(pool, lambda e: e.tensor_copy(out=xcb[:, 0:n], in_=xc[:, 0:n]), reads=[xcb_], writes=[xcbb])
                    if samp:
                        fw.op(pool, lambda e: e.tensor_copy(out=xlast[:, ct, 1:17, :], in_=xesv(ct)[:, :, 16:19]),
                              reads=[bx], writes=[b_xlast])
                    else:
                        if tb == 7:
                            fw.op(pool, lambda e: e.tensor_copy(out=xlast[:, ct, 0, :], in_=xe[:, ct, 512:515]),
                                  reads=[bx], writes=[b_xlast])
                        fw.op(pool, lambda e: e.tensor_copy(out=xe[:, ct, 0:3], in_=xe[:, ct, 512:515]), reads=[bx], writes=[bx])
                    xcs.append((xc, xcb_))
                    xcbs.append((xcb, xcbb))
                for c2 in range(2):
                    ct = blk * 2 + c2
                    xc, xcb_ = xcs[c2]
                    sz, szb = szs[c2]
                    r, rb = rp.next()
                    ig, ib = ip.next()
                    for (wG, bwG, bias_i, dst, dstb) in ((wR, b_wR, 1, r, rb), (wI, b_wI, 2, ig, ib)):
                        pt, pb = ps_next()
                        for kh in range(2):
                            fw.op(pe, lambda e: e.matmul(pt[:, 0:n], lhsT=wG[:, blk * 2 + kh, c2 * 128:(c2 + 1) * 128],
                                                         rhs=xcbs[kh][0][:, 0:n], start=(kh == 0), stop=(kh == 1)),
                                  reads=[bwG, xcbs[kh][1]], writes=[pb])
                        fw.op(act, lambda e: e.activation(out=dst[:, 0:n], in_=pt[:, 0:n], func=AF.Sigmoid,
                                                          bias=cp[:, bias_i, ct:ct + 1]), reads=[pb, b_cp], writes=[dstb])
                    a, ab = ap_.next()
                    a2, a2b = a2p.next()
                    fw.op(act, lambda e: e.activation(out=a[:, 0:n], in_=r[:, 0:n], func=AF.Exp, scale=negc[:, 0, ct:ct + 1]),
                          reads=[rb, b_negc], writes=[ab])
                    fw.op(act, lambda e: e.activation(out=a2[:, 0:n], in_=r[:, 0:n], func=AF.Exp, scale=negc[:, 1, ct:ct + 1]),
                          reads=[rb, b_negc], writes=[a2b])
                    fw.op(act, lambda e: e.activation(out=a2[:, 0:n], in_=a2[:, 0:n], func=AF.Sqrt, scale=-1.0, bias=1.0),
                          reads=[a2b], writes=[a2b])
                    u, ub = up.next()
                    fw.op(pool, lambda e: e.tensor_tensor(out=u[:, 0:n], in0=ig[:, 0:n], in1=xc[:, 0:n], op=ALU.mult),
                          reads=[ib, xcb_], writes=[ub])
                    fw.op(dve, lambda e: e.tensor_tensor(out=u[:, 0:n], in0=u[:, 0:n], in1=a2[:, 0:n], op=ALU.mult),
                          reads=[ub, a2b], writes=[ub])
                    h, hb2 = hp.next()
                    if samp:
                        for s in range(16):
                            fw.op(dve, lambda e: e.tensor_tensor_scan(out=h[:, s * 16:(s + 1) * 16], data0=a[:, s * 16:(s + 1) * 16],
                                                                      data1=u[:, s * 16:(s + 1) * 16], initial=stT[:, ct, s:s + 1],
                                                                      op0=ALU.mult, op1=ALU.add),
                                  reads=[ab, ub, b_stT], writes=[hb2])
                        fw.op(pool, lambda e: e.tensor_copy(out=hfin[:, ct, 1:17],
                                                            in_=h[:, 0:n].rearrange("p (s t) -> p s t", t=16)[:, :, 15]),
                              reads=[hb2], writes=[b_hfin])
                    else:
                        fw.op(dve, lambda e: e.tensor_tensor_scan(out=h[:, 0:n], data0=a[:, 0:n], data1=u[:, 0:n],
                                                                  initial=hc[:, ct:ct + 1], op0=ALU.mult, op1=ALU.add),
                              reads=[ab, ub, b_hc[ct]], writes=[hb2])
                        fw.op(pool, lambda e: e.tensor_copy(out=hc[:, ct:ct + 1], in_=h[:, n - 1:n]), reads=[hb2], writes=[b_hc[ct]])
                        if tb == 7:
                            fw.op(pool, lambda e: e.tensor_copy(out=hfin[:, ct, 0:1], in_=h[:, n - 1:n]), reads=[hb2], writes=[b_hfin])
                    y, yb = yp.next()
                    fw.op(dve, lambda e: e.tensor_tensor(out=y[:, 0:n], in0=h[:, 0:n], in1=sz[:, 0:n], op=ALU.mult),
                          reads=[hb2, szb], writes=[yb])
                    fw.dma(pool, y0_src[tb][ct * 128:(ct + 1) * 128, :], y[:, 0:n], reads=[yb], writes=[d_y0s[tb]])
            fw.collective("AllGather", y0_src[tb][:, :], y0_all[tb][:, :], reads=[d_y0s[tb]], writes=[d_y0a[tb]])
        so = strow; b_so = b_strow
        for (srcT, bsrc, ncols, r0) in ((hfin, b_hfin, 17, 0), (xlast, b_xlast, 51, 17)):
            for half in range(2):
                pt, pb = ps_next()
                for c4 in range(4):
                    ct = half * 4 + c4
                    s_ap = srcT[:, ct, :] if ncols == 17 else srcT[:, ct, :, :].rearrange("p s i -> p (s i)")
                    fw.op(pe, lambda e: e.transpose(pt[0:ncols, c4 * 128:(c4 + 1) * 128], s_ap, ident_f[:]),
                          reads=[bsrc, b_idf], writes=[pb])
                stage = so[0:ncols, half * 512:(half + 1) * 512]
                fw.op(act, lambda e: e.activation(out=stage, in_=pt[0:ncols, :], func=AF.Copy), reads=[pb], writes=[b_so])
            if ncols == 17:
                fw.dma(pool, lru_sl[:, :], so[0:17, :], reads=[b_so], writes=[d_out[3]])
            else:
                fw.dma(pool, conv_sl[:, :], so[0:51, :], reads=[b_so], writes=[d_out[4]])
        fw.barrier()
    if stages <= 2:
        fw.barrier()
        return nc

    def exchange_stats(es_, i_ss, i_r, which):
        sg = sb(es_, f"sg{which}", [128, 4, NTT], F32); b_sg = Buf()
        fw.dma(pool, st_src[which][:, :], stat[:, i_ss, :], reads=[b_stat[i_ss]], writes=[d_sts[which]])
        fw.collective("AllGather", st_src[which][:, :], st_all[which][:, :], reads=[d_sts[which]], writes=[d_sta[which]])
        fw.dma(sp, sg[:], st_all[which].rearrange("(r p) t -> p r t", p=128), reads=[d_sta[which]], writes=[b_sg])
        fw.op(dve, lambda e: e.tensor_tensor(out=stat[:, i_ss, :], in0=sg[:, 0, :], in1=sg[:, 1, :], op=ALU.add),
              reads=[b_sg], writes=[b_stat[i_ss]])
        for r_ in (2, 3):
            fw.op(dve, lambda e: e.tensor_tensor(out=stat[:, i_ss, :], in0=stat[:, i_ss, :], in1=sg[:, r_, :], op=ALU.add),
                  reads=[b_sg, b_stat[i_ss]], writes=[b_stat[i_ss]])
        rstd_from(i_ss, i_r)

    with ExitStack() as esR:
        R1 = sb(esR, "R1", [128, NTT, 512], F32); b_R1 = [Buf() for _ in range(NTT)]
        with ExitStack() as es:
            wO = sb(es, "wO", [128, 32, 512], BF16); b_wO = Buf()
            y0p = Pool(es, nc, "y0t", [128, 32, 512], BF16, 2)
            junk3 = sb(es, "junk3", [128, 512], BF16); b_j3 = Buf()
            xsp = Pool(es, nc, "xsl", [128, 512], F32, 3)
            for kq in range(4):
                fw.dma(pool, wO[:, kq * 8:(kq + 1) * 8, :],
                       w_out_a.rearrange("(kt p) f -> p kt f", p=128)[:, kq * 8:(kq + 1) * 8, :], writes=[b_wO])
            for tb in range(9):
                n128 = 4 if tb < 8 else 2
                n = n128 * 128
                yt, ytb = y0p.next()
                fw.dma(sp, yt[:, :, 0:n], y0_all[tb].rearrange("(kt p) t -> p kt t", p=128),
                       reads=[d_y0a[tb]], writes=[ytb])
                for j in range(n128):
                    tile = tb * 4 + j
                    pt, pb = ps_next()
                    for kt in range(32):
                        fw.op(pe, lambda e: e.matmul(pt[:, :], lhsT=yt[:, kt, j * 128:(j + 1) * 128], rhs=wO[:, kt, :],
                                                     start=(kt == 0), stop=(kt == 31)), reads=[ytb, b_wO], writes=[pb])
                    fw.op(act, lambda e: e.activation(out=R1[:, tile, :], in_=pt[:, :], func=AF.Copy), reads=[pb], writes=[b_R1[tile]])
                    fw.op(act, lambda e: e.activation(out=junk3[:], in_=pt[:, :], func=AF.Square,
                                                      accum_out=stat[:, 2, tile:tile + 1]),
                          reads=[pb], writes=[b_j3, b_stat[2]])
            if stages == 3:
                fw.barrier()
                return nc
            exchange_stats(es, 2, 3, 0)
            if stages == 3.5:
                fw.barrier()
                return nc
            for tile in range(NTT):
                rg = rg_of(tile)
                xs, xsb = xsp.next()
                fw.dma(sp, xs[:], xg_sl[tile * 128:(tile + 1) * 128, :], writes=[xsb])
                fw.op(dve, lambda e: e.scalar_tensor_tensor(out=R1[:, tile, :], in0=R1[:, tile, :], scalar=stat[:, 3, tile:tile + 1],
                                                            in1=GG0bc[:, rg, :], op0=ALU.mult, op1=ALU.mult),
                      reads=[b_R1[tile], b_stat[3], b_GG0], writes=[b_R1[tile]])
                fw.op(pool, lambda e: e.tensor_tensor(out=R1[:, tile, :], in0=R1[:, tile, :], in1=xs[:], op=ALU.add),
                      reads=[b_R1[tile], xsb], writes=[b_R1[tile]])
                fw.op(act, lambda e: e.activation(out=junk3[:], in_=R1[:, tile, :], func=AF.Square,
                                                  accum_out=stat[:, 4, tile:tile + 1]),
                      reads=[b_R1[tile]], writes=[b_j3, b_stat[4]])
                fw.dma(pool, x1_scr[tile * 128:(tile + 1) * 128, :], R1[:, tile, :], reads=[b_R1[tile]], writes=[d_x1])
            exchange_stats(es, 4, 5, 1)
            fw.barrier()
        with ExitStack() as es:
            t4p = Pool(es, nc, "t4", [128, 512], F32, 2)
            h1p = Pool(es, nc, "h1", [128, 512], BF16, 2)
            hkp = Pool(es, nc, "hk", [128, 512], BF16, 2)
            stg4 = Pool(es, nc, "stg4", [128, 8, 512], BF16, 2)
            for tb in range(9):
                n128 = 4 if tb < 8 else 2
                n = n128 * 128
                st_t, st_b = stg4.next()
                for j in range(n128):
                    tile = tb * 4 + j
                    rg = rg_of(tile)
                    t4, t4b = t4p.next()
                    fw.op(dve, lambda e: e.scalar_tensor_tensor(out=t4[:], in0=R1[:, tile, :], scalar=stat[:, 5, tile:tile + 1],
                                                                in1=A1bc[:, rg, :], op0=ALU.mult, op1=ALU.mult),
                          reads=[b_R1[tile], b_stat[5], b_A1], writes=[t4b])
                    h1, h1b = h1p.next()
                    fw.op(pool, lambda e: e.tensor_tensor(out=h1[:], in0=t4[:], in1=B1bc[:, rg, :], op=ALU.add),
                          reads=[t4b, b_B1], writes=[h1b])
                    hk, hkb = hkp.next()
                    fw.op(dve, lambda e: e.scalar_tensor_tensor(out=hk[:], in0=R1[:, tile, :], scalar=stat[:, 5, tile:tile + 1],
                                                                in1=GKVbc[:, :], op0=ALU.mult, op1=ALU.mult),
                          reads=[b_R1[tile], b_stat[5], b_GKV], writes=[hkb])
                    pt, pb = ps_next()
                    ptb = pt.bitcast(BF16)
                    for f4 in range(4):
                        fw.op(pe, lambda e: e.transpose(ptb[:, f4 * 128:(f4 + 1) * 128], hk[:, f4 * 128:(f4 + 1) * 128], ident_b[:]),
                              reads=[hkb, b_idb], writes=[pb])
                    for f4 in range(4):
                        fw.op(pe, lambda e: e.transpose(ptb[:, (4 + f4) * 128:(5 + f4) * 128], h1[:, f4 * 128:(f4 + 1) * 128], ident_b[:]),
                              reads=[h1b, b_idb], writes=[pb])
                    fw.op(act, lambda e: e.activation(out=st_t[:, :, j * 128:(j + 1) * 128],
                                                      in_=ptb[:, :].rearrange("p (a b) -> p a b", a=8), func=AF.Copy),
                          reads=[pb], writes=[st_b])
                fw.dma(pool, hn_src[tb].rearrange("(k p) t -> p k t", p=128), st_t[:, :, 0:n],
                       reads=[st_b], writes=[d_hns[tb]])
                fw.collective("AllGather", hn_src[tb][:, :], hn_all[tb][:, :], reads=[d_hns[tb]], writes=[d_hna[tb]])
            fw.barrier()
    if stages <= 4:
        fw.barrier()
        return nc
    lvl = 9 if stages >= 5 else int(round((stages - 4) * 10))

    with ExitStack() as es:
        wQ = sb(es, "wQ", [128, 16, 1024], BF16); b_wQ = Buf()
        wKV = sb(es, "wKV", [128, 16, 1024], BF16); b_wKV = Buf()
        h1T = Pool(es, nc, "h1T", [128, 16, 512], BF16, 2)
        hkT = Pool(es, nc, "hkT", [128, 16, 512], BF16, 2)
        qst = Pool(es, nc, "qst", [128, 512], BF16, 3)
        kvf = Pool(es, nc, "kvf", [128, 512], F32, 3)
        kvb = Pool(es, nc, "kvb", [128, 512], BF16, 3)
        kts = Pool(es, nc, "kts", [128, 4, 128], BF16, 2)
        for kq in range(4):
            fw.dma(pool, wQ[:, kq * 4:(kq + 1) * 4, :], w_in_b.rearrange("(kt p) c -> p kt c", p=128)[:, kq * 4:(kq + 1) * 4, :], writes=[b_wQ])
            fw.dma(pool, wKV[:, kq * 4:(kq + 1) * 4, :], w_kv.rearrange("(kt p) c -> p kt c", p=128)[:, kq * 4:(kq + 1) * 4, :], writes=[b_wKV])
        for tb in range(9):
            n128 = 4 if tb < 8 else 2
            n = n128 * 128
            c0 = tb * 512
            a1, a1b = h1T.next()
            ak, akb = hkT.next()
            hview = hn_all[tb].rearrange("(r k f p) t -> k p r f t", r=4, k=2, f=4, p=128)
            for r_ in range(4):
                fw.dma(sp, ak[:, r_ * 4:(r_ + 1) * 4, 0:n], hview[0][:, r_, :, :], reads=[d_hna[tb]], writes=[akb])
                fw.dma(sp, a1[:, r_ * 4:(r_ + 1) * 4, 0:n], hview[1][:, r_, :, :], reads=[d_hna[tb]], writes=[a1b])
            for hh in range(4):
                for which in range(2):
                    pt, pb = ps_next()
                    for kt in range(16):
                        fw.op(pe, lambda e: e.matmul(pt[:, 0:n], lhsT=wQ[:, kt, which * 512 + hh * 128:which * 512 + (hh + 1) * 128],
                                                     rhs=a1[:, kt, 0:n], start=(kt == 0), stop=(kt == 15)),
                              reads=[b_wQ, a1b], writes=[pb])
                    st, stb = qst.next()
                    fw.op(act, lambda e: e.activation(out=st[:, 0:n], in_=pt[:, 0:n], func=(AF.Copy if which == 0 else AF.Silu)),
                          reads=[pb], writes=[stb])
                    if lvl < 4:
                        continue
                    if which == 0:
                        fw.dma(pool, qT_scr[hh, :, c0:c0 + n], st[:, 0:n], reads=[stb], writes=[d_qT])
                    else:
                        fw.dma(pool, sz_scr[hh, :, c0:c0 + n], st[:, 0:n], reads=[stb], writes=[d_sz])
            for j in range(n128 if lvl >= 6 else 0):
                tile = tb * 4 + j
                for half in range(2):
                    pt, pb = ps_next()
                    for kt in range(16):
                        fw.op(pe, lambda e: e.matmul(pt[:, :], lhsT=ak[:, kt, j * 128:(j + 1) * 128], rhs=wKV[:, kt, half * 512:(half + 1) * 512],
                                                     start=(kt == 0), stop=(kt == 15)), reads=[b_wKV, akb], writes=[pb])
                    f_, fb = kvf.next()
                    fw.op(act, lambda e: e.activation(out=f_[:], in_=pt[:, :], func=AF.Copy), reads=[pb], writes=[fb])
                    fw.dma(pool, (k_sl if half == 0 else v_sl)[tile * 128:(tile + 1) * 128, :], f_[:], reads=[fb],
                           writes=[d_out[1 + half]])
                    b_, bb = kvb.next()
                    fw.op(dve, lambda e: e.tensor_copy(out=b_[:], in_=pt[:, :]), reads=[pb], writes=[bb])
                    if half == 1:
                        fw.dma(pool, v_scr[tile * 128:(tile + 1) * 128, :], b_[:], reads=[bb], writes=[d_v])
                    elif lvl >= 8:
                        pt2, pb2 = ps_next()
                        ptb = pt2.bitcast(BF16)
                        for hh in range(4):
                            fw.op(pe, lambda e: e.transpose(ptb[:, hh * 128:(hh + 1) * 128], b_[:, hh * 128:(hh + 1) * 128], ident_b[:]),
                                  reads=[bb, b_idb], writes=[pb2])
                        kt_, ktb = kts.next()
                        fw.op(act, lambda e: e.activation(out=kt_[:], in_=ptb[:, 0:512].rearrange("p (a b) -> p a b", a=4), func=AF.Copy),
                              reads=[pb2], writes=[ktb])
                        fw.dma(pool, kT_scr.rearrange("h p t -> p h t")[:, :, tile * 128:(tile + 1) * 128], kt_[:],
                               reads=[ktb], writes=[d_kT])
        fw.barrier()
    if stages <= 5:
        fw.barrier()
        return nc

    def attn_p1(E, Eb_, Fx, Fb_, S, spp, z_chunk, mask_fn):
        nch = (S + 511) // 512
        fw.op(pool, lambda e: e.memset(Fx[:, 0:1], 0.0), writes=[Fb_])
        for c in range(nch):
            s0 = c * 512
            w = min(512, S - s0)
            pt, pb = z_chunk(c, s0, w)
            fw.op(act, lambda e: e.activation(out=E[:, s0:s0 + w], in_=pt[:, 0:w], func=AF.Exp, scale=SCALE),
                  reads=[pb], writes=[Eb_])
            if c == nch - 1:
                mask_fn(E, Eb_)
            spc, spb = spp.next()
            fw.op(act, lambda e: e.activation(out=spc[:, 0:w], in_=E[:, s0:s0 + w], func=AF.Ln, bias=1.0),
                  reads=[Eb_], writes=[spb])
            fw.op(dve, lambda e: e.tensor_tensor_scan(out=Fx[:, s0 + 1:s0 + w + 1], data0=ones_c[:, 0:1].to_broadcast([128, w]),
                                                      data1=spc[:, 0:w], initial=Fx[:, s0:s0 + 1], op0=ALU.mult, op1=ALU.add),
                  reads=[spb, b_ones, Fb_], writes=[Fb_])

    def attn_p2(E, Eb_, Fx, Fb_, P, Pcb, PT, PTcb, S, spp, negt):
        nch = (S + 511) // 512
        nt, ntb = negt.next()
        fw.op(dve, lambda e: e.tensor_scalar(out=nt[:], in0=Fx[:, S:S + 1], scalar1=-1.0, scalar2=None, op0=ALU.mult),
              reads=[Fb_], writes=[ntb])
        for c in range(nch):
            s0 = c * 512
            w = min(512, S - s0)
            wc, wcb = spp.next()
            fw.op(act, lambda e: e.activation(out=wc[:, 0:w], in_=Fx[:, s0:s0 + w], func=AF.Exp, bias=nt[:, 0:1]),
                  reads=[Fb_, ntb], writes=[wcb])
            fw.op(pool if c % 3 == 0 else dve,
                  lambda e: e.tensor_tensor(out=P[:, s0:s0 + w], in0=E[:, s0:s0 + w], in1=wc[:, 0:w], op=ALU.mult),
                  reads=[Eb_, wcb], writes=[Pcb[c]])
            pt, pb = ps_next()
            ptb = pt.bitcast(BF16)
            nb = (w + 127) // 128
            for b_ in range(nb):
                kb = c * 4 + b_
                wk = min(128, S - kb * 128)
                fw.op(pe, lambda e: e.transpose(ptb[0:wk, b_ * 128:(b_ + 1) * 128], P[:, kb * 128:kb * 128 + wk], ident_b[:]),
                      reads=[Pcb[c], b_idb], writes=[pb])
            full = w // 128
            if full > 0:
                srcv = ptb[:, 0:full * 128].rearrange("p (a b) -> p a b", a=full)
                fw.op(dve, lambda e: e.tensor_copy(out=PT[:, c * 4:c * 4 + full, :], in_=srcv), reads=[pb], writes=[PTcb[c]])
            if full < nb:
                wk = w - full * 128
                fw.op(dve, lambda e: e.tensor_copy(out=PT[0:wk, c * 4 + full, :], in_=ptb[0:wk, full * 128:(full + 1) * 128]),
                      reads=[pb], writes=[PTcb[c]])

    with ExitStack() as es:
        qTh = sb(es, "qTh", [128, NT], BF16); b_qTh = Buf()
        kTh = sb(es, "kTh", [128, NT], BF16); b_kTh = Buf()
        szh = sb(es, "szh", [128, NT], BF16); b_szh = Buf()
        Vh = sb(es, "Vh", [128, NTT, 128], BF16); b_Vh = Buf()
        y1h = sb(es, "y1h", [128, NPT], BF16); b_y1h = Buf()
        Ep = Pool(es, nc, "E", [128, 4096], F32, 2)
        Fp = Pool(es, nc, "Fx", [128, 4100], F32, 2)
        spp = Pool(es, nc, "spc", [128, 512], F32, 6)
        Pp = Pool(es, nc, "P", [128, 4096], BF16, 2)
        PTp = Pool(es, nc, "PT", [128, 32, 128], BF16, 2)
        negt = Pool(es, nc, "negt", [128, 1], F32, 4)
        Pcbs = [[Buf() for _ in range(9)] for _ in range(2)]
        PTcbs = [[Buf() for _ in range(9)] for _ in range(2)]
        n_heads = 4 if stages >= 6 else 1
        for hh in range(n_heads):
            fw.dma(sp, qTh[:], qT_scr[hh, :, :], reads=[d_qT], writes=[b_qTh])
            fw.dma(sp, kTh[:], kT_scr[hh, :, :], reads=[d_kT], writes=[b_kTh])
            fw.dma(sp, szh[:], sz_scr[hh, :, :], reads=[d_sz], writes=[b_szh])
            fw.dma(sp, Vh[:], v_scr.rearrange("(b p) (h d) -> p b h d", p=128, d=128)[:, :, hh, :], reads=[d_v], writes=[b_Vh])
            def finish(pend):
                qb_, S_, E_, Eb__, Fx_, Fb__ = pend
                P, _ = Pp.next()
                PT, _ = PTp.next()
                Pcb = Pcbs[(Pp.i - 1) % 2]
                PTcb = PTcbs[(PTp.i - 1) % 2]
                attn_p2(E_, Eb__, Fx_, Fb__, P, Pcb, PT, PTcb, S_, spp, negt)
                po, pob = ps_next()
                for kb in range(qb_ + 1):
                    fw.op(pe, lambda e: e.matmul(po[:, 0:128], lhsT=Vh[:, kb, :], rhs=PT[:, kb, :], start=(kb == 0), stop=(kb == qb_)),
                          reads=[b_Vh, PTcb[kb // 4]], writes=[pob])
                fw.op(dve, lambda e: e.tensor_tensor(out=y1h[:, qb_ * 128:(qb_ + 1) * 128], in0=po[:, 0:128],
                                                     in1=szh[:, qb_ * 128:(qb_ + 1) * 128], op=ALU.mult),
                      reads=[pob, b_szh], writes=[b_y1h])

            pend = None
            for qb in range(32):
                S = 128 * (qb + 1)
                E, Eb_ = Ep.next()
                Fx, Fb_ = Fp.next()

                def z_chunk(c, s0, w):
                    pt, pb = ps_next()
                    fw.op(pe, lambda e: e.matmul(pt[:, 0:w], lhsT=qTh[:, qb * 128:(qb + 1) * 128], rhs=kTh[:, s0:s0 + w],
                                                 start=True, stop=True), reads=[b_qTh, b_kTh], writes=[pb])
                    return pt, pb

                def mask_fn(E_, Eb__):
                    fw.op(dve, lambda e: e.tensor_tensor(out=E_[:, S - 128:S], in0=E_[:, S - 128:S], in1=tri[:], op=ALU.mult),
                          reads=[Eb__, b_tri], writes=[Eb__])
                attn_p1(E, Eb_, Fx, Fb_, S, spp, z_chunk, mask_fn)
                if pend is not None:
                    finish(pend)
                pend = (qb, S, E, Eb_, Fx, Fb_)
            finish(pend)
            for tb in range(8):
                fw.dma(pool, y1_src[tb][hh * 128:(hh + 1) * 128, :], y1h[:, tb * 512:(tb + 1) * 512], reads=[b_y1h], writes=[d_y1s[tb]])
        fw.barrier()
    if stages < 6:
        fw.barrier()
        return nc

    if stages >= 7:
        ps_banks[0] = list(range(6))
        with ExitStack() as es:
            qs = sb(es, "qs", [128, 4, 256], BF16); b_qs = Buf()
            ks = sb(es, "ks", [128, 4, 256], BF16); b_ks = Buf()
            szs = sb(es, "szs", [128, 4, 256], BF16); b_szs = Buf()
            vnew = sb(es, "vnew", [16, 16, 512], BF16); b_vnew = Buf()
            y1sT = sb(es, "y1sT", [128, 4, 256], BF16); b_y1sT = Buf()
            QP = sb(es, "QP", [128, 8, 128], BF16); b_QP = Buf()
            Vbp = Pool(es, nc, "Vbc", [128, 4, 2, 512], BF16, 3)
            zer = sb(es, "zer", [128, 128], BF16); b_zer = Buf()
            E = sb(es, "Es", [128, 4112], F32); Eb_ = Buf()
            Fx = sb(es, "Fxs", [128, 4116], F32); Fb_ = Buf()
            P = sb(es, "Ps", [128, 4112], BF16); Pcb = [Buf() for _ in range(9)]
            PT = sb(es, "PTs", [128, 33, 128], BF16); PTcb = [Buf() for _ in range(9)]
            spp = Pool(es, nc, "spcs", [128, 512], F32, 3)
            negt = Pool(es, nc, "negts", [128, 1], F32, 2)
            Kcp = Pool(es, nc, "Kc", [128, 4, 512], F32, 3)
            Kbp = Pool(es, nc, "Kb", [128, 4, 512], BF16, 3)
            Vcp = Pool(es, nc, "Vc", [128, 4, 512], F32, 2)
            KTp = Pool(es, nc, "KTc", [128, 512], BF16, 4)
            fw.dma(sp, qs[:], qT_scr.rearrange("h p t -> p h t")[:, :, NPT:NT], reads=[d_qT], writes=[b_qs])
            fw.dma(sp, ks[:], kT_scr.rearrange("h p t -> p h t")[:, :, NPT:NT], reads=[d_kT], writes=[b_ks])
            fw.dma(sp, szs[:], sz_scr.rearrange("h p t -> p h t")[:, :, NPT:NT], reads=[d_sz], writes=[b_szs])
            fw.dma(sp, vnew[:], v_scr[NPT:NT, :].rearrange("(s j) f -> j s f", j=16), reads=[d_v], writes=[b_vnew])
            fw.op(pool, lambda e: e.memset(QP[:], 0.0), writes=[b_QP])
            fw.op(pool, lambda e: e.memset(zer[:], 0.0), writes=[b_zer])
            zi = [0]
            for pg in range(8):
                for i in range(8):
                    sl, hh = i // 4, i % 4
                    sq = 2 * pg + sl
                    fw.op(pool, lambda e: e.tensor_copy(out=QP[:, i, 16 * i:16 * i + 16], in_=qs[:, hh, sq * 16:(sq + 1) * 16]),
                          reads=[b_qs], writes=[b_QP])

                def z_chunk(c, s0, w):
                    zt, zb = psb[6 + zi[0] % 2]
                    zi[0] += 1
                    if c < 8:
                        kcs = []
                        for sl in range(2):
                            kc, kcb = Kcp.next()
                            fw.dma(sp, kc[:], cache_k[2 * pg + sl, c * 512:(c + 1) * 512, :].rearrange("(b p) f -> p b f", p=128),
                                   writes=[kcb])
                            kb16, kb16b = Kbp.next()
                            fw.op(act, lambda e: e.activation(out=kb16[:], in_=kc[:], func=AF.Copy), reads=[kcb], writes=[kb16b])
                            kcs.append((kb16, kb16b))
                        for i in range(8):
                            sl, hh = i // 4, i % 4
                            kc, kcb = kcs[sl]
                            pt, pb = ps_next()
                            ptb = pt.bitcast(BF16)
                            for b in range(4):
                                fw.op(pe, lambda e: e.transpose(ptb[:, b * 128:(b + 1) * 128], kc[:, b, hh * 128:(hh + 1) * 128], ident_b[:]),
                                      reads=[kcb, b_idb], writes=[pb])
                            kt_, ktb = KTp.next()
                            if i % 4 == 0:
                                fw.op(act, lambda e: e.activation(out=kt_[:], in_=ptb[:, 0:512], func=AF.Copy), reads=[pb], writes=[ktb])
                            else:
                                fw.op(dve, lambda e: e.tensor_copy(out=kt_[:], in_=ptb[:, 0:512]), reads=[pb], writes=[ktb])
                            fw.op(pe, lambda e: e.matmul(zt[:, 0:512], lhsT=QP[:, i, :], rhs=kt_[:], start=(i == 0), stop=(i == 7)),
                                  reads=[b_QP, ktb], writes=[zb])
                    else:
                        for i in range(8):
                            sl, hh = i // 4, i % 4
                            sq = 2 * pg + sl
                            fw.op(pe, lambda e: e.matmul(zt[:, 0:16], lhsT=QP[:, i, :], rhs=ks[:, hh, sq * 16:(sq + 1) * 16],
                                                         start=(i == 0), stop=(i == 7)), reads=[b_QP, b_ks], writes=[zb])
                    return zt, zb

                def mask_fn(E_, Eb__):
                    fw.op(dve, lambda e: e.tensor_tensor(out=E_[:, 4096:4112], in0=E_[:, 4096:4112], in1=m16[:], op=ALU.mult),
                          reads=[Eb__, b_m16], writes=[Eb__])
                attn_p1(E, Eb_, Fx, Fb_, 4112, spp, z_chunk, mask_fn)
                attn_p2(E, Eb_, Fx, Fb_, P, Pcb, PT, PTcb, 4112, spp, negt)
                po, pob = ps_next()
                fw.op(pe, lambda e: e.matmul(po[:, 0:128], lhsT=zer[:], rhs=zer[:], start=True, stop=False, skip_group_check=True),
                      reads=[b_zer], writes=[pob])
                for c in range(8):
                    vb, vbb = Vbp.next()
                    for sl in range(2):
                        vc, vcb = Vcp.next()
                        fw.dma(sp, vc[:], cache_v[2 * pg + sl, c * 512:(c + 1) * 512, :].rearrange("(b p) f -> p b f", p=128),
                               writes=[vcb])
                        fw.op(act, lambda e: e.activation(out=vb[:, :, sl, :], in_=vc[:], func=AF.Copy), reads=[vcb], writes=[vbb])
                    for i in range(8):
                        sl, hh = i // 4, i % 4
                        for b in range(4):
                            kb = c * 4 + b
                            fw.op(pe, lambda e: e.matmul(po[:, i * 16:(i + 1) * 16], lhsT=vb[:, b, sl, hh * 128:(hh + 1) * 128],
                                                         rhs=PT[:, kb, i * 16:(i + 1) * 16], start=False, stop=False,
                                                         skip_group_check=True),
                                  reads=[vbb, PTcb[c]], writes=[pob])
                for i in range(8):
                    sl, hh = i // 4, i % 4
                    sq = 2 * pg + sl
                    fw.op(pe, lambda e: e.matmul(po[:, i * 16:(i + 1) * 16], lhsT=vnew[0:16, sq, hh * 128:(hh + 1) * 128],
                                                 rhs=PT[0:16, 32, i * 16:(i + 1) * 16], start=False, stop=True,
                                                 skip_group_check=True),
                          reads=[b_vnew, PTcb[8]], writes=[pob])
                for sl in range(2):
                    sq = 2 * pg + sl
                    fw.op(dve, lambda e: e.tensor_tensor(out=y1sT[:, :, sq * 16:(sq + 1) * 16],
                                                         in0=po[:, sl * 64:(sl + 1) * 64].rearrange("p (h t) -> p h t", h=4),
                                                         in1=szs[:, :, sq * 16:(sq + 1) * 16], op=ALU.mult),
                          reads=[pob, b_szs], writes=[b_y1sT])
            fw.dma(pool, y1_src[8].rearrange("(h p) t -> p h t", p=128), y1sT[:], reads=[b_y1sT], writes=[d_y1s[8]])
            fw.barrier()
        ps_banks[0] = list(range(8))

    for tb in range(9):
        fw.collective("AllGather", y1_src[tb][:, :], y1_all[tb][:, :], reads=[d_y1s[tb]], writes=[d_y1a[tb]])
    with ExitStack() as es:
        wO2 = sb(es, "wO2", [128, 16, 512], BF16); b_wO2 = Buf()
        R2 = sb(es, "R2", [128, NTT, 512], F32); b_R2 = [Buf() for _ in range(NTT)]
        y1p = Pool(es, nc, "y1t", [128, 16, 512], BF16, 2)
        junk8 = sb(es, "junk8", [128, 512], BF16); b_j8 = Buf()
        x1p = Pool(es, nc, "x1t", [128, 512], F32, 3)
        for kq in range(4):
            fw.dma(pool, wO2[:, kq * 4:(kq + 1) * 4, :], w_out_b.rearrange("(kt p) f -> p kt f", p=128)[:, kq * 4:(kq + 1) * 4, :], writes=[b_wO2])
        for tb in range(9):
            n128 = 4 if tb < 8 else 2
            n = n128 * 128
            yt, ytb = y1p.next()
            fw.dma(sp, yt[:, :, 0:n], y1_all[tb].rearrange("(kt p) t -> p kt t", p=128), reads=[d_y1a[tb]], writes=[ytb])
            for j in range(n128):
                tile = tb * 4 + j
                pt, pb = ps_next()
                for kt in range(16):
                    fw.op(pe, lambda e: e.matmul(pt[:, :], lhsT=yt[:, kt, j * 128:(j + 1) * 128], rhs=wO2[:, kt, :],
                                                 start=(kt == 0), stop=(kt == 15)), reads=[ytb, b_wO2], writes=[pb])
                fw.op(act, lambda e: e.activation(out=R2[:, tile, :], in_=pt[:, :], func=AF.Copy), reads=[pb], writes=[b_R2[tile]])
                fw.op(act, lambda e: e.activation(out=junk8[:], in_=pt[:, :], func=AF.Square, accum_out=stat[:, 6, tile:tile + 1]),
                      reads=[pb], writes=[b_j8, b_stat[6]])
        exchange_stats(es, 6, 7, 2)
        for tile in range(NTT):
            rg = rg_of(tile)
            xs, xsb = x1p.next()
            fw.dma(sp, xs[:], x1_scr[tile * 128:(tile + 1) * 128, :], reads=[d_x1], writes=[xsb])
            fw.op(dve, lambda e: e.scalar_tensor_tensor(out=R2[:, tile, :], in0=R2[:, tile, :], scalar=stat[:, 7, tile:tile + 1],
                                                        in1=GG1bc[:, rg, :], op0=ALU.mult, op1=ALU.mult),
                  reads=[b_R2[tile], b_stat[7], b_GG1], writes=[b_R2[tile]])
            fw.op(pool, lambda e: e.tensor_tensor(out=R2[:, tile, :], in0=R2[:, tile, :], in1=xs[:], op=ALU.add),
                  reads=[b_R2[tile], xsb], writes=[b_R2[tile]])
            fw.dma(pool, y_sl[tile * 128:(tile + 1) * 128, :], R2[:, tile, :], reads=[b_R2[tile]], writes=[d_out[0]])
        fw.barrier()
    return nc


_NC_CACHE = {}


def _consts():
    ident = np.eye(128, dtype=np.float32)
    q = np.arange(128)
    tri = (q[None, :] < q[:, None]).astype(np.float32)
    m16 = ((np.arange(16)[None, :]) < (q[:, None] % 16)).astype(np.float32)
    sel = np.zeros((17, 3, 128), np.float32)
    sel[0, 0, :] = 1.0
    for p in range(128):
        sel[1 + p // 16, 1, p] = 1.0
        sel[9 + p // 16, 2, p] = 1.0
    return ident, tri, m16, sel


def make_in_maps(x_prompt, x_sample, c_prompt, c_sample, cache_k, cache_v, state_lru, state_conv,
                 g_pre, g_post, w_ada, b_ada, w_in_a, conv_w, conv_b, w_rgate, b_rgate, w_igate, b_igate,
                 lru_lambda, w_out_a, g_kv, w_kv, w_in_b, w_out_b):
    f = lambda a: np.ascontiguousarray(np.asarray(a, dtype=np.float32))
    ident, tri, m16, sel = _consts()
    in_maps = []
    for c in range(8):
        g, m = c // 4, c % 4
        fs = slice(512 * m, 512 * m + 512)
        cs = slice(1024 * m, 1024 * m + 1024)
        ss = slice(16 * g, 16 * g + 16)
        xg = np.concatenate([x_prompt[g], x_sample[ss].reshape(256, D)], axis=0)
        d = {
            "xg": f(xg),
            "xg_sl": f(xg[:, fs]),
            "cg": f(np.concatenate([c_prompt[g:g + 1], c_sample[ss]], axis=0)),
            "w_ada0": f(w_ada[0][:, 0:4096]),
            "b_ada0": f(b_ada[0][0:4096]),
            "w_adas": f(np.concatenate([w_ada[0][:, 4096 + 512 * m:4096 + 512 * m + 512],
                                        w_ada[1][:, 512 * m:512 * m + 512],
                                        w_ada[1][:, 2048 + 512 * m:2048 + 512 * m + 512],
                                        w_ada[1][:, 4096 + 512 * m:4096 + 512 * m + 512]], axis=1)),
            "b_adas": f(np.concatenate([b_ada[0][4096 + 512 * m:4096 + 512 * m + 512],
                                        b_ada[1][512 * m:512 * m + 512],
                                        b_ada[1][2048 + 512 * m:2048 + 512 * m + 512],
                                        b_ada[1][4096 + 512 * m:4096 + 512 * m + 512]])),
            "g_pre0": f(g_pre[0]),
            "g_sl": f(np.stack([g_post[0][fs], g_pre[1][fs], g_post[1][fs], g_kv[fs]])),
            "w_in_a": f(np.concatenate([w_in_a[0][:, cs], w_in_a[0][:, 4096 + 1024 * m:4096 + 1024 * m + 1024]], axis=1)),
            "conv_w": f(conv_w[0][:, cs]),
            "chp": f(np.stack([conv_b[0][cs], b_rgate[0][cs], b_igate[0][cs], lru_lambda[0][cs]])),
            "w_rg": f(w_rgate[0][4 * m:4 * m + 4]),
            "w_ig": f(w_igate[0][4 * m:4 * m + 4]),
            "st_in": f(np.concatenate([state_lru[0][ss][:, cs], state_conv[0][ss][:, :, cs].reshape(48, 1024)], axis=0)),
            "w_out_a": f(w_out_a[0][:, fs]),
            "w_kv": f(np.concatenate([w_kv[:, fs], w_kv[:, 2048 + 512 * m:2048 + 512 * m + 512]], axis=1)),
            "w_in_b": f(np.concatenate([w_in_b[0][:, fs], w_in_b[0][:, 2048 + 512 * m:2048 + 512 * m + 512]], axis=1)),
            "w_out_b": f(w_out_b[0][:, fs]),
            "cache_k": f(cache_k[ss][:, :, 4 * m:4 * m + 4, :].reshape(16, 4096, 512)),
            "cache_v": f(cache_v[ss][:, :, 4 * m:4 * m + 4, :].reshape(16, 4096, 512)),
            "ident": ident, "tri": tri, "m16": m16, "sel": sel,
        }
        in_maps.append(d)
    return in_maps


def assemble(results):
    y_prompt = np.zeros((2, 4096, D), np.float32)
    y_sample = np.zeros((32, 16, D), np.float32)
    k_prompt = np.zeros((2, 4096, 16, 128), np.float32)
    v_prompt = np.zeros((2, 4096, 16, 128), np.float32)
    k_sample = np.zeros((32, 16, 16, 128), np.float32)
    v_sample = np.zeros((32, 16, 16, 128), np.float32)
    lru_prompt = np.zeros((1, 2, 4096), np.float32)
    lru_sample = np.zeros((1, 32, 4096), np.float32)
    conv_prompt = np.zeros((1, 2, 3, 4096), np.float32)
    conv_sample = np.zeros((1, 32, 3, 4096), np.float32)
    for c in range(8):
        g, m = c // 4, c % 4
        r = results[c]
        fs = slice(512 * m, 512 * m + 512)
        cs = slice(1024 * m, 1024 * m + 1024)
        ss = slice(16 * g, 16 * g + 16)
        y_prompt[g][:, fs] = r["y_sl"][:4096]
        y_sample[ss].reshape(256, D)[:, fs] = r["y_sl"][4096:]
        k_prompt[g].reshape(4096, D)[:, fs] = r["k_sl"][:4096]
        v_prompt[g].reshape(4096, D)[:, fs] = r["v_sl"][:4096]
        k_sample[ss].reshape(256, D)[:, fs] = r["k_sl"][4096:]
        v_sample[ss].reshape(256, D)[:, fs] = r["v_sl"][4096:]
        lru_prompt[0, g, cs] = r["lru_sl"][0]
        lru_sample[0, ss, cs] = r["lru_sl"][1:17]
        cv = r["conv_sl"].reshape(17, 3, 1024)
        conv_prompt[0, g, :, cs] = cv[0]
        conv_sample[0, ss, :, cs] = cv[1:17]
    return (y_prompt, y_sample, k_prompt, v_prompt, k_sample, v_sample,
            lru_prompt, lru_sample, conv_prompt, conv_sample)


def kernel(**inputs):
    inputs = {k: np.asarray(v) for k, v in inputs.items()}
    in_maps = make_in_maps(**inputs)
    if "nc" not in _NC_CACHE:
        _NC_CACHE["nc"] = build_nc()
    names = _NC_CACHE["in_names"]
    in_maps = [{k: d[k] for k in names} for d in in_maps]
    res = run_bass_kernel_spmd(_NC_CACHE["nc"], in_maps, core_ids=list(range(8)))
    return assemble(res.results)
```

```python
import numpy as np
import concourse.bass as bass
import concourse.mybir as mybir
from concourse.bass_utils import run_bass_kernel_spmd

F32 = mybir.dt.float32
BF16 = mybir.dt.bfloat16
AF = mybir.ActivationFunctionType
ALU = mybir.AluOpType

D = 2048
NPT = 4096
NST = 256
NT = NPT + NST
NTT = NT // 128
EPS = 1e-6
GROUPS = [[0, 1, 2, 3], [4, 5, 6, 7]]
SCALE = 128 ** -0.5


class Buf:
    __slots__ = ("name", "w", "r", "excl")

    def __init__(self, name="", excl=False):
        self.name = name
        self.w = None
        self.r = {}
        self.excl = excl


class _Eng:
    def __init__(self, fw, name, eng, same_sync):
        self.fw = fw
        self.name = name
        self.eng = eng
        self.epoch = 0
        self.count = 0
        self.sem = fw.nc.alloc_semaphore(name=f"es_{name}_0")
        self.key = (name, 0)
        fw.sems[self.key] = self.sem
        self.seen = {}
        self.same_sync = same_sync
        self.dslots = []
        self.dnext = 0

    def tick(self, inst):
        if self.count >= 30000:
            self.epoch += 1
            self.count = 0
            self.sem = self.fw.nc.alloc_semaphore(name=f"es_{self.name}_{self.epoch}")
            self.key = (self.name, self.epoch)
            self.fw.sems[self.key] = self.sem
        self.count += 1
        inst.then_inc(self.sem, 1)
        return (self.key, self.count)


class FW:
    def __init__(self, nc, same_sync=True):
        self.nc = nc
        self.sems = {}
        self.pe = _Eng(self, "pe", nc.tensor, False)
        self.act = _Eng(self, "act", nc.scalar, same_sync)
        self.dve = _Eng(self, "dve", nc.vector, same_sync)
        self.pool = _Eng(self, "pool", nc.gpsimd, same_sync)
        self.sp = _Eng(self, "sp", nc.sync, False)
        self.engs = [self.pe, self.act, self.dve, self.pool, self.sp]
        for q, n in ((self.sp, 16), (self.pool, 12)):
            for i in range(n):
                key = ("d" + q.name, i)
                self.sems[key] = nc.alloc_semaphore(name=f"ds_{q.name}_{i}")
                q.dslots.append([key, 0, None])
        self.n_cc = 0
        self.cc_tickets = []

    def _wait(self, E, ticket):
        key, val = ticket
        if key[0] == E.name and not E.same_sync:
            return
        if E.seen.get(key, 0) >= val:
            return
        E.eng.wait_ge(self.sems[key], val)
        E.seen[key] = val

    def _deps(self, E, reads, writes):
        deps = {}

        def add(t):
            if t is None:
                return
            k, v = t
            if deps.get(k, 0) < v:
                deps[k] = v
        for b in reads:
            add(b.w)
            if b.excl:
                for k, v in b.r.items():
                    if k[0] != E.name:
                        add((k, v))
        for b in writes:
            add(b.w)
            for k, v in b.r.items():
                add((k, v))
        for k, v in deps.items():
            self._wait(E, (k, v))

    def _commit(self, ticket, reads, writes):
        k, v = ticket
        for b in reads:
            if b.r.get(k, 0) < v:
                b.r[k] = v
        for b in writes:
            b.w = ticket
            b.r = {}

    def op(self, E, emit, reads=(), writes=()):
        self._deps(E, reads, writes)
        inst = emit(E.eng)
        t = E.tick(inst)
        self._commit(t, reads, writes)
        return t

    def dma(self, Q, out, in_, reads=(), writes=(), **kw):
        slot = Q.dslots[Q.dnext % len(Q.dslots)]
        Q.dnext += 1
        if slot[2] is not None:
            self._wait(Q, slot[2])
        self._deps(Q, reads, writes)
        inst = Q.eng.dma_start(out=out, in_=in_, **kw)
        slot[1] += 1
        t = (slot[0], 16 * slot[1])
        inst.then_inc(self.sems[slot[0]], 16)
        slot[2] = t
        self._commit(t, reads, writes)
        return t

    def collective(self, kind, in_ap, out_ap, reads=(), writes=()):
        Q = self.pool
        self._deps(Q, reads, writes)
        key = ("cc", self.n_cc)
        self.n_cc += 1
        self.sems[key] = self.nc.alloc_semaphore(name=f"cc_{key[1]}")
        inst = Q.eng.collective_compute(kind, ALU.bypass, replica_groups=GROUPS,
                                        ins=[in_ap], outs=[out_ap])
        inst.then_inc(self.sems[key], 1)
        t = (key, 1)
        self.cc_tickets.append(t)
        self._commit(t, reads, writes)
        return t

    def barrier(self):
        tickets = []
        for E in self.engs:
            if E.count > 0:
                tickets.append((E.key, E.count))
            for s in E.dslots:
                if s[2] is not None:
                    tickets.append(s[2])
        tickets += self.cc_tickets
        for E in self.engs:
            for t in tickets:
                if t[0][0] == E.name:
                    continue
                self._wait(E, t)


class Pool:
    def __init__(self, es, nc, name, shape, dtype, n):
        self.tiles = []
        for i in range(n):
            t = es.enter_context(nc.sbuf_tensor(f"p_{name}_{i}", list(shape), dtype))
            self.tiles.append((t, Buf(f"{name}_{i}")))
        self.i = 0

    def next(self):
        t = self.tiles[self.i % len(self.tiles)]
        self.i += 1
        return t


from contextlib import ExitStack


def build_nc(stages=9):
    nc = bass.Bass("TRN2", target_bir_lowering=False)

    in_names = []
    _NC_CACHE["in_names"] = in_names

    def din(name, shape, dt=F32):
        in_names.append(name)
        return nc.dram_tensor(name, list(shape), dt, kind="ExternalInput").ap()

    def dout(name, shape, dt=F32):
        return nc.dram_tensor(name, list(shape), dt, kind="ExternalOutput").ap()

    def dscr(name, shape, dt, local=False):
        if local:
            return nc.dram_tensor(name, list(shape), dt, kind="Internal", addr_space="Local").ap()
        return nc.dram_tensor(name, list(shape), dt, kind="Internal").ap()

    xg = din("xg", [NT, D])
    xg_sl = din("xg_sl", [NT, 512])
    cg = din("cg", [17, D])
    w_ada0 = din("w_ada0", [D, 4096])
    b_ada0 = din("b_ada0", [4096])
    w_adas = din("w_adas", [D, 2048])
    b_adas = din("b_adas", [2048])
    g_pre0 = din("g_pre0", [D])
    g_sl = din("g_sl", [4, 512])
    w_in_a = din("w_in_a", [D, 2048])
    conv_w = din("conv_w", [4, 1024])
    chp = din("chp", [4, 1024])
    w_rg = din("w_rg", [4, 256, 256])
    w_ig = din("w_ig", [4, 256, 256])
    st_in = din("st_in", [64, 1024])
    w_out_a = din("w_out_a", [4096, 512])
    w_kv = din("w_kv", [D, 1024])
    w_in_b = din("w_in_b", [D, 1024])
    w_out_b = din("w_out_b", [D, 512])
    if stages >= 7:
        cache_k = din("cache_k", [16, 4096, 512])
        cache_v = din("cache_v", [16, 4096, 512])
    ident_in = din("ident", [128, 128])
    tri_in = din("tri", [128, 128])
    m16_in = din("m16", [128, 16])
    sel_in = din("sel", [17, 3, 128])

    y_sl = dout("y_sl", [NT, 512])
    k_sl = dout("k_sl", [NT, 512])
    v_sl = dout("v_sl", [NT, 512])
    lru_sl = dout("lru_sl", [17, 1024])
    conv_sl = dout("conv_sl", [51, 1024])

    hn0_scr = dscr("hn0_scr", [D, NT], BF16)
    TBN = [512] * 8 + [256]
    y0_src = [dscr(f"y0_src{tb}", [1024, TBN[tb]], BF16) for tb in range(9)]
    y0_all = [dscr(f"y0_all{tb}", [4096, TBN[tb]], BF16, local=True) for tb in range(9)]
    st_src = [dscr(f"st_src{i}", [128, NTT], F32) for i in range(3)]
    st_all = [dscr(f"st_all{i}", [512, NTT], F32, local=True) for i in range(3)]
    x1_scr = dscr("x1_scr", [NT, 512], F32)
    hn_src = [dscr(f"hn_src{tb}", [1024, TBN[tb]], BF16) for tb in range(9)]
    hn_all = [dscr(f"hn_all{tb}", [4096, TBN[tb]], BF16, local=True) for tb in range(9)]
    qT_scr = dscr("qT_scr", [4, 128, NT], BF16)
    kT_scr = dscr("kT_scr", [4, 128, NT], BF16)
    sz_scr = dscr("sz_scr", [4, 128, NT], BF16)
    v_scr = dscr("v_scr", [NT, 512], BF16)
    y1_src = [dscr(f"y1_src{tb}", [512, TBN[tb]], BF16) for tb in range(9)]
    y1_all = [dscr(f"y1_all{tb}", [2048, TBN[tb]], BF16, local=True) for tb in range(9)]

    fw = FW(nc)
    pe, act, dve, pool, sp = fw.pe, fw.act, fw.dve, fw.pool, fw.sp
    d_hn0 = Buf()
    d_y0s = [Buf() for _ in range(9)]
    d_y0a = [Buf() for _ in range(9)]
    d_sts = [Buf() for _ in range(3)]
    d_sta = [Buf() for _ in range(3)]
    d_x1 = Buf()
    d_hns = [Buf() for _ in range(9)]
    d_hna = [Buf() for _ in range(9)]
    d_y1s = [Buf() for _ in range(9)]
    d_y1a = [Buf() for _ in range(9)]
    d_qT, d_kT, d_sz, d_v = Buf(), Buf(), Buf(), Buf()
    d_out = [Buf() for _ in range(5)]

    glob = ExitStack()

    def sb(es, name, shape, dt):
        return es.enter_context(nc.sbuf_tensor("s_" + name, list(shape), dt))

    psb = [(nc.alloc_psum_tensor(f"ps{i}", [128, 512], F32), Buf(f"ps{i}", excl=True)) for i in range(8)]
    ps_i = [0]

    ps_banks = [list(range(8))]

    def ps_next():
        bl = ps_banks[0]
        t = psb[bl[ps_i[0] % len(bl)]]
        ps_i[0] += 1
        return t

    ident_f = sb(glob, "ident_f", [128, 128], F32); b_idf = Buf()
    ident_b = sb(glob, "ident_b", [128, 128], BF16); b_idb = Buf()
    tri = sb(glob, "tri", [128, 128], F32); b_tri = Buf()
    m16 = sb(glob, "m16", [128, 16], F32); b_m16 = Buf()
    sel = sb(glob, "sel", [17, 3, 128], F32); b_sel = Buf()
    ones_c = sb(glob, "ones_c", [128, 1], F32); b_ones = Buf()
    GG0bc = sb(glob, "GG0bc", [128, 3, 512], F32); b_GG0 = Buf()
    A1bc = sb(glob, "A1bc", [128, 3, 512], F32); b_A1 = Buf()
    B1bc = sb(glob, "B1bc", [128, 3, 512], F32); b_B1 = Buf()
    GG1bc = sb(glob, "GG1bc", [128, 3, 512], F32); b_GG1 = Buf()
    GKVbc = sb(glob, "GKVbc", [128, 512], F32); b_GKV = Buf()
    stat = sb(glob, "stat", [128, 8, NTT], F32)
    b_stat = [Buf() for _ in range(8)]

    fw.dma(sp, ident_f[:], ident_in[:, :], writes=[b_idf])
    fw.dma(sp, tri[:], tri_in[:, :], writes=[b_tri])
    fw.dma(sp, m16[:], m16_in[:, :], writes=[b_m16])
    fw.dma(sp, sel[:], sel_in[:, :, :], writes=[b_sel])
    fw.dma(sp, GKVbc[:], g_sl[3, :].partition_broadcast(128), writes=[b_GKV])
    fw.op(act, lambda e: e.activation(out=ident_b[:], in_=ident_f[:], func=AF.Copy), reads=[b_idf], writes=[b_idb])
    fw.op(pool, lambda e: e.memset(ones_c[:], 1.0), writes=[b_ones])

    def rg_of(tile):
        return 0 if tile < 32 else tile - 31

    def rstd_from(ss_i, r_i, ncols=NTT):
        fw.op(dve, lambda e: e.tensor_scalar(out=stat[:, r_i, 0:ncols], in0=stat[:, ss_i, 0:ncols], scalar1=1.0 / D,
                                             scalar2=EPS, op0=ALU.mult, op1=ALU.add),
              reads=[b_stat[ss_i]], writes=[b_stat[r_i]])
        fw.op(act, lambda e: e.activation(out=stat[:, r_i, 0:ncols], in_=stat[:, r_i, 0:ncols], func=AF.Sqrt),
              reads=[b_stat[r_i]], writes=[b_stat[r_i]])
        fw.op(dve, lambda e: e.reciprocal(out=stat[:, r_i, 0:ncols], in_=stat[:, r_i, 0:ncols]),
              reads=[b_stat[r_i]], writes=[b_stat[r_i]])

    with ExitStack() as es:
        A0bc = sb(es, "A0bc", [128, 3, D], F32); b_A0 = Buf()
        B0bc = sb(es, "B0bc", [128, 3, D], F32); b_B0 = Buf()
        with ExitStack() as es0:
            cgt = sb(es0, "cgt", [17, D], F32); b_cgt = Buf()
            scT = sb(es0, "scT", [128, 16, 17], BF16); b_scT = Buf()
            modrow = sb(es0, "modrow", [17, 6144], F32); b_mod = Buf()
            bbp = Pool(es0, nc, "bbc", [17, 512], F32, 2)
            g0bc = sb(es0, "g0bc", [17, D], F32); b_g0 = Buf()
            gslbc = sb(es0, "gslbc", [17, 3, 512], F32); b_gsl = Buf()
            rows = sb(es0, "rows", [17, 4, 512], F32); b_rows = Buf()
            A0row = sb(es0, "A0row", [17, D], F32); b_A0r = Buf()
            wap = Pool(es0, nc, "wap", [128, 16, 512], BF16, 2)

            fw.dma(sp, cgt[:], cg[:, :], writes=[b_cgt])
            fw.dma(sp, g0bc[:], g_pre0.partition_broadcast(17), writes=[b_g0])
            for i in range(3):
                fw.dma(sp, gslbc[:, i, :], g_sl[i, :].partition_broadcast(17), writes=[b_gsl])
            fw.op(act, lambda e: e.activation(out=cgt[:], in_=cgt[:], func=AF.Silu), reads=[b_cgt], writes=[b_cgt])
            pt, pb = ps_next()
            for kt in range(16):
                fw.op(pe, lambda e: e.transpose(pt[:, kt * 17:(kt + 1) * 17], cgt[0:17, kt * 128:(kt + 1) * 128],
                                                ident_f[0:17, 0:17]),
                      reads=[b_cgt, b_idf], writes=[pb])
            fw.op(act, lambda e: e.activation(out=scT[:].rearrange("p a b -> p (a b)"), in_=pt[:, 0:272], func=AF.Copy),
                  reads=[pb], writes=[b_scT])
            for cc in range(12):
                wt, wb = wap.next()
                src = (w_ada0 if cc < 8 else w_adas).rearrange("(kt p) c -> p kt c", p=128)
                c0 = (cc % 8) * 512 if cc < 8 else (cc - 8) * 512
                fw.dma(pool, wt[:], src[:, :, c0:c0 + 512], writes=[wb])
                bbc, b_bbc = bbp.next()
                fw.dma(sp, bbc[:], (b_ada0 if cc < 8 else b_adas)[c0:c0 + 512].partition_broadcast(17), writes=[b_bbc])
                pt, pb = ps_next()
                for kt in range(16):
                    fw.op(pe, lambda e: e.matmul(pt[0:17, :], lhsT=scT[:, kt, :], rhs=wt[:, kt, :],
                                                 start=(kt == 0), stop=(kt == 15)),
                          reads=[b_scT, wb], writes=[pb])
                fw.op(dve, lambda e: e.tensor_tensor(out=modrow[:, cc * 512:(cc + 1) * 512], in0=pt[0:17, :],
                                                     in1=bbc[:, :], op=ALU.add),
                      reads=[pb, b_bbc], writes=[b_mod])
            fw.op(dve, lambda e: e.scalar_tensor_tensor(out=A0row[:], in0=modrow[:, 2048:4096], scalar=1.0, in1=g0bc[:],
                                                        op0=ALU.add, op1=ALU.mult),
                  reads=[b_mod, b_g0], writes=[b_A0r])
            fw.op(dve, lambda e: e.tensor_tensor(out=rows[:, 0, :], in0=modrow[:, 4096:4608], in1=gslbc[:, 0, :], op=ALU.mult),
                  reads=[b_mod, b_gsl], writes=[b_rows])
            fw.op(dve, lambda e: e.scalar_tensor_tensor(out=rows[:, 1, :], in0=modrow[:, 5120:5632], scalar=1.0,
                                                        in1=gslbc[:, 1, :], op0=ALU.add, op1=ALU.mult),
                  reads=[b_mod, b_gsl], writes=[b_rows])
            fw.op(dve, lambda e: e.tensor_tensor(out=rows[:, 2, :], in0=modrow[:, 5632:6144], in1=gslbc[:, 2, :], op=ALU.mult),
                  reads=[b_mod, b_gsl], writes=[b_rows])
            jobs = []
            for rg in range(3):
                for c4 in range(4):
                    jobs.append((A0row[:, c4 * 512:(c4 + 1) * 512], b_A0r, A0bc[:, rg, c4 * 512:(c4 + 1) * 512], b_A0, rg))
                    jobs.append((modrow[:, c4 * 512:(c4 + 1) * 512], b_mod, B0bc[:, rg, c4 * 512:(c4 + 1) * 512], b_B0, rg))
                jobs.append((rows[:, 0, :], b_rows, GG0bc[:, rg, :], b_GG0, rg))
                jobs.append((rows[:, 1, :], b_rows, A1bc[:, rg, :], b_A1, rg))
                jobs.append((modrow[:, 4608:5120], b_mod, B1bc[:, rg, :], b_B1, rg))
                jobs.append((rows[:, 2, :], b_rows, GG1bc[:, rg, :], b_GG1, rg))
            for ji, (src_ap, src_b, dst_ap, dst_b, rg) in enumerate(jobs):
                pt, pb = ps_next()
                fw.op(pe, lambda e: e.matmul(pt[:, :], lhsT=sel[:, rg, :], rhs=src_ap, start=True, stop=True),
                      reads=[b_sel, src_b], writes=[pb])
                E = act if ji % 2 == 0 else dve
                if E is act:
                    fw.op(act, lambda e: e.activation(out=dst_ap, in_=pt[:, :], func=AF.Copy), reads=[pb], writes=[dst_b])
                else:
                    fw.op(dve, lambda e: e.tensor_copy(out=dst_ap, in_=pt[:, :]), reads=[pb], writes=[dst_b])
            fw.barrier()

        with ExitStack() as es1:
            xtp = Pool(es1, nc, "xt", [128, D], F32, 3)
            junk = sb(es1, "junk", [128, D], BF16); b_junk = Buf()
            hnp = Pool(es1, nc, "hn", [128, D], BF16, 2)
            stg = Pool(es1, nc, "hstg", [128, 16, 512], BF16, 2)
            tmpp = Pool(es1, nc, "tmp1", [128, 4], F32, 4)
            for tb in range(9):
                n128 = 4 if tb < 8 else 2
                st_t, st_b = stg.next()
                for j in range(n128):
                    tile = tb * 4 + j
                    rg = rg_of(tile)
                    xt, xb_ = xtp.next()
                    tmp1, b_tmp1 = tmpp.next()
                    fw.dma(sp, xt[:], xg[tile * 128:(tile + 1) * 128, :], writes=[xb_])
                    fw.op(act, lambda e: e.activation(out=junk[:], in_=xt[:], func=AF.Square, accum_out=tmp1[:, 0:1]),
                          reads=[xb_], writes=[b_junk, b_tmp1])
                    fw.op(dve, lambda e: e.tensor_scalar(out=tmp1[:, 1:2], in0=tmp1[:, 0:1], scalar1=1.0 / D, scalar2=EPS,
                                                         op0=ALU.mult, op1=ALU.add), reads=[b_tmp1], writes=[b_tmp1])
                    fw.op(act, lambda e: e.activation(out=tmp1[:, 2:3], in_=tmp1[:, 1:2], func=AF.Sqrt),
                          reads=[b_tmp1], writes=[b_tmp1])
                    fw.op(dve, lambda e: e.reciprocal(out=tmp1[:, 3:4], in_=tmp1[:, 2:3]), reads=[b_tmp1], writes=[b_tmp1])
                    fw.op(dve, lambda e: e.scalar_tensor_tensor(out=xt[:], in0=xt[:], scalar=tmp1[:, 3:4], in1=A0bc[:, rg, :],
                                                                op0=ALU.mult, op1=ALU.mult),
                          reads=[xb_, b_tmp1, b_A0], writes=[xb_])
                    hn, hb = hnp.next()
                    fw.op(pool, lambda e: e.tensor_tensor(out=hn[:], in0=xt[:], in1=B0bc[:, rg, :], op=ALU.add),
                          reads=[xb_, b_B0], writes=[hb])
                    for half in range(2):
                        pt, pb = ps_next()
                        ptb = pt.bitcast(BF16)
                        for k8 in range(8):
                            kt = half * 8 + k8
                            fw.op(pe, lambda e: e.transpose(ptb[:, k8 * 128:(k8 + 1) * 128], hn[:, kt * 128:(kt + 1) * 128],
                                                            ident_b[:]),
                                  reads=[hb, b_idb], writes=[pb])
                        dst = st_t[:, half * 8:(half + 1) * 8, j * 128:(j + 1) * 128]
                        srcp = ptb[:, :].rearrange("p (a b) -> p a b", a=8)
                        if half == 0:
                            fw.op(act, lambda e: e.activation(out=dst, in_=srcp, func=AF.Copy), reads=[pb], writes=[st_b])
                        else:
                            fw.op(dve, lambda e: e.tensor_copy(out=dst, in_=srcp), reads=[pb], writes=[st_b])
                n = n128 * 128
                fw.dma(pool, hn0_scr.rearrange("(kt p) t -> p kt t", p=128)[:, :, tb * 512:tb * 512 + n], st_t[:, :, 0:n],
                       reads=[st_b], writes=[d_hn0])
            fw.barrier()
    if stages <= 1:
        fw.barrier()
        return nc

    with ExitStack() as es:
        wA = sb(es, "wA", [128, 16, 2048], BF16); b_wA = Buf()
        wR = sb(es, "wR", [128, 8, 256], BF16); b_wR = Buf()
        wI = sb(es, "wI", [128, 8, 256], BF16); b_wI = Buf()
        cw = sb(es, "cw", [128, 4, 8], F32); b_cw = Buf()
        cp = sb(es, "cp", [128, 4, 8], F32); b_cp = Buf()
        negc = sb(es, "negc", [128, 2, 8], F32); b_negc = Buf()
        strow = sb(es, "strow", [64, 1024], F32); b_strow = Buf()
        stT = sb(es, "stT", [128, 8, 64], F32); b_stT = Buf()
        xe = sb(es, "xe", [128, 8, 515], F32); b_xe = [Buf() for _ in range(8)]
        b_xes = b_xe

        def xesv(ct_):
            return xe[:, ct_, 0:304].rearrange("p (s j) -> p s j", j=19)
        hc = sb(es, "hc", [128, 8], F32); b_hc = [Buf() for _ in range(8)]
        hfin = sb(es, "hfin", [128, 8, 17], F32); b_hfin = Buf()
        xlast = sb(es, "xlast", [128, 8, 17, 3], F32); b_xlast = Buf()
        hnT = Pool(es, nc, "hnT", [128, 16, 512], BF16, 2)
        xcp = Pool(es, nc, "xc", [128, 512], F32, 4)
        xcbp = Pool(es, nc, "xcb", [128, 512], BF16, 4)
        szp = Pool(es, nc, "sz", [128, 512], F32, 4)
        rp = Pool(es, nc, "r", [128, 512], F32, 2)
        ip = Pool(es, nc, "ig", [128, 512], F32, 2)
        ap_ = Pool(es, nc, "a", [128, 512], F32, 2)
        a2p = Pool(es, nc, "a2", [128, 512], F32, 2)
        up = Pool(es, nc, "u", [128, 512], F32, 2)
        hp = Pool(es, nc, "h", [128, 512], F32, 2)
        yp = Pool(es, nc, "y", [128, 512], BF16, 3)

        for kq in range(4):
            fw.dma(pool, wA[:, kq * 4:(kq + 1) * 4, :],
                   w_in_a.rearrange("(kt p) c -> p kt c", p=128)[:, kq * 4:(kq + 1) * 4, :], writes=[b_wA])
        fw.dma(pool, wR[:], w_rg.rearrange("b (kh p) j -> p (b kh) j", p=128), writes=[b_wR])
        fw.dma(pool, wI[:], w_ig.rearrange("b (kh p) j -> p (b kh) j", p=128), writes=[b_wI])
        with nc.allow_non_contiguous_dma(reason="tiny per-channel params"):
            fw.dma(sp, cw[:], conv_w.rearrange("i (ct p) -> p i ct", p=128), writes=[b_cw])
            fw.dma(sp, cp[:], chp.rearrange("i (ct p) -> p i ct", p=128), writes=[b_cp])
        fw.dma(sp, strow[:], st_in[:, :], writes=[b_strow])
        fw.op(act, lambda e: e.activation(out=negc[:, 0, :], in_=cp[:, 3, :], func=AF.Exp, scale=-1.0), reads=[b_cp], writes=[b_negc])
        fw.op(act, lambda e: e.activation(out=negc[:, 0, :], in_=negc[:, 0, :], func=AF.Ln, bias=1.0), reads=[b_negc], writes=[b_negc])
        fw.op(dve, lambda e: e.tensor_scalar(out=negc[:, 1, :], in0=negc[:, 0, :], scalar1=-16.0, scalar2=None, op0=ALU.mult),
              reads=[b_negc], writes=[b_negc])
        fw.op(dve, lambda e: e.tensor_scalar(out=negc[:, 0, :], in0=negc[:, 0, :], scalar1=-8.0, scalar2=None, op0=ALU.mult),
              reads=[b_negc], writes=[b_negc])
        pt, pb = ps_next()
        for ct in range(8):
            fw.op(pe, lambda e: e.transpose(pt[:, ct * 64:(ct + 1) * 64], strow[0:64, ct * 128:(ct + 1) * 128], ident_f[0:64, 0:64]),
                  reads=[b_strow, b_idf], writes=[pb])
        fw.op(act, lambda e: e.activation(out=stT[:].rearrange("p a b -> p (a b)"), in_=pt[:, :], func=AF.Copy), reads=[pb], writes=[b_stT])
        for ct in range(8):
            fw.op(pool, lambda e: e.memset(xe[:, ct, 0:3], 0.0), writes=[b_xe[ct]])
            fw.op(pool, lambda e: e.memset(hc[:, ct:ct + 1], 0.0), writes=[b_hc[ct]])

        for tb in range(9):
            samp = tb == 8
            n = 256 if samp else 512
            ht, hb_ = hnT.next()
            fw.dma(sp, ht[:, :, 0:n], hn0_scr.rearrange("(kt p) t -> p kt t", p=128)[:, :, tb * 512:tb * 512 + n],
                   reads=[d_hn0], writes=[hb_])
            if samp:
                for ct in range(8):
                    fw.op(pool, lambda e: e.tensor_copy(out=xesv(ct)[:, :, 0:3],
                                                        in_=stT[:, ct, 16:64].rearrange("p (s i) -> p s i", i=3)),
                          reads=[b_stT], writes=[b_xes[ct]])
            for blk in range(4):
                xcs, xcbs, szs = [], [], []
                for c2 in range(2):
                    ct = blk * 2 + c2
                    pt, pb = ps_next()
                    for kt in range(16):
                        fw.op(pe, lambda e: e.matmul(pt[:, 0:n], lhsT=wA[:, kt, ct * 128:(ct + 1) * 128], rhs=ht[:, kt, 0:n],
                                                     start=(kt == 0), stop=(kt == 15)), reads=[b_wA, hb_], writes=[pb])
                    if samp:
                        xv = xesv(ct)[:, :, 3:19]
                        bx = b_xes[ct]
                        fw.op(act, lambda e: e.activation(out=xv, in_=pt[:, 0:n].rearrange("p (s t) -> p s t", t=16), func=AF.Copy),
                              reads=[pb], writes=[bx])
                    else:
                        bx = b_xe[ct]
                        fw.op(act, lambda e: e.activation(out=xe[:, ct, 3:515], in_=pt[:, 0:n], func=AF.Copy), reads=[pb], writes=[bx])
                    pt2, pb2 = ps_next()
                    for kt in range(16):
                        fw.op(pe, lambda e: e.matmul(pt2[:, 0:n], lhsT=wA[:, kt, 1024 + ct * 128:1024 + (ct + 1) * 128],
                                                     rhs=ht[:, kt, 0:n], start=(kt == 0), stop=(kt == 15)),
                              reads=[b_wA, hb_], writes=[pb2])
                    sz, szb = szp.next()
                    fw.op(act, lambda e: e.activation(out=sz[:, 0:n], in_=pt2[:, 0:n], func=AF.Silu), reads=[pb2], writes=[szb])
                    szs.append((sz, szb))
                    xc, xcb_ = xcp.next()

                    def xin(i):
                        if samp:
                            return xesv(ct)[:, :, i:i + 16]
                        return xe[:, ct, i:i + 512]

                    def xo(t_):
                        if samp:
                            return t_[:, 0:n].rearrange("p (s t) -> p s t", t=16)
                        return t_[:, 0:n]
                    fw.op(dve, lambda e: e.tensor_scalar(out=xo(xc), in0=xin(3), scalar1=cw[:, 3, ct:ct + 1], scalar2=cp[:, 0, ct:ct + 1],
                                                         op0=ALU.mult, op1=ALU.add), reads=[bx, b_cw, b_cp], writes=[xcb_])
                    for i in (2, 1, 0):
                        fw.op(dve, lambda e: e.scalar_tensor_tensor(out=xo(xc), in0=xin(i), scalar=cw[:, i, ct:ct + 1], in1=xo(xc),
                                                                    op0=ALU.mult, op1=ALU.add), reads=[bx, b_cw, xcb_], writes=[xcb_])
                    xcb, xcbb = xcbp.next()
                    fw.op(pool, lambda e: e.tensor_copy(out=xcb[:, 0:n], in_=xc[:, 0:n]), reads=[xcb_], writes=[xcbb])
                    if samp:
                        fw.op(pool, lambda e: e.tensor_copy(out=xlast[:, ct, 1:17, :], in_=xesv(ct)[:, :, 16:19]),
                              reads=[bx], writes=[b_xlast])
                    else:
                        if tb == 7:
                            fw.op(pool, lambda e: e.tensor_copy(out=xlast[:, ct, 0, :], in_=xe[:, ct, 512:515]),
                                  reads=[bx], writes=[b_xlast])
                        fw.op(pool, lambda e: e.tensor_copy(out=xe[:, ct, 0:3], in_=xe[:, ct, 512:515]), reads=[bx], writes=[bx])
                    xcs.append((xc, xcb_))
                    xcbs.append((xcb, xcbb))
                for c2 in range(2):
                    ct = blk * 2 + c2
                    xc, xcb_ = xcs[c2]
                    sz, szb = szs[c2]
                    r, rb = rp.next()
                    ig, ib = ip.next()
                    for (wG, bwG, bias_i, dst, dstb) in ((wR, b_wR, 1, r, rb), (wI, b_wI, 2, ig, ib)):
                        pt, pb = ps_next()
                        for kh in range(2):
                            fw.op(pe, lambda e: e.matmul(pt[:, 0:n], lhsT=wG[:, blk * 2 + kh, c2 * 128:(c2 + 1) * 128],
                                                         rhs=xcbs[kh][0][:, 0:n], start=(kh == 0), stop=(kh == 1)),
                                  reads=[bwG, xcbs[kh][1]], writes=[pb])
                        fw.op(act, lambda e: e.activation(out=dst[:, 0:n], in_=pt[:, 0:n], func=AF.Sigmoid,
                                                          bias=cp[:, bias_i, ct:ct + 1]), reads=[pb, b_cp], writes=[dstb])
                    a, ab = ap_.next()
                    a2, a2b = a2p.next()
                    fw.op(act, lambda e: e.activation(out=a[:, 0:n], in_=r[:, 0:n], func=AF.Exp, scale=negc[:, 0, ct:ct + 1]),
                          reads=[rb, b_negc], writes=[ab])
                    fw.op(act, lambda e: e.activation(out=a2[:, 0:n], in_=r[:, 0:n], func=AF.Exp, scale=negc[:, 1, ct:ct + 1]),
                          reads=[rb, b_negc], writes=[a2b])
                    fw.op(act, lambda e: e.activation(out=a2[:, 0:n], in_=a2[:, 0:n], func=AF.Sqrt, scale=-1.0, bias=1.0),
                          reads=[a2b], writes=[a2b])
                    u, ub = up.next()
                    fw.op(pool, lambda e: e.tensor_tensor(out=u[:, 0:n], in0=ig[:, 0:n], in1=xc[:, 0:n], op=ALU.mult),
                          reads=[ib, xcb_], writes=[ub])
                    fw.op(dve, lambda e: e.tensor_tensor(out=u[:, 0:n], in0=u[:, 0:n], in1=a2[:, 0:n], op=ALU.mult),
                          reads=[ub, a2b], writes=[ub])
                    h, hb2 = hp.next()
                    if samp:
                        for s in range(16):
                            fw.op(dve, lambda e: e.tensor_tensor_scan(out=h[:, s * 16:(s + 1) * 16], data0=a[:, s * 16:(s + 1) * 16],
                                                                      data1=u[:, s * 16:(s + 1) * 16], initial=stT[:, ct, s:s + 1],
                                                                      op0=ALU.mult, op1=ALU.add),
                                  reads=[ab, ub, b_stT], writes=[hb2])
                        fw.op(pool, lambda e: e.tensor_copy(out=hfin[:, ct, 1:17],
                                                            in_=h[:, 0:n].rearrange("p (s t) -> p s t", t=16)[:, :, 15]),
                              reads=[hb2], writes=[b_hfin])
                    else:
                        fw.op(dve, lambda e: e.tensor_tensor_scan(out=h[:, 0:n], data0=a[:, 0:n], data1=u[:, 0:n],
                                                                  initial=hc[:, ct:ct + 1], op0=ALU.mult, op1=ALU.add),
                              reads=[ab, ub, b_hc[ct]], writes=[hb2])
                        fw.op(pool, lambda e: e.tensor_copy(out=hc[:, ct:ct + 1], in_=h[:, n - 1:n]), reads=[hb2], writes=[b_hc[ct]])
                        if tb == 7:
                            fw.op(pool, lambda e: e.tensor_copy(out=hfin[:, ct, 0:1], in_=h[:, n - 1:n]), reads=[hb2], writes=[b_hfin])
                    y, yb = yp.next()
                    fw.op(dve, lambda e: e.tensor_tensor(out=y[:, 0:n], in0=h[:, 0:n], in1=sz[:, 0:n], op=ALU.mult),
                          reads=[hb2, szb], writes=[yb])
                    fw.dma(pool, y0_src[tb][ct * 128:(ct + 1) * 128, :], y[:, 0:n], reads=[yb], writes=[d_y0s[tb]])
            fw.collective("AllGather", y0_src[tb][:, :], y0_all[tb][:, :], reads=[d_y0s[tb]], writes=[d_y0a[tb]])
        so = strow; b_so = b_strow
        for (srcT, bsrc, ncols, r0) in ((hfin, b_hfin, 17, 0), (xlast, b_xlast, 51, 17)):
            for half in range(2):
                pt, pb = ps_next()
                for c4 in range(4):
                    ct = half * 4 + c4
                    s_ap = srcT[:, ct, :] if ncols == 17 else srcT[:, ct, :, :].rearrange("p s i -> p (s i)")
                    fw.op(pe, lambda e: e.transpose(pt[0:ncols, c4 * 128:(c4 + 1) * 128], s_ap, ident_f[:]),
                          reads=[bsrc, b_idf], writes=[pb])
                stage = so[0:ncols, half * 512:(half + 1) * 512]
                fw.op(act, lambda e: e.activation(out=stage, in_=pt[0:ncols, :], func=AF.Copy), reads=[pb], writes=[b_so])
            if ncols == 17:
                fw.dma(pool, lru_sl[:, :], so[0:17, :], reads=[b_so], writes=[d_out[3]])
            else:
                fw.dma(pool, conv_sl[:, :], so[0:51, :], reads=[b_so], writes=[d_out[4]])
        fw.barrier()
    if stages <= 2:
        fw.barrier()
        return nc

    def exchange_stats(es_, i_ss, i_r, which):
        sg = sb(es_, f"sg{which}", [128, 4, NTT], F32); b_sg = Buf()
        fw.dma(pool, st_src[which][:, :], stat[:, i_ss, :], reads=[b_stat[i_ss]], writes=[d_sts[which]])
        fw.collective("AllGather", st_src[which][:, :], st_all[which][:, :], reads=[d_sts[which]], writes=[d_sta[which]])
        fw.dma(sp, sg[:], st_all[which].rearrange("(r p) t -> p r t", p=128), reads=[d_sta[which]], writes=[b_sg])
        fw.op(dve, lambda e: e.tensor_tensor(out=stat[:, i_ss, :], in0=sg[:, 0, :], in1=sg[:, 1, :], op=ALU.add),
              reads=[b_sg], writes=[b_stat[i_ss]])
        for r_ in (2, 3):
            fw.op(dve, lambda e: e.tensor_tensor(out=stat[:, i_ss, :], in0=stat[:, i_ss, :], in1=sg[:, r_, :], op=ALU.add),
                  reads=[b_sg, b_stat[i_ss]], writes=[b_stat[i_ss]])
        rstd_from(i_ss, i_r)

    with ExitStack() as esR:
        R1 = sb(esR, "R1", [128, NTT, 512], F32); b_R1 = [Buf() for _ in range(NTT)]
        with ExitStack() as es:
            wO = sb(es, "wO", [128, 32, 512], BF16); b_wO = Buf()
            y0p = Pool(es, nc, "y0t", [128, 32, 512], BF16, 2)
            junk3 = sb(es, "junk3", [128, 512], BF16); b_j3 = Buf()
            xsp = Pool(es, nc, "xsl", [128, 512], F32, 3)
            for kq in range(4):
                fw.dma(pool, wO[:, kq * 8:(kq + 1) * 8, :],
                       w_out_a.rearrange("(kt p) f -> p kt f", p=128)[:, kq * 8:(kq + 1) * 8, :], writes=[b_wO])
            for tb in range(9):
                n128 = 4 if tb < 8 else 2
                n = n128 * 128
                yt, ytb = y0p.next()
                fw.dma(sp, yt[:, :, 0:n], y0_all[tb].rearrange("(kt p) t -> p kt t", p=128),
                       reads=[d_y0a[tb]], writes=[ytb])
                for j in range(n128):
                    tile = tb * 4 + j
                    pt, pb = ps_next()
                    for kt in range(32):
                        fw.op(pe, lambda e: e.matmul(pt[:, :], lhsT=yt[:, kt, j * 128:(j + 1) * 128], rhs=wO[:, kt, :],
                                                     start=(kt == 0), stop=(kt == 31)), reads=[ytb, b_wO], writes=[pb])
                    fw.op(act, lambda e: e.activation(out=R1[:, tile, :], in_=pt[:, :], func=AF.Copy), reads=[pb], writes=[b_R1[tile]])
                    fw.op(act, lambda e: e.activation(out=junk3[:], in_=pt[:, :], func=AF.Square,
                                                      accum_out=stat[:, 2, tile:tile + 1]),
                          reads=[pb], writes=[b_j3, b_stat[2]])
            if stages == 3:
                fw.barrier()
                return nc
            exchange_stats(es, 2, 3, 0)
            if stages == 3.5:
                fw.barrier()
                return nc
            for tile in range(NTT):
                rg = rg_of(tile)
                xs, xsb = xsp.next()
                fw.dma(sp, xs[:], xg_sl[tile * 128:(tile + 1) * 128, :], writes=[xsb])
                fw.op(dve, lambda e: e.scalar_tensor_tensor(out=R1[:, tile, :], in0=R1[:, tile, :], scalar=stat[:, 3, tile:tile + 1],
                                                            in1=GG0bc[:, rg, :], op0=ALU.mult, op1=ALU.mult),
                      reads=[b_R1[tile], b_stat[3], b_GG0], writes=[b_R1[tile]])
                fw.op(pool, lambda e: e.tensor_tensor(out=R1[:, tile, :], in0=R1[:, tile, :], in1=xs[:], op=ALU.add),
                      reads=[b_R1[tile], xsb], writes=[b_R1[tile]])
                fw.op(act, lambda e: e.activation(out=junk3[:], in_=R1[:, tile, :], func=AF.Square,
                                                  accum_out=stat[:, 4, tile:tile + 1]),
                      reads=[b_R1[tile]], writes=[b_j3, b_stat[4]])
                fw.dma(pool, x1_scr[tile * 128:(tile + 1) * 128, :], R1[:, tile, :], reads=[b_R1[tile]], writes=[d_x1])
            exchange_stats(es, 4, 5, 1)
            fw.barrier()
        with ExitStack() as es:
            t4p = Pool(es, nc, "t4", [128, 512], F32, 2)
            h1p = Pool(es, nc, "h1", [128, 512], BF16, 2)
            hkp = Pool(es, nc, "hk", [128, 512], BF16, 2)
            stg4 = Pool(es, nc, "stg4", [128, 8, 512], BF16, 2)
            for tb in range(9):
                n128 = 4 if tb < 8 else 2
                n = n128 * 128
                st_t, st_b = stg4.next()
                for j in range(n128):
                    tile = tb * 4 + j
                    rg = rg_of(tile)
                    t4, t4b = t4p.next()
                    fw.op(dve, lambda e: e.scalar_tensor_tensor(out=t4[:], in0=R1[:, tile, :], scalar=stat[:, 5, tile:tile + 1],
                                                                in1=A1bc[:, rg, :], op0=ALU.mult, op1=ALU.mult),
                          reads=[b_R1[tile], b_stat[5], b_A1], writes=[t4b])
                    h1, h1b = h1p.next()
                    fw.op(pool, lambda e: e.tensor_tensor(out=h1[:], in0=t4[:], in1=B1bc[:, rg, :], op=ALU.add),
                          reads=[t4b, b_B1], writes=[h1b])
                    hk, hkb = hkp.next()
                    fw.op(dve, lambda e: e.scalar_tensor_tensor(out=hk[:], in0=R1[:, tile, :], scalar=stat[:, 5, tile:tile + 1],
                                                                in1=GKVbc[:, :], op0=ALU.mult, op1=ALU.mult),
                          reads=[b_R1[tile], b_stat[5], b_GKV], writes=[hkb])
                    pt, pb = ps_next()
                    ptb = pt.bitcast(BF16)
                    for f4 in range(4):
                        fw.op(pe, lambda e: e.transpose(ptb[:, f4 * 128:(f4 + 1) * 128], hk[:, f4 * 128:(f4 + 1) * 128], ident_b[:]),
                              reads=[hkb, b_idb], writes=[pb])
                    for f4 in range(4):
                        fw.op(pe, lambda e: e.transpose(ptb[:, (4 + f4) * 128:(5 + f4) * 128], h1[:, f4 * 128:(f4 + 1) * 128], ident_b[:]),
                              reads=[h1b, b_idb], writes=[pb])
                    fw.op(act, lambda e: e.activation(out=st_t[:, :, j * 128:(j + 1) * 128],
                                                      in_=ptb[:, :].rearrange("p (a b) -> p a b", a=8), func=AF.Copy),
                          reads=[pb], writes=[st_b])
                fw.dma(pool, hn_src[tb].rearrange("(k p) t -> p k t", p=128), st_t[:, :, 0:n],
                       reads=[st_b], writes=[d_hns[tb]])
                fw.collective("AllGather", hn_src[tb][:, :], hn_all[tb][:, :], reads=[d_hns[tb]], writes=[d_hna[tb]])
            fw.barrier()
    if stages <= 4:
        fw.barrier()
        return nc
    lvl = 9 if stages >= 5 else int(round((stages - 4) * 10))

    with ExitStack() as es:
        wQ = sb(es, "wQ", [128, 16, 1024], BF16); b_wQ = Buf()
        wKV = sb(es, "wKV", [128, 16, 1024], BF16); b_wKV = Buf()
        h1T = Pool(es, nc, "h1T", [128, 16, 512], BF16, 2)
        hkT = Pool(es, nc, "hkT", [128, 16, 512], BF16, 2)
        qst = Pool(es, nc, "qst", [128, 512], BF16, 3)
        kvf = Pool(es, nc, "kvf", [128, 512], F32, 3)
        kvb = Pool(es, nc, "kvb", [128, 512], BF16, 3)
        kts = Pool(es, nc, "kts", [128, 4, 128], BF16, 2)
        for kq in range(4):
            fw.dma(pool, wQ[:, kq * 4:(kq + 1) * 4, :], w_in_b.rearrange("(kt p) c -> p kt c", p=128)[:, kq * 4:(kq + 1) * 4, :], writes=[b_wQ])
            fw.dma(pool, wKV[:, kq * 4:(kq + 1) * 4, :], w_kv.rearrange("(kt p) c -> p kt c", p=128)[:, kq * 4:(kq + 1) * 4, :], writes=[b_wKV])
        for tb in range(9):
            n128 = 4 if tb < 8 else 2
            n = n128 * 128
            c0 = tb * 512
            a1, a1b = h1T.next()
            ak, akb = hkT.next()
            hview = hn_all[tb].rearrange("(r k f p) t -> k p r f t", r=4, k=2, f=4, p=128)
            for r_ in range(4):
                fw.dma(sp, ak[:, r_ * 4:(r_ + 1) * 4, 0:n], hview[0][:, r_, :, :], reads=[d_hna[tb]], writes=[akb])
                fw.dma(sp, a1[:, r_ * 4:(r_ + 1) * 4, 0:n], hview[1][:, r_, :, :], reads=[d_hna[tb]], writes=[a1b])
            for hh in range(4):
                for which in range(2):
                    pt, pb = ps_next()
                    for kt in range(16):
                        fw.op(pe, lambda e: e.matmul(pt[:, 0:n], lhsT=wQ[:, kt, which * 512 + hh * 128:which * 512 + (hh + 1) * 128],
                                                     rhs=a1[:, kt, 0:n], start=(kt == 0), stop=(kt == 15)),
                              reads=[b_wQ, a1b], writes=[pb])
                    st, stb = qst.next()
                    fw.op(act, lambda e: e.activation(out=st[:, 0:n], in_=pt[:, 0:n], func=(AF.Copy if which == 0 else AF.Silu)),
                          reads=[pb], writes=[stb])
                    if lvl < 4:
                        continue
                    if which == 0:
                        fw.dma(pool, qT_scr[hh, :, c0:c0 + n], st[:, 0:n], reads=[stb], writes=[d_qT])
                    else:
                        fw.dma(pool, sz_scr[hh, :, c0:c0 + n], st[:, 0:n], reads=[stb], writes=[d_sz])
            for j in range(n128 if lvl >= 6 else 0):
                tile = tb * 4 + j
                for half in range(2):
                    pt, pb = ps_next()
                    for kt in range(16):
                        fw.op(pe, lambda e: e.matmul(pt[:, :], lhsT=ak[:, kt, j * 128:(j + 1) * 128], rhs=wKV[:, kt, half * 512:(half + 1) * 512],
                                                     start=(kt == 0), stop=(kt == 15)), reads=[b_wKV, akb], writes=[pb])
                    f_, fb = kvf.next()
                    fw.op(act, lambda e: e.activation(out=f_[:], in_=pt[:, :], func=AF.Copy), reads=[pb], writes=[fb])
                    fw.dma(pool, (k_sl if half == 0 else v_sl)[tile * 128:(tile + 1) * 128, :], f_[:], reads=[fb],
                           writes=[d_out[1 + half]])
                    b_, bb = kvb.next()
                    fw.op(dve, lambda e: e.tensor_copy(out=b_[:], in_=pt[:, :]), reads=[pb], writes=[bb])
                    if half == 1:
                        fw.dma(pool, v_scr[tile * 128:(tile + 1) * 128, :], b_[:], reads=[bb], writes=[d_v])
                    elif lvl >= 8:
                        pt2, pb2 = ps_next()
                        ptb = pt2.bitcast(BF16)
                        for hh in range(4):
                            fw.op(pe, lambda e: e.transpose(ptb[:, hh * 128:(hh + 1) * 128], b_[:, hh * 128:(hh + 1) * 128], ident_b[:]),
                                  reads=[bb, b_idb], writes=[pb2])
                        kt_, ktb = kts.next()
                        fw.op(act, lambda e: e.activation(out=kt_[:], in_=ptb[:, 0:512].rearrange("p (a b) -> p a b", a=4), func=AF.Copy),
                              reads=[pb2], writes=[ktb])
                        fw.dma(pool, kT_scr.rearrange("h p t -> p h t")[:, :, tile * 128:(tile + 1) * 128], kt_[:],
                               reads=[ktb], writes=[d_kT])
        fw.barrier()
    if stages <= 5:
        fw.barrier()
        return nc

    def attn_p1(E, Eb_, Fx, Fb_, S, spp, z_chunk, mask_fn):
        nch = (S + 511) // 512
        fw.op(pool, lambda e: e.memset(Fx[:, 0:1], 0.0), writes=[Fb_])
        for c in range(nch):
            s0 = c * 512
            w = min(512, S - s0)
            pt, pb = z_chunk(c, s0, w)
            fw.op(act, lambda e: e.activation(out=E[:, s0:s0 + w], in_=pt[:, 0:w], func=AF.Exp, scale=SCALE),
                  reads=[pb], writes=[Eb_])
            if c == nch - 1:
                mask_fn(E, Eb_)
            spc, spb = spp.next()
            fw.op(act, lambda e: e.activation(out=spc[:, 0:w], in_=E[:, s0:s0 + w], func=AF.Ln, bias=1.0),
                  reads=[Eb_], writes=[spb])
            fw.op(dve, lambda e: e.tensor_tensor_scan(out=Fx[:, s0 + 1:s0 + w + 1], data0=ones_c[:, 0:1].to_broadcast([128, w]),
                                                      data1=spc[:, 0:w], initial=Fx[:, s0:s0 + 1], op0=ALU.mult, op1=ALU.add),
                  reads=[spb, b_ones, Fb_], writes=[Fb_])

    def attn_p2(E, Eb_, Fx, Fb_, P, Pcb, PT, PTcb, S, spp, negt):
        nch = (S + 511) // 512
        nt, ntb = negt.next()
        fw.op(dve, lambda e: e.tensor_scalar(out=nt[:], in0=Fx[:, S:S + 1], scalar1=-1.0, scalar2=None, op0=ALU.mult),
              reads=[Fb_], writes=[ntb])
        for c in range(nch):
            s0 = c * 512
            w = min(512, S - s0)
            wc, wcb = spp.next()
            fw.op(act, lambda e: e.activation(out=wc[:, 0:w], in_=Fx[:, s0:s0 + w], func=AF.Exp, bias=nt[:, 0:1]),
                  reads=[Fb_, ntb], writes=[wcb])
            fw.op(pool, lambda e: e.tensor_tensor(out=P[:, s0:s0 + w], in0=E[:, s0:s0 + w], in1=wc[:, 0:w], op=ALU.mult),
                  reads=[Eb_, wcb], writes=[Pcb[c]])
            pt, pb = ps_next()
            ptb = pt.bitcast(BF16)
            nb = (w + 127) // 128
            for b_ in range(nb):
                kb = c * 4 + b_
                wk = min(128, S - kb * 128)
                fw.op(pe, lambda e: e.transpose(ptb[0:wk, b_ * 128:(b_ + 1) * 128], P[:, kb * 128:kb * 128 + wk], ident_b[:]),
                      reads=[Pcb[c], b_idb], writes=[pb])
            full = w // 128
            if full > 0:
                srcv = ptb[:, 0:full * 128].rearrange("p (a b) -> p a b", a=full)
                fw.op(dve, lambda e: e.tensor_copy(out=PT[:, c * 4:c * 4 + full, :], in_=srcv), reads=[pb], writes=[PTcb[c]])
            if full < nb:
                wk = w - full * 128
                fw.op(dve, lambda e: e.tensor_copy(out=PT[0:wk, c * 4 + full, :], in_=ptb[0:wk, full * 128:(full + 1) * 128]),
                      reads=[pb], writes=[PTcb[c]])

    with ExitStack() as es:
        qTh = sb(es, "qTh", [128, NT], BF16); b_qTh = Buf()
        kTh = sb(es, "kTh", [128, NT], BF16); b_kTh = Buf()
        szh = sb(es, "szh", [128, NT], BF16); b_szh = Buf()
        Vh = sb(es, "Vh", [128, NTT, 128], BF16); b_Vh = Buf()
        y1h = sb(es, "y1h", [128, NPT], BF16); b_y1h = Buf()
        Ep = Pool(es, nc, "E", [128, 4096], F32, 2)
        Fp = Pool(es, nc, "Fx", [128, 4100], F32, 2)
        spp = Pool(es, nc, "spc", [128, 512], F32, 6)
        Pp = Pool(es, nc, "P", [128, 4096], BF16, 2)
        PTp = Pool(es, nc, "PT", [128, 32, 128], BF16, 2)
        negt = Pool(es, nc, "negt", [128, 1], F32, 4)
        Pcbs = [[Buf() for _ in range(9)] for _ in range(2)]
        PTcbs = [[Buf() for _ in range(9)] for _ in range(2)]
        n_heads = 4 if stages >= 6 else 1
        for hh in range(n_heads):
            fw.dma(sp, qTh[:], qT_scr[hh, :, :], reads=[d_qT], writes=[b_qTh])
            fw.dma(sp, kTh[:], kT_scr[hh, :, :], reads=[d_kT], writes=[b_kTh])
            fw.dma(sp, szh[:], sz_scr[hh, :, :], reads=[d_sz], writes=[b_szh])
            fw.dma(sp, Vh[:], v_scr.rearrange("(b p) (h d) -> p b h d", p=128, d=128)[:, :, hh, :], reads=[d_v], writes=[b_Vh])
            def finish(pend):
                qb_, S_, E_, Eb__, Fx_, Fb__ = pend
                P, _ = Pp.next()
                PT, _ = PTp.next()
                Pcb = Pcbs[(Pp.i - 1) % 2]
                PTcb = PTcbs[(PTp.i - 1) % 2]
                attn_p2(E_, Eb__, Fx_, Fb__, P, Pcb, PT, PTcb, S_, spp, negt)
                po, pob = ps_next()
                for kb in range(qb_ + 1):
                    fw.op(pe, lambda e: e.matmul(po[:, 0:128], lhsT=Vh[:, kb, :], rhs=PT[:, kb, :], start=(kb == 0), stop=(kb == qb_)),
                          reads=[b_Vh, PTcb[kb // 4]], writes=[pob])
                fw.op(dve, lambda e: e.tensor_tensor(out=y1h[:, qb_ * 128:(qb_ + 1) * 128], in0=po[:, 0:128],
                                                     in1=szh[:, qb_ * 128:(qb_ + 1) * 128], op=ALU.mult),
                      reads=[pob, b_szh], writes=[b_y1h])

            pend = None
            for qb in range(32):
                S = 128 * (qb + 1)
                E, Eb_ = Ep.next()
                Fx, Fb_ = Fp.next()

                def z_chunk(c, s0, w):
                    pt, pb = ps_next()
                    fw.op(pe, lambda e: e.matmul(pt[:, 0:w], lhsT=qTh[:, qb * 128:(qb + 1) * 128], rhs=kTh[:, s0:s0 + w],
                                                 start=True, stop=True), reads=[b_qTh, b_kTh], writes=[pb])
                    return pt, pb

                def mask_fn(E_, Eb__):
                    fw.op(dve, lambda e: e.tensor_tensor(out=E_[:, S - 128:S], in0=E_[:, S - 128:S], in1=tri[:], op=ALU.mult),
                          reads=[Eb__, b_tri], writes=[Eb__])
                attn_p1(E, Eb_, Fx, Fb_, S, spp, z_chunk, mask_fn)
                if pend is not None:
                    finish(pend)
                pend = (qb, S, E, Eb_, Fx, Fb_)
            finish(pend)
            for tb in range(8):
                fw.dma(pool, y1_src[tb][hh * 128:(hh + 1) * 128, :], y1h[:, tb * 512:(tb + 1) * 512], reads=[b_y1h], writes=[d_y1s[tb]])
        fw.barrier()
    if stages < 6:
        fw.barrier()
        return nc
    for tb in range(8):
        fw.collective("AllGather", y1_src[tb][:, :], y1_all[tb][:, :], reads=[d_y1s[tb]], writes=[d_y1a[tb]])

    if stages >= 7:
        ps_banks[0] = list(range(6))
        with ExitStack() as es:
            qs = sb(es, "qs", [128, 4, 256], BF16); b_qs = Buf()
            ks = sb(es, "ks", [128, 4, 256], BF16); b_ks = Buf()
            szs = sb(es, "szs", [128, 4, 256], BF16); b_szs = Buf()
            vnew = sb(es, "vnew", [16, 16, 512], BF16); b_vnew = Buf()
            y1sT = sb(es, "y1sT", [128, 4, 256], BF16); b_y1sT = Buf()
            QP = sb(es, "QP", [128, 8, 128], BF16); b_QP = Buf()
            Vbp = Pool(es, nc, "Vbc", [128, 4, 2, 512], BF16, 3)
            zer = sb(es, "zer", [128, 128], BF16); b_zer = Buf()
            E = sb(es, "Es", [128, 4112], F32); Eb_ = Buf()
            Fx = sb(es, "Fxs", [128, 4116], F32); Fb_ = Buf()
            P = sb(es, "Ps", [128, 4112], BF16); Pcb = [Buf() for _ in range(9)]
            PT = sb(es, "PTs", [128, 33, 128], BF16); PTcb = [Buf() for _ in range(9)]
            spp = Pool(es, nc, "spcs", [128, 512], F32, 3)
            negt = Pool(es, nc, "negts", [128, 1], F32, 2)
            Kcp = Pool(es, nc, "Kc", [128, 4, 512], F32, 3)
            Kbp = Pool(es, nc, "Kb", [128, 4, 512], BF16, 3)
            Vcp = Pool(es, nc, "Vc", [128, 4, 512], F32, 2)
            KTp = Pool(es, nc, "KTc", [128, 512], BF16, 4)
            fw.dma(sp, qs[:], qT_scr.rearrange("h p t -> p h t")[:, :, NPT:NT], reads=[d_qT], writes=[b_qs])
            fw.dma(sp, ks[:], kT_scr.rearrange("h p t -> p h t")[:, :, NPT:NT], reads=[d_kT], writes=[b_ks])
            fw.dma(sp, szs[:], sz_scr.rearrange("h p t -> p h t")[:, :, NPT:NT], reads=[d_sz], writes=[b_szs])
            fw.dma(sp, vnew[:], v_scr[NPT:NT, :].rearrange("(s j) f -> j s f", j=16), reads=[d_v], writes=[b_vnew])
            fw.op(pool, lambda e: e.memset(QP[:], 0.0), writes=[b_QP])
            fw.op(pool, lambda e: e.memset(zer[:], 0.0), writes=[b_zer])
            zi = [0]
            for pg in range(8):
                for i in range(8):
                    sl, hh = i // 4, i % 4
                    sq = 2 * pg + sl
                    fw.op(pool, lambda e: e.tensor_copy(out=QP[:, i, 16 * i:16 * i + 16], in_=qs[:, hh, sq * 16:(sq + 1) * 16]),
                          reads=[b_qs], writes=[b_QP])

                def z_chunk(c, s0, w):
                    zt, zb = psb[6 + zi[0] % 2]
                    zi[0] += 1
                    if c < 8:
                        kcs = []
                        for sl in range(2):
                            kc, kcb = Kcp.next()
                            fw.dma(sp, kc[:], cache_k[2 * pg + sl, c * 512:(c + 1) * 512, :].rearrange("(b p) f -> p b f", p=128),
                                   writes=[kcb])
                            kb16, kb16b = Kbp.next()
                            fw.op(act, lambda e: e.activation(out=kb16[:], in_=kc[:], func=AF.Copy), reads=[kcb], writes=[kb16b])
                            kcs.append((kb16, kb16b))
                        for i in range(8):
                            sl, hh = i // 4, i % 4
                            kc, kcb = kcs[sl]
                            pt, pb = ps_next()
                            ptb = pt.bitcast(BF16)
                            for b in range(4):
                                fw.op(pe, lambda e: e.transpose(ptb[:, b * 128:(b + 1) * 128], kc[:, b, hh * 128:(hh + 1) * 128], ident_b[:]),
                                      reads=[kcb, b_idb], writes=[pb])
                            kt_, ktb = KTp.next()
                            if i % 4 == 0:
                                fw.op(act, lambda e: e.activation(out=kt_[:], in_=ptb[:, 0:512], func=AF.Copy), reads=[pb], writes=[ktb])
                            else:
                                fw.op(dve, lambda e: e.tensor_copy(out=kt_[:], in_=ptb[:, 0:512]), reads=[pb], writes=[ktb])
                            fw.op(pe, lambda e: e.matmul(zt[:, 0:512], lhsT=QP[:, i, :], rhs=kt_[:], start=(i == 0), stop=(i == 7)),
                                  reads=[b_QP, ktb], writes=[zb])
                    else:
                        for i in range(8):
                            sl, hh = i // 4, i % 4
                            sq = 2 * pg + sl
                            fw.op(pe, lambda e: e.matmul(zt[:, 0:16], lhsT=QP[:, i, :], rhs=ks[:, hh, sq * 16:(sq + 1) * 16],
                                                         start=(i == 0), stop=(i == 7)), reads=[b_QP, b_ks], writes=[zb])
                    return zt, zb

                def mask_fn(E_, Eb__):
                    fw.op(dve, lambda e: e.tensor_tensor(out=E_[:, 4096:4112], in0=E_[:, 4096:4112], in1=m16[:], op=ALU.mult),
                          reads=[Eb__, b_m16], writes=[Eb__])
                attn_p1(E, Eb_, Fx, Fb_, 4112, spp, z_chunk, mask_fn)
                attn_p2(E, Eb_, Fx, Fb_, P, Pcb, PT, PTcb, 4112, spp, negt)
                po, pob = ps_next()
                fw.op(pe, lambda e: e.matmul(po[:, 0:128], lhsT=zer[:], rhs=zer[:], start=True, stop=False, skip_group_check=True),
                      reads=[b_zer], writes=[pob])
                for c in range(8):
                    vb, vbb = Vbp.next()
                    for sl in range(2):
                        vc, vcb = Vcp.next()
                        fw.dma(sp, vc[:], cache_v[2 * pg + sl, c * 512:(c + 1) * 512, :].rearrange("(b p) f -> p b f", p=128),
                               writes=[vcb])
                        fw.op(act, lambda e: e.activation(out=vb[:, :, sl, :], in_=vc[:], func=AF.Copy), reads=[vcb], writes=[vbb])
                    for i in range(8):
                        sl, hh = i // 4, i % 4
                        for b in range(4):
                            kb = c * 4 + b
                            fw.op(pe, lambda e: e.matmul(po[:, i * 16:(i + 1) * 16], lhsT=vb[:, b, sl, hh * 128:(hh + 1) * 128],
                                                         rhs=PT[:, kb, i * 16:(i + 1) * 16], start=False, stop=False,
                                                         skip_group_check=True),
                                  reads=[vbb, PTcb[c]], writes=[pob])
                for i in range(8):
                    sl, hh = i // 4, i % 4
                    sq = 2 * pg + sl
                    fw.op(pe, lambda e: e.matmul(po[:, i * 16:(i + 1) * 16], lhsT=vnew[0:16, sq, hh * 128:(hh + 1) * 128],
                                                 rhs=PT[0:16, 32, i * 16:(i + 1) * 16], start=False, stop=True,
                                                 skip_group_check=True),
                          reads=[b_vnew, PTcb[8]], writes=[pob])
                for sl in range(2):
                    sq = 2 * pg + sl
                    fw.op(dve, lambda e: e.tensor_tensor(out=y1sT[:, :, sq * 16:(sq + 1) * 16],
                                                         in0=po[:, sl * 64:(sl + 1) * 64].rearrange("p (h t) -> p h t", h=4),
                                                         in1=szs[:, :, sq * 16:(sq + 1) * 16], op=ALU.mult),
                          reads=[pob, b_szs], writes=[b_y1sT])
            fw.dma(pool, y1_src[8].rearrange("(h p) t -> p h t", p=128), y1sT[:], reads=[b_y1sT], writes=[d_y1s[8]])
            fw.barrier()
        ps_banks[0] = list(range(8))

    fw.collective("AllGather", y1_src[8][:, :], y1_all[8][:, :], reads=[d_y1s[8]], writes=[d_y1a[8]])
    with ExitStack() as es:
        wO2 = sb(es, "wO2", [128, 16, 512], BF16); b_wO2 = Buf()
        R2 = sb(es, "R2", [128, NTT, 512], F32); b_R2 = [Buf() for _ in range(NTT)]
        y1p = Pool(es, nc, "y1t", [128, 16, 512], BF16, 2)
        junk8 = sb(es, "junk8", [128, 512], BF16); b_j8 = Buf()
        x1p = Pool(es, nc, "x1t", [128, 512], F32, 3)
        for kq in range(4):
            fw.dma(pool, wO2[:, kq * 4:(kq + 1) * 4, :], w_out_b.rearrange("(kt p) f -> p kt f", p=128)[:, kq * 4:(kq + 1) * 4, :], writes=[b_wO2])
        for tb in range(9):
            n128 = 4 if tb < 8 else 2
            n = n128 * 128
            yt, ytb = y1p.next()
            fw.dma(sp, yt[:, :, 0:n], y1_all[tb].rearrange("(kt p) t -> p kt t", p=128), reads=[d_y1a[tb]], writes=[ytb])
            for j in range(n128):
                tile = tb * 4 + j
                pt, pb = ps_next()
                for kt in range(16):
                    fw.op(pe, lambda e: e.matmul(pt[:, :], lhsT=yt[:, kt, j * 128:(j + 1) * 128], rhs=wO2[:, kt, :],
                                                 start=(kt == 0), stop=(kt == 15)), reads=[ytb, b_wO2], writes=[pb])
                fw.op(act, lambda e: e.activation(out=R2[:, tile, :], in_=pt[:, :], func=AF.Copy), reads=[pb], writes=[b_R2[tile]])
                fw.op(act, lambda e: e.activation(out=junk8[:], in_=pt[:, :], func=AF.Square, accum_out=stat[:, 6, tile:tile + 1]),
                      reads=[pb], writes=[b_j8, b_stat[6]])
        exchange_stats(es, 6, 7, 2)
        for tile in range(NTT):
            rg = rg_of(tile)
            xs, xsb = x1p.next()
            fw.dma(sp, xs[:], x1_scr[tile * 128:(tile + 1) * 128, :], reads=[d_x1], writes=[xsb])
            fw.op(dve, lambda e: e.scalar_tensor_tensor(out=R2[:, tile, :], in0=R2[:, tile, :], scalar=stat[:, 7, tile:tile + 1],
                                                        in1=GG1bc[:, rg, :], op0=ALU.mult, op1=ALU.mult),
                  reads=[b_R2[tile], b_stat[7], b_GG1], writes=[b_R2[tile]])
            fw.op(pool, lambda e: e.tensor_tensor(out=R2[:, tile, :], in0=R2[:, tile, :], in1=xs[:], op=ALU.add),
                  reads=[b_R2[tile], xsb], writes=[b_R2[tile]])
            fw.dma(pool, y_sl[tile * 128:(tile + 1) * 128, :], R2[:, tile, :], reads=[b_R2[tile]], writes=[d_out[0]])
        fw.barrier()
    return nc


_NC_CACHE = {}


def _consts():
    ident = np.eye(128, dtype=np.float32)
    q = np.arange(128)
    tri = (q[None, :] < q[:, None]).astype(np.float32)
    m16 = ((np.arange(16)[None, :]) < (q[:, None] % 16)).astype(np.float32)
    sel = np.zeros((17, 3, 128), np.float32)
    sel[0, 0, :] = 1.0
    for p in range(128):
        sel[1 + p // 16, 1, p] = 1.0
        sel[9 + p // 16, 2, p] = 1.0
    return ident, tri, m16, sel


def make_in_maps(x_prompt, x_sample, c_prompt, c_sample, cache_k, cache_v, state_lru, state_conv,
                 g_pre, g_post, w_ada, b_ada, w_in_a, conv_w, conv_b, w_rgate, b_rgate, w_igate, b_igate,
                 lru_lambda, w_out_a, g_kv, w_kv, w_in_b, w_out_b):
    f = lambda a: np.ascontiguousarray(np.asarray(a, dtype=np.float32))
    ident, tri, m16, sel = _consts()
    in_maps = []
    for c in range(8):
        g, m = c // 4, c % 4
        fs = slice(512 * m, 512 * m + 512)
        cs = slice(1024 * m, 1024 * m + 1024)
        ss = slice(16 * g, 16 * g + 16)
        xg = np.concatenate([x_prompt[g], x_sample[ss].reshape(256, D)], axis=0)
        d = {
            "xg": f(xg),
            "xg_sl": f(xg[:, fs]),
            "cg": f(np.concatenate([c_prompt[g:g + 1], c_sample[ss]], axis=0)),
            "w_ada0": f(w_ada[0][:, 0:4096]),
            "b_ada0": f(b_ada[0][0:4096]),
            "w_adas": f(np.concatenate([w_ada[0][:, 4096 + 512 * m:4096 + 512 * m + 512],
                                        w_ada[1][:, 512 * m:512 * m + 512],
                                        w_ada[1][:, 2048 + 512 * m:2048 + 512 * m + 512],
                                        w_ada[1][:, 4096 + 512 * m:4096 + 512 * m + 512]], axis=1)),
            "b_adas": f(np.concatenate([b_ada[0][4096 + 512 * m:4096 + 512 * m + 512],
                                        b_ada[1][512 * m:512 * m + 512],
                                        b_ada[1][2048 + 512 * m:2048 + 512 * m + 512],
                                        b_ada[1][4096 + 512 * m:4096 + 512 * m + 512]])),
            "g_pre0": f(g_pre[0]),
            "g_sl": f(np.stack([g_post[0][fs], g_pre[1][fs], g_post[1][fs], g_kv[fs]])),
            "w_in_a": f(np.concatenate([w_in_a[0][:, cs], w_in_a[0][:, 4096 + 1024 * m:4096 + 1024 * m + 1024]], axis=1)),
            "conv_w": f(conv_w[0][:, cs]),
            "chp": f(np.stack([conv_b[0][cs], b_rgate[0][cs], b_igate[0][cs], lru_lambda[0][cs]])),
            "w_rg": f(w_rgate[0][4 * m:4 * m + 4]),
            "w_ig": f(w_igate[0][4 * m:4 * m + 4]),
            "st_in": f(np.concatenate([state_lru[0][ss][:, cs], state_conv[0][ss][:, :, cs].reshape(48, 1024)], axis=0)),
            "w_out_a": f(w_out_a[0][:, fs]),
            "w_kv": f(np.concatenate([w_kv[:, fs], w_kv[:, 2048 + 512 * m:2048 + 512 * m + 512]], axis=1)),
            "w_in_b": f(np.concatenate([w_in_b[0][:, fs], w_in_b[0][:, 2048 + 512 * m:2048 + 512 * m + 512]], axis=1)),
            "w_out_b": f(w_out_b[0][:, fs]),
            "cache_k": f(cache_k[ss][:, :, 4 * m:4 * m + 4, :].reshape(16, 4096, 512)),
            "cache_v": f(cache_v[ss][:, :, 4 * m:4 * m + 4, :].reshape(16, 4096, 512)),
            "ident": ident, "tri": tri, "m16": m16, "sel": sel,
        }
        in_maps.append(d)
    return in_maps


def assemble(results):
    y_prompt = np.zeros((2, 4096, D), np.float32)
    y_sample = np.zeros((32, 16, D), np.float32)
    k_prompt = np.zeros((2, 4096, 16, 128), np.float32)
    v_prompt = np.zeros((2, 4096, 16, 128), np.float32)
    k_sample = np.zeros((32, 16, 16, 128), np.float32)
    v_sample = np.zeros((32, 16, 16, 128), np.float32)
    lru_prompt = np.zeros((1, 2, 4096), np.float32)
    lru_sample = np.zeros((1, 32, 4096), np.float32)
    conv_prompt = np.zeros((1, 2, 3, 4096), np.float32)
    conv_sample = np.zeros((1, 32, 3, 4096), np.float32)
    for c in range(8):
        g, m = c // 4, c % 4
        r = results[c]
        fs = slice(512 * m, 512 * m + 512)
        cs = slice(1024 * m, 1024 * m + 1024)
        ss = slice(16 * g, 16 * g + 16)
        y_prompt[g][:, fs] = r["y_sl"][:4096]
        y_sample[ss].reshape(256, D)[:, fs] = r["y_sl"][4096:]
        k_prompt[g].reshape(4096, D)[:, fs] = r["k_sl"][:4096]
        v_prompt[g].reshape(4096, D)[:, fs] = r["v_sl"][:4096]
        k_sample[ss].reshape(256, D)[:, fs] = r["k_sl"][4096:]
        v_sample[ss].reshape(256, D)[:, fs] = r["v_sl"][4096:]
        lru_prompt[0, g, cs] = r["lru_sl"][0]
        lru_sample[0, ss, cs] = r["lru_sl"][1:17]
        cv = r["conv_sl"].reshape(17, 3, 1024)
        conv_prompt[0, g, :, cs] = cv[0]
        conv_sample[0, ss, :, cs] = cv[1:17]
    return (y_prompt, y_sample, k_prompt, v_prompt, k_sample, v_sample,
            lru_prompt, lru_sample, conv_prompt, conv_sample)


def kernel(**inputs):
    inputs = {k: np.asarray(v) for k, v in inputs.items()}
    in_maps = make_in_maps(**inputs)
    if "nc" not in _NC_CACHE:
        _NC_CACHE["nc"] = build_nc()
    names = _NC_CACHE["in_names"]
    in_maps = [{k: d[k] for k in names} for d in in_maps]
    res = run_bass_kernel_spmd(_NC_CACHE["nc"], in_maps, core_ids=list(range(8)))
    return assemble(res.results)
```

```python
import numpy as np
import concourse.bass as bass
import concourse.mybir as mybir
from concourse.bass_utils import run_bass_kernel_spmd

F32 = mybir.dt.float32
BF16 = mybir.dt.bfloat16
AF = mybir.ActivationFunctionType
ALU = mybir.AluOpType

D = 2048
NPT = 4096
NST = 256
NT = NPT + NST
NTT = NT // 128
EPS = 1e-6
GROUPS = [[0, 1, 2, 3], [4, 5, 6, 7]]
SCALE = 128 ** -0.5


class Buf:
    __slots__ = ("name", "w", "r", "excl")

    def __init__(self, name="", excl=False):
        self.name = name
        self.w = None
        self.r = {}
        self.excl = excl


class _Eng:
    def __init__(self, fw, name, eng, same_sync):
        self.fw = fw
        self.name = name
        self.eng = eng
        self.epoch = 0
        self.count = 0
        self.sem = fw.nc.alloc_semaphore(name=f"es_{name}_0")
        self.key = (name, 0)
        fw.sems[self.key] = self.sem
        self.seen = {}
        self.same_sync = same_sync
        self.dslots = []
        self.dnext = 0

    def tick(self, inst):
        if self.count >= 30000:
            self.epoch += 1
            self.count = 0
            self.sem = self.fw.nc.alloc_semaphore(name=f"es_{self.name}_{self.epoch}")
            self.key = (self.name, self.epoch)
            self.fw.sems[self.key] = self.sem
        self.count += 1
        inst.then_inc(self.sem, 1)
        return (self.key, self.count)


class FW:
    def __init__(self, nc, same_sync=True):
        self.nc = nc
        self.sems = {}
        self.pe = _Eng(self, "pe", nc.tensor, False)
        self.act = _Eng(self, "act", nc.scalar, same_sync)
        self.dve = _Eng(self, "dve", nc.vector, same_sync)
        self.pool = _Eng(self, "pool", nc.gpsimd, same_sync)
        self.sp = _Eng(self, "sp", nc.sync, False)
        self.engs = [self.pe, self.act, self.dve, self.pool, self.sp]
        for q, n in ((self.sp, 16), (self.pool, 12)):
            for i in range(n):
                key = ("d" + q.name, i)
                self.sems[key] = nc.alloc_semaphore(name=f"ds_{q.name}_{i}")
                q.dslots.append([key, 0, None])
        self.n_cc = 0
        self.cc_tickets = []

    def _wait(self, E, ticket):
        key, val = ticket
        if key[0] == E.name and not E.same_sync:
            return
        if E.seen.get(key, 0) >= val:
            return
        E.eng.wait_ge(self.sems[key], val)
        E.seen[key] = val

    def _deps(self, E, reads, writes):
        deps = {}

        def add(t):
            if t is None:
                return
            k, v = t
            if deps.get(k, 0) < v:
                deps[k] = v
        for b in reads:
            add(b.w)
            if b.excl:
                for k, v in b.r.items():
                    if k[0] != E.name:
                        add((k, v))
        for b in writes:
            add(b.w)
            for k, v in b.r.items():
                add((k, v))
        for k, v in deps.items():
            self._wait(E, (k, v))

    def _commit(self, ticket, reads, writes):
        k, v = ticket
        for b in reads:
            if b.r.get(k, 0) < v:
                b.r[k] = v
        for b in writes:
            b.w = ticket
            b.r = {}

    def op(self, E, emit, reads=(), writes=()):
        self._deps(E, reads, writes)
        inst = emit(E.eng)
        t = E.tick(inst)
        self._commit(t, reads, writes)
        return t

    def dma(self, Q, out, in_, reads=(), writes=(), **kw):
        slot = Q.dslots[Q.dnext % len(Q.dslots)]
        Q.dnext += 1
        if slot[2] is not None:
            self._wait(Q, slot[2])
        self._deps(Q, reads, writes)
        inst = Q.eng.dma_start(out=out, in_=in_, **kw)
        slot[1] += 1
        t = (slot[0], 16 * slot[1])
        inst.then_inc(self.sems[slot[0]], 16)
        slot[2] = t
        self._commit(t, reads, writes)
        return t

    def collective(self, kind, in_ap, out_ap, reads=(), writes=()):
        Q = self.pool
        self._deps(Q, reads, writes)
        key = ("cc", self.n_cc)
        self.n_cc += 1
        self.sems[key] = self.nc.alloc_semaphore(name=f"cc_{key[1]}")
        inst = Q.eng.collective_compute(kind, ALU.bypass, replica_groups=GROUPS,
                                        ins=[in_ap], outs=[out_ap])
        inst.then_inc(self.sems[key], 1)
        t = (key, 1)
        self.cc_tickets.append(t)
        self._commit(t, reads, writes)
        return t

    def barrier(self):
        tickets = []
        for E in self.engs:
            if E.count > 0:
                tickets.append((E.key, E.count))
            for s in E.dslots:
                if s[2] is not None:
                    tickets.append(s[2])
        tickets += self.cc_tickets
        for E in self.engs:
            for t in tickets:
                if t[0][0] == E.name:
                    continue
                self._wait(E, t)


class Pool:
    def __init__(self, es, nc, name, shape, dtype, n):
        self.tiles = []
        for i in range(n):
            t = es.enter_context(nc.sbuf_tensor(f"p_{name}_{i}", list(shape), dtype))
            self.tiles.append((t, Buf(f"{name}_{i}")))
        self.i = 0

    def next(self):
        t = self.tiles[self.i % len(self.tiles)]
        self.i += 1
        return t


from contextlib import ExitStack


def build_nc(stages=9):
    nc = bass.Bass("TRN2", target_bir_lowering=False)

    in_names = []
    _NC_CACHE["in_names"] = in_names

    def din(name, shape, dt=F32):
        in_names.append(name)
        return nc.dram_tensor(name, list(shape), dt, kind="ExternalInput").ap()

    def dout(name, shape, dt=F32):
        return nc.dram_tensor(name, list(shape), dt, kind="ExternalOutput").ap()

    def dscr(name, shape, dt, local=False):
        if local:
            return nc.dram_tensor(name, list(shape), dt, kind="Internal", addr_space="Local").ap()
        return nc.dram_tensor(name, list(shape), dt, kind="Internal").ap()

    xg = din("xg", [NT, D])
    xg_sl = din("xg_sl", [NT, 512])
    cg = din("cg", [17, D])
    w_ada0 = din("w_ada0", [D, 4096])
    b_ada0 = din("b_ada0", [4096])
    w_adas = din("w_adas", [D, 2048])
    b_adas = din("b_adas", [2048])
    g_pre0 = din("g_pre0", [D])
    g_sl = din("g_sl", [4, 512])
    w_in_a = din("w_in_a", [D, 2048])
    conv_w = din("conv_w", [4, 1024])
    chp = din("chp", [4, 1024])
    w_rg = din("w_rg", [4, 256, 256])
    w_ig = din("w_ig", [4, 256, 256])
    st_in = din("st_in", [64, 1024])
    w_out_a = din("w_out_a", [4096, 512])
    w_kv = din("w_kv", [D, 1024])
    w_in_b = din("w_in_b", [D, 1024])
    w_out_b = din("w_out_b", [D, 512])
    if stages >= 7:
        cache_k = din("cache_k", [16, 4096, 512])
        cache_v = din("cache_v", [16, 4096, 512])
    ident_in = din("ident", [128, 128])
    tri_in = din("tri", [128, 128])
    m16_in = din("m16", [128, 16])
    sel_in = din("sel", [17, 3, 128])

    y_sl = dout("y_sl", [NT, 512])
    k_sl = dout("k_sl", [NT, 512])
    v_sl = dout("v_sl", [NT, 512])
    lru_sl = dout("lru_sl", [17, 1024])
    conv_sl = dout("conv_sl", [51, 1024])

    hn0_scr = dscr("hn0_scr", [D, NT], BF16)
    TBN = [512] * 8 + [256]
    y0_src = [dscr(f"y0_src{tb}", [1024, TBN[tb]], BF16) for tb in range(9)]
    y0_all = [dscr(f"y0_all{tb}", [4096, TBN[tb]], BF16, local=True) for tb in range(9)]
    st_src = [dscr(f"st_src{i}", [128, NTT], F32) for i in range(3)]
    st_all = [dscr(f"st_all{i}", [512, NTT], F32, local=True) for i in range(3)]
    x1_scr = dscr("x1_scr", [NT, 512], F32)
    hn_src = [dscr(f"hn_src{tb}", [1024, TBN[tb]], BF16) for tb in range(9)]
    hn_all = [dscr(f"hn_all{tb}", [4096, TBN[tb]], BF16, local=True) for tb in range(9)]
    qT_scr = dscr("qT_scr", [4, 128, NT], BF16)
    kT_scr = dscr("kT_scr", [4, 128, NT], BF16)
    sz_scr = dscr("sz_scr", [4, 128, NT], BF16)
    v_scr = dscr("v_scr", [NT, 512], BF16)
    y1_src = [dscr(f"y1_src{tb}", [512, TBN[tb]], BF16) for tb in range(9)]
    y1_all = [dscr(f"y1_all{tb}", [2048, TBN[tb]], BF16, local=True) for tb in range(9)]

    fw = FW(nc)
    pe, act, dve, pool, sp = fw.pe, fw.act, fw.dve, fw.pool, fw.sp
    d_hn0 = Buf()
    d_y0s = [Buf() for _ in range(9)]
    d_y0a = [Buf() for _ in range(9)]
    d_sts = [Buf() for _ in range(3)]
    d_sta = [Buf() for _ in range(3)]
    d_x1 = Buf()
    d_hns = [Buf() for _ in range(9)]
    d_hna = [Buf() for _ in range(9)]
    d_y1s = [Buf() for _ in range(9)]
    d_y1a = [Buf() for _ in range(9)]
    d_qT, d_kT, d_sz, d_v = Buf(), Buf(), Buf(), Buf()
    d_out = [Buf() for _ in range(5)]

    glob = ExitStack()

    def sb(es, name, shape, dt):
        return es.enter_context(nc.sbuf_tensor("s_" + name, list(shape), dt))

    psb = [(nc.alloc_psum_tensor(f"ps{i}", [128, 512], F32), Buf(f"ps{i}", excl=True)) for i in range(8)]
    ps_i = [0]

    ps_banks = [list(range(8))]

    def ps_next():
        bl = ps_banks[0]
        t = psb[bl[ps_i[0] % len(bl)]]
        ps_i[0] += 1
        return t

    ident_f = sb(glob, "ident_f", [128, 128], F32); b_idf = Buf()
    ident_b = sb(glob, "ident_b", [128, 128], BF16); b_idb = Buf()
    tri = sb(glob, "tri", [128, 128], F32); b_tri = Buf()
    m16 = sb(glob, "m16", [128, 16], F32); b_m16 = Buf()
    sel = sb(glob, "sel", [17, 3, 128], F32); b_sel = Buf()
    ones_c = sb(glob, "ones_c", [128, 1], F32); b_ones = Buf()
    GG0bc = sb(glob, "GG0bc", [128, 3, 512], F32); b_GG0 = Buf()
    A1bc = sb(glob, "A1bc", [128, 3, 512], F32); b_A1 = Buf()
    B1bc = sb(glob, "B1bc", [128, 3, 512], F32); b_B1 = Buf()
    GG1bc = sb(glob, "GG1bc", [128, 3, 512], F32); b_GG1 = Buf()
    GKVbc = sb(glob, "GKVbc", [128, 512], F32); b_GKV = Buf()
    stat = sb(glob, "stat", [128, 8, NTT], F32)
    b_stat = [Buf() for _ in range(8)]

    fw.dma(sp, ident_f[:], ident_in[:, :], writes=[b_idf])
    fw.dma(sp, tri[:], tri_in[:, :], writes=[b_tri])
    fw.dma(sp, m16[:], m16_in[:, :], writes=[b_m16])
    fw.dma(sp, sel[:], sel_in[:, :, :], writes=[b_sel])
    fw.dma(sp, GKVbc[:], g_sl[3, :].partition_broadcast(128), writes=[b_GKV])
    fw.op(act, lambda e: e.activation(out=ident_b[:], in_=ident_f[:], func=AF.Copy), reads=[b_idf], writes=[b_idb])
    fw.op(pool, lambda e: e.memset(ones_c[:], 1.0), writes=[b_ones])

    def rg_of(tile):
        return 0 if tile < 32 else tile - 31

    def rstd_from(ss_i, r_i, ncols=NTT):
        fw.op(dve, lambda e: e.tensor_scalar(out=stat[:, r_i, 0:ncols], in0=stat[:, ss_i, 0:ncols], scalar1=1.0 / D,
                                             scalar2=EPS, op0=ALU.mult, op1=ALU.add),
              reads=[b_stat[ss_i]], writes=[b_stat[r_i]])
        fw.op(act, lambda e: e.activation(out=stat[:, r_i, 0:ncols], in_=stat[:, r_i, 0:ncols], func=AF.Sqrt),
              reads=[b_stat[r_i]], writes=[b_stat[r_i]])
        fw.op(dve, lambda e: e.reciprocal(out=stat[:, r_i, 0:ncols], in_=stat[:, r_i, 0:ncols]),
              reads=[b_stat[r_i]], writes=[b_stat[r_i]])

    with ExitStack() as es:
        A0bc = sb(es, "A0bc", [128, 3, D], F32); b_A0 = Buf()
        B0bc = sb(es, "B0bc", [128, 3, D], F32); b_B0 = Buf()
        with ExitStack() as es0:
            cgt = sb(es0, "cgt", [17, D], F32); b_cgt = Buf()
            scT = sb(es0, "scT", [128, 16, 17], BF16); b_scT = Buf()
            modrow = sb(es0, "modrow", [17, 6144], F32); b_mod = Buf()
            bbp = Pool(es0, nc, "bbc", [17, 512], F32, 2)
            g0bc = sb(es0, "g0bc", [17, D], F32); b_g0 = Buf()
            gslbc = sb(es0, "gslbc", [17, 3, 512], F32); b_gsl = Buf()
            rows = sb(es0, "rows", [17, 4, 512], F32); b_rows = Buf()
            A0row = sb(es0, "A0row", [17, D], F32); b_A0r = Buf()
            wap = Pool(es0, nc, "wap", [128, 16, 512], BF16, 2)

            fw.dma(sp, cgt[:], cg[:, :], writes=[b_cgt])
            fw.dma(sp, g0bc[:], g_pre0.partition_broadcast(17), writes=[b_g0])
            for i in range(3):
                fw.dma(sp, gslbc[:, i, :], g_sl[i, :].partition_broadcast(17), writes=[b_gsl])
            fw.op(act, lambda e: e.activation(out=cgt[:], in_=cgt[:], func=AF.Silu), reads=[b_cgt], writes=[b_cgt])
            pt, pb = ps_next()
            for kt in range(16):
                fw.op(pe, lambda e: e.transpose(pt[:, kt * 17:(kt + 1) * 17], cgt[0:17, kt * 128:(kt + 1) * 128],
                                                ident_f[0:17, 0:17]),
                      reads=[b_cgt, b_idf], writes=[pb])
            fw.op(act, lambda e: e.activation(out=scT[:].rearrange("p a b -> p (a b)"), in_=pt[:, 0:272], func=AF.Copy),
                  reads=[pb], writes=[b_scT])
            for cc in range(12):
                wt, wb = wap.next()
                src = (w_ada0 if cc < 8 else w_adas).rearrange("(kt p) c -> p kt c", p=128)
                c0 = (cc % 8) * 512 if cc < 8 else (cc - 8) * 512
                fw.dma(pool, wt[:], src[:, :, c0:c0 + 512], writes=[wb])
                bbc, b_bbc = bbp.next()
                fw.dma(sp, bbc[:], (b_ada0 if cc < 8 else b_adas)[c0:c0 + 512].partition_broadcast(17), writes=[b_bbc])
                pt, pb = ps_next()
                for kt in range(16):
                    fw.op(pe, lambda e: e.matmul(pt[0:17, :], lhsT=scT[:, kt, :], rhs=wt[:, kt, :],
                                                 start=(kt == 0), stop=(kt == 15)),
                          reads=[b_scT, wb], writes=[pb])
                fw.op(dve, lambda e: e.tensor_tensor(out=modrow[:, cc * 512:(cc + 1) * 512], in0=pt[0:17, :],
                                                     in1=bbc[:, :], op=ALU.add),
                      reads=[pb, b_bbc], writes=[b_mod])
            fw.op(dve, lambda e: e.scalar_tensor_tensor(out=A0row[:], in0=modrow[:, 2048:4096], scalar=1.0, in1=g0bc[:],
                                                        op0=ALU.add, op1=ALU.mult),
                  reads=[b_mod, b_g0], writes=[b_A0r])
            fw.op(dve, lambda e: e.tensor_tensor(out=rows[:, 0, :], in0=modrow[:, 4096:4608], in1=gslbc[:, 0, :], op=ALU.mult),
                  reads=[b_mod, b_gsl], writes=[b_rows])
            fw.op(dve, lambda e: e.scalar_tensor_tensor(out=rows[:, 1, :], in0=modrow[:, 5120:5632], scalar=1.0,
                                                        in1=gslbc[:, 1, :], op0=ALU.add, op1=ALU.mult),
                  reads=[b_mod, b_gsl], writes=[b_rows])
            fw.op(dve, lambda e: e.tensor_tensor(out=rows[:, 2, :], in0=modrow[:, 5632:6144], in1=gslbc[:, 2, :], op=ALU.mult),
                  reads=[b_mod, b_gsl], writes=[b_rows])
            jobs = []
            for rg in range(3):
                for c4 in range(4):
                    jobs.append((A0row[:, c4 * 512:(c4 + 1) * 512], b_A0r, A0bc[:, rg, c4 * 512:(c4 + 1) * 512], b_A0, rg))
                    jobs.append((modrow[:, c4 * 512:(c4 + 1) * 512], b_mod, B0bc[:, rg, c4 * 512:(c4 + 1) * 512], b_B0, rg))
                jobs.append((rows[:, 0, :], b_rows, GG0bc[:, rg, :], b_GG0, rg))
                jobs.append((rows[:, 1, :], b_rows, A1bc[:, rg, :], b_A1, rg))
                jobs.append((modrow[:, 4608:5120], b_mod, B1bc[:, rg, :], b_B1, rg))
                jobs.append((rows[:, 2, :], b_rows, GG1bc[:, rg, :], b_GG1, rg))
            for ji, (src_ap, src_b, dst_ap, dst_b, rg) in enumerate(jobs):
                pt, pb = ps_next()
                fw.op(pe, lambda e: e.matmul(pt[:, :], lhsT=sel[:, rg, :], rhs=src_ap, start=True, stop=True),
                      reads=[b_sel, src_b], writes=[pb])
                E = act if ji % 2 == 0 else dve
                if E is act:
                    fw.op(act, lambda e: e.activation(out=dst_ap, in_=pt[:, :], func=AF.Copy), reads=[pb], writes=[dst_b])
                else:
                    fw.op(dve, lambda e: e.tensor_copy(out=dst_ap, in_=pt[:, :]), reads=[pb], writes=[dst_b])
            fw.barrier()

        with ExitStack() as es1:
            xtp = Pool(es1, nc, "xt", [128, D], F32, 3)
            junk = sb(es1, "junk", [128, D], BF16); b_junk = Buf()
            hnp = Pool(es1, nc, "hn", [128, D], BF16, 2)
            stg = Pool(es1, nc, "hstg", [128, 16, 512], BF16, 2)
            tmpp = Pool(es1, nc, "tmp1", [128, 4], F32, 4)
            for tb in range(9):
                n128 = 4 if tb < 8 else 2
                st_t, st_b = stg.next()
                for j in range(n128):
                    tile = tb * 4 + j
                    rg = rg_of(tile)
                    xt, xb_ = xtp.next()
                    tmp1, b_tmp1 = tmpp.next()
                    fw.dma(sp, xt[:], xg[tile * 128:(tile + 1) * 128, :], writes=[xb_])
                    fw.op(act, lambda e: e.activation(out=junk[:], in_=xt[:], func=AF.Square, accum_out=tmp1[:, 0:1]),
                          reads=[xb_], writes=[b_junk, b_tmp1])
                    fw.op(dve, lambda e: e.tensor_scalar(out=tmp1[:, 1:2], in0=tmp1[:, 0:1], scalar1=1.0 / D, scalar2=EPS,
                                                         op0=ALU.mult, op1=ALU.add), reads=[b_tmp1], writes=[b_tmp1])
                    fw.op(act, lambda e: e.activation(out=tmp1[:, 2:3], in_=tmp1[:, 1:2], func=AF.Sqrt),
                          reads=[b_tmp1], writes=[b_tmp1])
                    fw.op(dve, lambda e: e.reciprocal(out=tmp1[:, 3:4], in_=tmp1[:, 2:3]), reads=[b_tmp1], writes=[b_tmp1])
                    fw.op(dve, lambda e: e.scalar_tensor_tensor(out=xt[:], in0=xt[:], scalar=tmp1[:, 3:4], in1=A0bc[:, rg, :],
                                                                op0=ALU.mult, op1=ALU.mult),
                          reads=[xb_, b_tmp1, b_A0], writes=[xb_])
                    hn, hb = hnp.next()
                    fw.op(pool, lambda e: e.tensor_tensor(out=hn[:], in0=xt[:], in1=B0bc[:, rg, :], op=ALU.add),
                          reads=[xb_, b_B0], writes=[hb])
                    for half in range(2):
                        pt, pb = ps_next()
                        ptb = pt.bitcast(BF16)
                        for k8 in range(8):
                            kt = half * 8 + k8
                            fw.op(pe, lambda e: e.transpose(ptb[:, k8 * 128:(k8 + 1) * 128], hn[:, kt * 128:(kt + 1) * 128],
                                                            ident_b[:]),
                                  reads=[hb, b_idb], writes=[pb])
                        dst = st_t[:, half * 8:(half + 1) * 8, j * 128:(j + 1) * 128]
                        srcp = ptb[:, :].rearrange("p (a b) -> p a b", a=8)
                        if half == 0:
                            fw.op(act, lambda e: e.activation(out=dst, in_=srcp, func=AF.Copy), reads=[pb], writes=[st_b])
                        else:
                            fw.op(dve, lambda e: e.tensor_copy(out=dst, in_=srcp), reads=[pb], writes=[st_b])
                n = n128 * 128
                fw.dma(pool, hn0_scr.rearrange("(kt p) t -> p kt t", p=128)[:, :, tb * 512:tb * 512 + n], st_t[:, :, 0:n],
                       reads=[st_b], writes=[d_hn0])
            fw.barrier()
    if stages <= 1:
        fw.barrier()
        return nc

    with ExitStack() as es:
        wA = sb(es, "wA", [128, 16, 2048], BF16); b_wA = Buf()
        wR = sb(es, "wR", [128, 8, 256], BF16); b_wR = Buf()
        wI = sb(es, "wI", [128, 8, 256], BF16); b_wI = Buf()
        cw = sb(es, "cw", [128, 4, 8], F32); b_cw = Buf()
        cp = sb(es, "cp", [128, 4, 8], F32); b_cp = Buf()
        negc = sb(es, "negc", [128, 2, 8], F32); b_negc = Buf()
        strow = sb(es, "strow", [64, 1024], F32); b_strow = Buf()
        stT = sb(es, "stT", [128, 8, 64], F32); b_stT = Buf()
        xe = sb(es, "xe", [128, 8, 515], F32); b_xe = [Buf() for _ in range(8)]
        b_xes = b_xe

        def xesv(ct_):
            return xe[:, ct_, 0:304].rearrange("p (s j) -> p s j", j=19)
        hc = sb(es, "hc", [128, 8], F32); b_hc = [Buf() for _ in range(8)]
        hfin = sb(es, "hfin", [128, 8, 17], F32); b_hfin = Buf()
        xlast = sb(es, "xlast", [128, 8, 17, 3], F32); b_xlast = Buf()
        hnT = Pool(es, nc, "hnT", [128, 16, 512], BF16, 2)
        xcp = Pool(es, nc, "xc", [128, 512], F32, 4)
        xcbp = Pool(es, nc, "xcb", [128, 512], BF16, 4)
        szp = Pool(es, nc, "sz", [128, 512], F32, 4)
        rp = Pool(es, nc, "r", [128, 512], F32, 2)
        ip = Pool(es, nc, "ig", [128, 512], F32, 2)
        ap_ = Pool(es, nc, "a", [128, 512], F32, 2)
        a2p = Pool(es, nc, "a2", [128, 512], F32, 2)
        up = Pool(es, nc, "u", [128, 512], F32, 2)
        hp = Pool(es, nc, "h", [128, 512], F32, 2)
        yp = Pool(es, nc, "y", [128, 512], BF16, 3)

        for kq in range(4):
            fw.dma(pool, wA[:, kq * 4:(kq + 1) * 4, :],
                   w_in_a.rearrange("(kt p) c -> p kt c", p=128)[:, kq * 4:(kq + 1) * 4, :], writes=[b_wA])
        fw.dma(pool, wR[:], w_rg.rearrange("b (kh p) j -> p (b kh) j", p=128), writes=[b_wR])
        fw.dma(pool, wI[:], w_ig.rearrange("b (kh p) j -> p (b kh) j", p=128), writes=[b_wI])
        with nc.allow_non_contiguous_dma(reason="tiny per-channel params"):
            fw.dma(sp, cw[:], conv_w.rearrange("i (ct p) -> p i ct", p=128), writes=[b_cw])
            fw.dma(sp, cp[:], chp.rearrange("i (ct p) -> p i ct", p=128), writes=[b_cp])
        fw.dma(sp, strow[:], st_in[:, :], writes=[b_strow])
        fw.op(act, lambda e: e.activation(out=negc[:, 0, :], in_=cp[:, 3, :], func=AF.Exp, scale=-1.0), reads=[b_cp], writes=[b_negc])
        fw.op(act, lambda e: e.activation(out=negc[:, 0, :], in_=negc[:, 0, :], func=AF.Ln, bias=1.0), reads=[b_negc], writes=[b_negc])
        fw.op(dve, lambda e: e.tensor_scalar(out=negc[:, 1, :], in0=negc[:, 0, :], scalar1=-16.0, scalar2=None, op0=ALU.mult),
              reads=[b_negc], writes=[b_negc])
        fw.op(dve, lambda e: e.tensor_scalar(out=negc[:, 0, :], in0=negc[:, 0, :], scalar1=-8.0, scalar2=None, op0=ALU.mult),
              reads=[b_negc], writes=[b_negc])
        pt, pb = ps_next()
        for ct in range(8):
            fw.op(pe, lambda e: e.transpose(pt[:, ct * 64:(ct + 1) * 64], strow[0:64, ct * 128:(ct + 1) * 128], ident_f[0:64, 0:64]),
                  reads=[b_strow, b_idf], writes=[pb])
        fw.op(act, lambda e: e.activation(out=stT[:].rearrange("p a b -> p (a b)"), in_=pt[:, :], func=AF.Copy), reads=[pb], writes=[b_stT])
        for ct in range(8):
            fw.op(pool, lambda e: e.memset(xe[:, ct, 0:3], 0.0), writes=[b_xe[ct]])
            fw.op(pool, lambda e: e.memset(hc[:, ct:ct + 1], 0.0), writes=[b_hc[ct]])

        for tb in range(9):
            samp = tb == 8
            n = 256 if samp else 512
            ht, hb_ = hnT.next()
            fw.dma(sp, ht[:, :, 0:n], hn0_scr.rearrange("(kt p) t -> p kt t", p=128)[:, :, tb * 512:tb * 512 + n],
                   reads=[d_hn0], writes=[hb_])
            if samp:
                for ct in range(8):
                    fw.op(pool, lambda e: e.tensor_copy(out=xesv(ct)[:, :, 0:3],
                                                        in_=stT[:, ct, 16:64].rearrange("p (s i) -> p s i", i=3)),
                          reads=[b_stT], writes=[b_xes[ct]])
            for blk in range(4):
                xcs, xcbs, szs = [], [], []
                for c2 in range(2):
                    ct = blk * 2 + c2
                    pt, pb = ps_next()
                    for kt in range(16):
                        fw.op(pe, lambda e: e.matmul(pt[:, 0:n], lhsT=wA[:, kt, ct * 128:(ct + 1) * 128], rhs=ht[:, kt, 0:n],
                                                     start=(kt == 0), stop=(kt == 15)), reads=[b_wA, hb_], writes=[pb])
                    if samp:
                        xv = xesv(ct)[:, :, 3:19]
                        bx = b_xes[ct]
                        fw.op(act, lambda e: e.activation(out=xv, in_=pt[:, 0:n].rearrange("p (s t) -> p s t", t=16), func=AF.Copy),
                              reads=[pb], writes=[bx])
                    else:
                        bx = b_xe[ct]
                        fw.op(act, lambda e: e.activation(out=xe[:, ct, 3:515], in_=pt[:, 0:n], func=AF.Copy), reads=[pb], writes=[bx])
                    pt2, pb2 = ps_next()
                    for kt in range(16):
                        fw.op(pe, lambda e: e.matmul(pt2[:, 0:n], lhsT=wA[:, kt, 1024 + ct * 128:1024 + (ct + 1) * 128],
                                                     rhs=ht[:, kt, 0:n], start=(kt == 0), stop=(kt == 15)),
                              reads=[b_wA, hb_], writes=[pb2])
                    sz, szb = szp.next()
                    fw.op(act, lambda e: e.activation(out=sz[:, 0:n], in_=pt2[:, 0:n], func=AF.Silu), reads=[pb2], writes=[szb])
                    szs.append((sz, szb))
                    xc, xcb_ = xcp.next()

                    def xin(i):
                        if samp:
                            return xesv(ct)[:, :, i:i + 16]
                        return xe[:, ct, i:i + 512]

                    def xo(t_):
                        if samp:
                            return t_[:, 0:n].rearrange("p (s t) -> p s t", t=16)
                        return t_[:, 0:n]
                    fw.op(dve, lambda e: e.tensor_scalar(out=xo(xc), in0=xin(3), scalar1=cw[:, 3, ct:ct + 1], scalar2=cp[:, 0, ct:ct + 1],
                                                         op0=ALU.mult, op1=ALU.add), reads=[bx, b_cw, b_cp], writes=[xcb_])
                    for i in (2, 1, 0):
                        fw.op(dve, lambda e: e.scalar_tensor_tensor(out=xo(xc), in0=xin(i), scalar=cw[:, i, ct:ct + 1], in1=xo(xc),
                                                                    op0=ALU.mult, op1=ALU.add), reads=[bx, b_cw, xcb_], writes=[xcb_])
                    xcb, xcbb = xcbp.next()
                    fw.op(pool, lambda e: e.tensor_copy(out=xcb[:, 0:n], in_=xc[:, 0:n]), reads=[xcb_], writes=[xcbb])
                    if samp:
                        fw.op(pool, lambda e: e.tensor_copy(out=xlast[:, ct, 1:17, :], in_=xesv(ct)[:, :, 16:19]),
                              reads=[bx], writes=[b_xlast])
                    else:
                        if tb == 7:
                            fw.op(pool, lambda e: e.tensor_copy(out=xlast[:, ct, 0, :], in_=xe[:, ct, 512:515]),
                                  reads=[bx], writes=[b_xlast])
                        fw.op(pool, lambda e: e.tensor_copy(out=xe[:, ct, 0:3], in_=xe[:, ct, 512:515]), reads=[bx], writes=[bx])
                    xcs.append((xc, xcb_))
                    xcbs.append((xcb, xcbb))
                for c2 in range(2):
                    ct = blk * 2 + c2
                    xc, xcb_ = xcs[c2]
                    sz, szb = szs[c2]
                    r, rb = rp.next()
                    ig, ib = ip.next()
                    for (wG, bwG, bias_i, dst, dstb) in ((wR, b_wR, 1, r, rb), (wI, b_wI, 2, ig, ib)):
                        pt, pb = ps_next()
                        for kh in range(2):
                            fw.op(pe, lambda e: e.matmul(pt[:, 0:n], lhsT=wG[:, blk * 2 + kh, c2 * 128:(c2 + 1) * 128],
                                                         rhs=xcbs[kh][0][:, 0:n], start=(kh == 0), stop=(kh == 1)),
                                  reads=[bwG, xcbs[kh][1]], writes=[pb])
                        fw.op(act, lambda e: e.activation(out=dst[:, 0:n], in_=pt[:, 0:n], func=AF.Sigmoid,
                                                          bias=cp[:, bias_i, ct:ct + 1]), reads=[pb, b_cp], writes=[dstb])
                    a, ab = ap_.next()
                    a2, a2b = a2p.next()
                    fw.op(act, lambda e: e.activation(out=a[:, 0:n], in_=r[:, 0:n], func=AF.Exp, scale=negc[:, 0, ct:ct + 1]),
                          reads=[rb, b_negc], writes=[ab])
                    fw.op(act, lambda e: e.activation(out=a2[:, 0:n], in_=r[:, 0:n], func=AF.Exp, scale=negc[:, 1, ct:ct + 1]),
                          reads=[rb, b_negc], writes=[a2b])
                    fw.op(act, lambda e: e.activation(out=a2[:, 0:n], in_=a2[:, 0:n], func=AF.Sqrt, scale=-1.0, bias=1.0),
                          reads=[a2b], writes=[a2b])
                    u, ub = up.next()
                    fw.op(pool, lambda e: e.tensor_tensor(out=u[:, 0:n], in0=ig[:, 0:n], in1=xc[:, 0:n], op=ALU.mult),
                          reads=[ib, xcb_], writes=[ub])
                    fw.op(dve, lambda e: e.tensor_tensor(out=u[:, 0:n], in0=u[:, 0:n], in1=a2[:, 0:n], op=ALU.mult),
                          reads=[ub, a2b], writes=[ub])
                    h, hb2 = hp.next()
                    if samp:
                        for s in range(16):
                            fw.op(dve, lambda e: e.tensor_tensor_scan(out=h[:, s * 16:(s + 1) * 16], data0=a[:, s * 16:(s + 1) * 16],
                                                                      data1=u[:, s * 16:(s + 1) * 16], initial=stT[:, ct, s:s + 1],
                                                                      op0=ALU.mult, op1=ALU.add),
                                  reads=[ab, ub, b_stT], writes=[hb2])
                        fw.op(pool, lambda e: e.tensor_copy(out=hfin[:, ct, 1:17],
                                                            in_=h[:, 0:n].rearrange("p (s t) -> p s t", t=16)[:, :, 15]),
                              reads=[hb2], writes=[b_hfin])
                    else:
                        fw.op(dve, lambda e: e.tensor_tensor_scan(out=h[:, 0:n], data0=a[:, 0:n], data1=u[:, 0:n],
                                                                  initial=hc[:, ct:ct + 1], op0=ALU.mult, op1=ALU.add),
                              reads=[ab, ub, b_hc[ct]], writes=[hb2])
                        fw.op(pool, lambda e: e.tensor_copy(out=hc[:, ct:ct + 1], in_=h[:, n - 1:n]), reads=[hb2], writes=[b_hc[ct]])
                        if tb == 7:
                            fw.op(pool, lambda e: e.tensor_copy(out=hfin[:, ct, 0:1], in_=h[:, n - 1:n]), reads=[hb2], writes=[b_hfin])
                    y, yb = yp.next()
                    fw.op(dve, lambda e: e.tensor_tensor(out=y[:, 0:n], in0=h[:, 0:n], in1=sz[:, 0:n], op=ALU.mult),
                          reads=[hb2, szb], writes=[yb])
                    fw.dma(pool, y0_src[tb][ct * 128:(ct + 1) * 128, :], y[:, 0:n], reads=[yb], writes=[d_y0s[tb]])
            fw.collective("AllGather", y0_src[tb][:, :], y0_all[tb][:, :], reads=[d_y0s[tb]], writes=[d_y0a[tb]])
        so = strow; b_so = b_strow
        for (srcT, bsrc, ncols, r0) in ((hfin, b_hfin, 17, 0), (xlast, b_xlast, 51, 17)):
            for half in range(2):
                pt, pb = ps_next()
                for c4 in range(4):
                    ct = half * 4 + c4
                    s_ap = srcT[:, ct, :] if ncols == 17 else srcT[:, ct, :, :].rearrange("p s i -> p (s i)")
                    fw.op(pe, lambda e: e.transpose(pt[0:ncols, c4 * 128:(c4 + 1) * 128], s_ap, ident_f[:]),
                          reads=[bsrc, b_idf], writes=[pb])
                stage = so[0:ncols, half * 512:(half + 1) * 512]
                fw.op(act, lambda e: e.activation(out=stage, in_=pt[0:ncols, :], func=AF.Copy), reads=[pb], writes=[b_so])
            if ncols == 17:
                fw.dma(pool, lru_sl[:, :], so[0:17, :], reads=[b_so], writes=[d_out[3]])
            else:
                fw.dma(pool, conv_sl[:, :], so[0:51, :], reads=[b_so], writes=[d_out[4]])
        fw.barrier()
    if stages <= 2:
        fw.barrier()
        return nc

    def exchange_stats(es_, i_ss, i_r, which):
        sg = sb(es_, f"sg{which}", [128, 4, NTT], F32); b_sg = Buf()
        fw.dma(pool, st_src[which][:, :], stat[:, i_ss, :], reads=[b_stat[i_ss]], writes=[d_sts[which]])
        fw.collective("AllGather", st_src[which][:, :], st_all[which][:, :], reads=[d_sts[which]], writes=[d_sta[which]])
        fw.dma(sp, sg[:], st_all[which].rearrange("(r p) t -> p r t", p=128), reads=[d_sta[which]], writes=[b_sg])
        fw.op(dve, lambda e: e.tensor_tensor(out=stat[:, i_ss, :], in0=sg[:, 0, :], in1=sg[:, 1, :], op=ALU.add),
              reads=[b_sg], writes=[b_stat[i_ss]])
        for r_ in (2, 3):
            fw.op(dve, lambda e: e.tensor_tensor(out=stat[:, i_ss, :], in0=stat[:, i_ss, :], in1=sg[:, r_, :], op=ALU.add),
                  reads=[b_sg, b_stat[i_ss]], writes=[b_stat[i_ss]])
        rstd_from(i_ss, i_r)

    with ExitStack() as esR:
        R1 = sb(esR, "R1", [128, NTT, 512], F32); b_R1 = [Buf() for _ in range(NTT)]
        with ExitStack() as es:
            wO = sb(es, "wO", [128, 32, 512], BF16); b_wO = Buf()
            y0p = Pool(es, nc, "y0t", [128, 32, 512], BF16, 2)
            junk3 = sb(es, "junk3", [128, 512], BF16); b_j3 = Buf()
            xsp = Pool(es, nc, "xsl", [128, 512], F32, 3)
            for kq in range(4):
                fw.dma(pool, wO[:, kq * 8:(kq + 1) * 8, :],
                       w_out_a.rearrange("(kt p) f -> p kt f", p=128)[:, kq * 8:(kq + 1) * 8, :], writes=[b_wO])
            for tb in range(9):
                n128 = 4 if tb < 8 else 2
                n = n128 * 128
                yt, ytb = y0p.next()
                fw.dma(sp, yt[:, :, 0:n], y0_all[tb].rearrange("(kt p) t -> p kt t", p=128),
                       reads=[d_y0a[tb]], writes=[ytb])
                for j in range(n128):
                    tile = tb * 4 + j
                    pt, pb = ps_next()
                    for kt in range(32):
                        fw.op(pe, lambda e: e.matmul(pt[:, :], lhsT=yt[:, kt, j * 128:(j + 1) * 128], rhs=wO[:, kt, :],
                                                     start=(kt == 0), stop=(kt == 31)), reads=[ytb, b_wO], writes=[pb])
                    fw.op(act, lambda e: e.activation(out=R1[:, tile, :], in_=pt[:, :], func=AF.Copy), reads=[pb], writes=[b_R1[tile]])
                    fw.op(act, lambda e: e.activation(out=junk3[:], in_=pt[:, :], func=AF.Square,
                                                      accum_out=stat[:, 2, tile:tile + 1]),
                          reads=[pb], writes=[b_j3, b_stat[2]])
            if stages == 3:
                fw.barrier()
                return nc
            exchange_stats(es, 2, 3, 0)
            if stages == 3.5:
                fw.barrier()
                return nc
            for tile in range(NTT):
                rg = rg_of(tile)
                xs, xsb = xsp.next()
                fw.dma(sp, xs[:], xg_sl[tile * 128:(tile + 1) * 128, :], writes=[xsb])
                fw.op(dve, lambda e: e.scalar_tensor_tensor(out=R1[:, tile, :], in0=R1[:, tile, :], scalar=stat[:, 3, tile:tile + 1],
                                                            in1=GG0bc[:, rg, :], op0=ALU.mult, op1=ALU.mult),
                      reads=[b_R1[tile], b_stat[3], b_GG0], writes=[b_R1[tile]])
                fw.op(pool, lambda e: e.tensor_tensor(out=R1[:, tile, :], in0=R1[:, tile, :], in1=xs[:], op=ALU.add),
                      reads=[b_R1[tile], xsb], writes=[b_R1[tile]])
                fw.op(act, lambda e: e.activation(out=junk3[:], in_=R1[:, tile, :], func=AF.Square,
                                                  accum_out=stat[:, 4, tile:tile + 1]),
                      reads=[b_R1[tile]], writes=[b_j3, b_stat[4]])
                fw.dma(pool, x1_scr[tile * 128:(tile + 1) * 128, :], R1[:, tile, :], reads=[b_R1[tile]], writes=[d_x1])
            exchange_stats(es, 4, 5, 1)
            fw.barrier()
        with ExitStack() as es:
            t4p = Pool(es, nc, "t4", [128, 512], F32, 2)
            h1p = Pool(es, nc, "h1", [128, 512], BF16, 2)
            hkp = Pool(es, nc, "hk", [128, 512], BF16, 2)
            stg4 = Pool(es, nc, "stg4", [128, 8, 512], BF16, 2)
            for tb in range(9):
                n128 = 4 if tb < 8 else 2
                n = n128 * 128
                st_t, st_b = stg4.next()
                for j in range(n128):
                    tile = tb * 4 + j
                    rg = rg_of(tile)
                    t4, t4b = t4p.next()
                    fw.op(dve, lambda e: e.scalar_tensor_tensor(out=t4[:], in0=R1[:, tile, :], scalar=stat[:, 5, tile:tile + 1],
                                                                in1=A1bc[:, rg, :], op0=ALU.mult, op1=ALU.mult),
                          reads=[b_R1[tile], b_stat[5], b_A1], writes=[t4b])
                    h1, h1b = h1p.next()
                    fw.op(pool, lambda e: e.tensor_tensor(out=h1[:], in0=t4[:], in1=B1bc[:, rg, :], op=ALU.add),
                          reads=[t4b, b_B1], writes=[h1b])
                    hk, hkb = hkp.next()
                    fw.op(dve, lambda e: e.scalar_tensor_tensor(out=hk[:], in0=R1[:, tile, :], scalar=stat[:, 5, tile:tile + 1],
                                                                in1=GKVbc[:, :], op0=ALU.mult, op1=ALU.mult),
                          reads=[b_R1[tile], b_stat[5], b_GKV], writes=[hkb])
                    pt, pb = ps_next()
                    ptb = pt.bitcast(BF16)
                    for f4 in range(4):
                        fw.op(pe, lambda e: e.transpose(ptb[:, f4 * 128:(f4 + 1) * 128], hk[:, f4 * 128:(f4 + 1) * 128], ident_b[:]),
                              reads=[hkb, b_idb], writes=[pb])
                    for f4 in range(4):
                        fw.op(pe, lambda e: e.transpose(ptb[:, (4 + f4) * 128:(5 + f4) * 128], h1[:, f4 * 128:(f4 + 1) * 128], ident_b[:]),
                              reads=[h1b, b_idb], writes=[pb])
                    fw.op(act, lambda e: e.activation(out=st_t[:, :, j * 128:(j + 1) * 128],
                                                      in_=ptb[:, :].rearrange("p (a b) -> p a b", a=8), func=AF.Copy),
                          reads=[pb], writes=[st_b])
                fw.dma(pool, hn_src[tb].rearrange("(k p) t -> p k t", p=128), st_t[:, :, 0:n],
                       reads=[st_b], writes=[d_hns[tb]])
                fw.collective("AllGather", hn_src[tb][:, :], hn_all[tb][:, :], reads=[d_hns[tb]], writes=[d_hna[tb]])
            fw.barrier()
    if stages <= 4:
        fw.barrier()
        return nc
    lvl = 9 if stages >= 5 else int(round((stages - 4) * 10))

    with ExitStack() as es:
        wQ = sb(es, "wQ", [128, 16, 1024], BF16); b_wQ = Buf()
        wKV = sb(es, "wKV", [128, 16, 1024], BF16); b_wKV = Buf()
        h1T = Pool(es, nc, "h1T", [128, 16, 512], BF16, 2)
        hkT = Pool(es, nc, "hkT", [128, 16, 512], BF16, 2)
        qst = Pool(es, nc, "qst", [128, 512], BF16, 3)
        kvf = Pool(es, nc, "kvf", [128, 512], F32, 3)
        kvb = Pool(es, nc, "kvb", [128, 512], BF16, 3)
        kts = Pool(es, nc, "kts", [128, 4, 128], BF16, 2)
        for kq in range(4):
            fw.dma(pool, wQ[:, kq * 4:(kq + 1) * 4, :], w_in_b.rearrange("(kt p) c -> p kt c", p=128)[:, kq * 4:(kq + 1) * 4, :], writes=[b_wQ])
            fw.dma(pool, wKV[:, kq * 4:(kq + 1) * 4, :], w_kv.rearrange("(kt p) c -> p kt c", p=128)[:, kq * 4:(kq + 1) * 4, :], writes=[b_wKV])
        for tb in range(9):
            n128 = 4 if tb < 8 else 2
            n = n128 * 128
            c0 = tb * 512
            a1, a1b = h1T.next()
            ak, akb = hkT.next()
            hview = hn_all[tb].rearrange("(r k f p) t -> k p r f t", r=4, k=2, f=4, p=128)
            for r_ in range(4):
                fw.dma(sp, ak[:, r_ * 4:(r_ + 1) * 4, 0:n], hview[0][:, r_, :, :], reads=[d_hna[tb]], writes=[akb])
                fw.dma(sp, a1[:, r_ * 4:(r_ + 1) * 4, 0:n], hview[1][:, r_, :, :], reads=[d_hna[tb]], writes=[a1b])
            for hh in range(4):
                for which in range(2):
                    pt, pb = ps_next()
                    for kt in range(16):
                        fw.op(pe, lambda e: e.matmul(pt[:, 0:n], lhsT=wQ[:, kt, which * 512 + hh * 128:which * 512 + (hh + 1) * 128],
                                                     rhs=a1[:, kt, 0:n], start=(kt == 0), stop=(kt == 15)),
                              reads=[b_wQ, a1b], writes=[pb])
                    st, stb = qst.next()
                    fw.op(act, lambda e: e.activation(out=st[:, 0:n], in_=pt[:, 0:n], func=(AF.Copy if which == 0 else AF.Silu)),
                          reads=[pb], writes=[stb])
                    if lvl < 4:
                        continue
                    if which == 0:
                        fw.dma(pool, qT_scr[hh, :, c0:c0 + n], st[:, 0:n], reads=[stb], writes=[d_qT])
                    else:
                        fw.dma(pool, sz_scr[hh, :, c0:c0 + n], st[:, 0:n], reads=[stb], writes=[d_sz])
            for j in range(n128 if lvl >= 6 else 0):
                tile = tb * 4 + j
                for half in range(2):
                    pt, pb = ps_next()
                    for kt in range(16):
                        fw.op(pe, lambda e: e.matmul(pt[:, :], lhsT=ak[:, kt, j * 128:(j + 1) * 128], rhs=wKV[:, kt, half * 512:(half + 1) * 512],
                                                     start=(kt == 0), stop=(kt == 15)), reads=[b_wKV, akb], writes=[pb])
                    f_, fb = kvf.next()
                    fw.op(act, lambda e: e.activation(out=f_[:], in_=pt[:, :], func=AF.Copy), reads=[pb], writes=[fb])
                    fw.dma(pool, (k_sl if half == 0 else v_sl)[tile * 128:(tile + 1) * 128, :], f_[:], reads=[fb],
                           writes=[d_out[1 + half]])
                    b_, bb = kvb.next()
                    fw.op(dve, lambda e: e.tensor_copy(out=b_[:], in_=pt[:, :]), reads=[pb], writes=[bb])
                    if half == 1:
                        fw.dma(pool, v_scr[tile * 128:(tile + 1) * 128, :], b_[:], reads=[bb], writes=[d_v])
                    elif lvl >= 8:
                        pt2, pb2 = ps_next()
                        ptb = pt2.bitcast(BF16)
                        for hh in range(4):
                            fw.op(pe, lambda e: e.transpose(ptb[:, hh * 128:(hh + 1) * 128], b_[:, hh * 128:(hh + 1) * 128], ident_b[:]),
                                  reads=[bb, b_idb], writes=[pb2])
                        kt_, ktb = kts.next()
                        fw.op(act, lambda e: e.activation(out=kt_[:], in_=ptb[:, 0:512].rearrange("p (a b) -> p a b", a=4), func=AF.Copy),
                              reads=[pb2], writes=[ktb])
                        fw.dma(pool, kT_scr.rearrange("h p t -> p h t")[:, :, tile * 128:(tile + 1) * 128], kt_[:],
                               reads=[ktb], writes=[d_kT])
        fw.barrier()
    if stages <= 5:
        fw.barrier()
        return nc

    def attn_p1(E, Eb_, Fx, Fb_, S, spp, z_chunk, mask_fn):
        nch = (S + 511) // 512
        fw.op(pool, lambda e: e.memset(Fx[:, 0:1], 0.0), writes=[Fb_])
        for c in range(nch):
            s0 = c * 512
            w = min(512, S - s0)
            pt, pb = z_chunk(c, s0, w)
            fw.op(act, lambda e: e.activation(out=E[:, s0:s0 + w], in_=pt[:, 0:w], func=AF.Exp, scale=SCALE),
                  reads=[pb], writes=[Eb_])
            if c == nch - 1:
                mask_fn(E, Eb_)
            spc, spb = spp.next()
            fw.op(act, lambda e: e.activation(out=spc[:, 0:w], in_=E[:, s0:s0 + w], func=AF.Ln, bias=1.0),
                  reads=[Eb_], writes=[spb])
            fw.op(dve, lambda e: e.tensor_tensor_scan(out=Fx[:, s0 + 1:s0 + w + 1], data0=ones_c[:, 0:1].to_broadcast([128, w]),
                                                      data1=spc[:, 0:w], initial=Fx[:, s0:s0 + 1], op0=ALU.mult, op1=ALU.add),
                  reads=[spb, b_ones, Fb_], writes=[Fb_])

    def attn_p2(E, Eb_, Fx, Fb_, P, Pcb, PT, PTcb, S, spp, negt):
        nch = (S + 511) // 512
        nt, ntb = negt.next()
        fw.op(dve, lambda e: e.tensor_scalar(out=nt[:], in0=Fx[:, S:S + 1], scalar1=-1.0, scalar2=None, op0=ALU.mult),
              reads=[Fb_], writes=[ntb])
        for c in range(nch):
            s0 = c * 512
            w = min(512, S - s0)
            wc, wcb = spp.next()
            fw.op(act, lambda e: e.activation(out=wc[:, 0:w], in_=Fx[:, s0:s0 + w], func=AF.Exp, bias=nt[:, 0:1]),
                  reads=[Fb_, ntb], writes=[wcb])
            fw.op(pool, lambda e: e.tensor_tensor(out=P[:, s0:s0 + w], in0=E[:, s0:s0 + w], in1=wc[:, 0:w], op=ALU.mult),
                  reads=[Eb_, wcb], writes=[Pcb[c]])
            pt, pb = ps_next()
            ptb = pt.bitcast(BF16)
            nb = (w + 127) // 128
            for b_ in range(nb):
                kb = c * 4 + b_
                wk = min(128, S - kb * 128)
                fw.op(pe, lambda e: e.transpose(ptb[0:wk, b_ * 128:(b_ + 1) * 128], P[:, kb * 128:kb * 128 + wk], ident_b[:]),
                      reads=[Pcb[c], b_idb], writes=[pb])
            full = w // 128
            if full > 0:
                srcv = ptb[:, 0:full * 128].rearrange("p (a b) -> p a b", a=full)
                fw.op(dve, lambda e: e.tensor_copy(out=PT[:, c * 4:c * 4 + full, :], in_=srcv), reads=[pb], writes=[PTcb[c]])
            if full < nb:
                wk = w - full * 128
                fw.op(dve, lambda e: e.tensor_copy(out=PT[0:wk, c * 4 + full, :], in_=ptb[0:wk, full * 128:(full + 1) * 128]),
                      reads=[pb], writes=[PTcb[c]])

    with ExitStack() as es:
        qTh = sb(es, "qTh", [128, NT], BF16); b_qTh = Buf()
        kTh = sb(es, "kTh", [128, NT], BF16); b_kTh = Buf()
        szh = sb(es, "szh", [128, NT], BF16); b_szh = Buf()
        Vh = sb(es, "Vh", [128, NTT, 128], BF16); b_Vh = Buf()
        y1h = sb(es, "y1h", [128, NPT], BF16); b_y1h = Buf()
        Ep = Pool(es, nc, "E", [128, 4096], F32, 2)
        Fp = Pool(es, nc, "Fx", [128, 4100], F32, 2)
        spp = Pool(es, nc, "spc", [128, 512], F32, 6)
        Pp = Pool(es, nc, "P", [128, 4096], BF16, 2)
        PTp = Pool(es, nc, "PT", [128, 32, 128], BF16, 2)
        negt = Pool(es, nc, "negt", [128, 1], F32, 4)
        Pcbs = [[Buf() for _ in range(9)] for _ in range(2)]
        PTcbs = [[Buf() for _ in range(9)] for _ in range(2)]
        n_heads = 4 if stages >= 6 else 1
        for hh in range(n_heads):
            fw.dma(sp, qTh[:], qT_scr[hh, :, :], reads=[d_qT], writes=[b_qTh])
            fw.dma(sp, kTh[:], kT_scr[hh, :, :], reads=[d_kT], writes=[b_kTh])
            fw.dma(sp, szh[:], sz_scr[hh, :, :], reads=[d_sz], writes=[b_szh])
            fw.dma(sp, Vh[:], v_scr.rearrange("(b p) (h d) -> p b h d", p=128, d=128)[:, :, hh, :], reads=[d_v], writes=[b_Vh])
            def finish(pend):
                qb_, S_, E_, Eb__, Fx_, Fb__ = pend
                P, _ = Pp.next()
                PT, _ = PTp.next()
                Pcb = Pcbs[(Pp.i - 1) % 2]
                PTcb = PTcbs[(PTp.i - 1) % 2]
                attn_p2(E_, Eb__, Fx_, Fb__, P, Pcb, PT, PTcb, S_, spp, negt)
                po, pob = ps_next()
                for kb in range(qb_ + 1):
                    fw.op(pe, lambda e: e.matmul(po[:, 0:128], lhsT=Vh[:, kb, :], rhs=PT[:, kb, :], start=(kb == 0), stop=(kb == qb_)),
                          reads=[b_Vh, PTcb[kb // 4]], writes=[pob])
                fw.op(dve, lambda e: e.tensor_tensor(out=y1h[:, qb_ * 128:(qb_ + 1) * 128], in0=po[:, 0:128],
                                                     in1=szh[:, qb_ * 128:(qb_ + 1) * 128], op=ALU.mult),
                      reads=[pob, b_szh], writes=[b_y1h])

            pend = None
            for qb in range(32):
                S = 128 * (qb + 1)
                E, Eb_ = Ep.next()
                Fx, Fb_ = Fp.next()

                def z_chunk(c, s0, w):
                    pt, pb = ps_next()
                    fw.op(pe, lambda e: e.matmul(pt[:, 0:w], lhsT=qTh[:, qb * 128:(qb + 1) * 128], rhs=kTh[:, s0:s0 + w],
                                                 start=True, stop=True), reads=[b_qTh, b_kTh], writes=[pb])
                    return pt, pb

                def mask_fn(E_, Eb__):
                    fw.op(dve, lambda e: e.tensor_tensor(out=E_[:, S - 128:S], in0=E_[:, S - 128:S], in1=tri[:], op=ALU.mult),
                          reads=[Eb__, b_tri], writes=[Eb__])
                attn_p1(E, Eb_, Fx, Fb_, S, spp, z_chunk, mask_fn)
                if pend is not None:
                    finish(pend)
                pend = (qb, S, E, Eb_, Fx, Fb_)
            finish(pend)
            for tb in range(8):
                fw.dma(pool, y1_src[tb][hh * 128:(hh + 1) * 128, :], y1h[:, tb * 512:(tb + 1) * 512], reads=[b_y1h], writes=[d_y1s[tb]])
        fw.barrier()
    if stages < 6:
        fw.barrier()
        return nc
    for tb in range(8):
        fw.collective("AllGather", y1_src[tb][:, :], y1_all[tb][:, :], reads=[d_y1s[tb]], writes=[d_y1a[tb]])

    if stages >= 7:
        ps_banks[0] = list(range(6))
        with ExitStack() as es:
            qs = sb(es, "qs", [128, 4, 256], BF16); b_qs = Buf()
            ks = sb(es, "ks", [128, 4, 256], BF16); b_ks = Buf()
            szs = sb(es, "szs", [128, 4, 256], BF16); b_szs = Buf()
            vnew = sb(es, "vnew", [16, 16, 512], BF16); b_vnew = Buf()
            y1sT = sb(es, "y1sT", [128, 4, 256], BF16); b_y1sT = Buf()
            QP = sb(es, "QP", [128, 8, 128], BF16); b_QP = Buf()
            Vbp = Pool(es, nc, "Vbc", [128, 4, 2, 512], BF16, 3)
            zer = sb(es, "zer", [128, 128], BF16); b_zer = Buf()
            E = sb(es, "Es", [128, 4112], F32); Eb_ = Buf()
            Fx = sb(es, "Fxs", [128, 4116], F32); Fb_ = Buf()
            P = sb(es, "Ps", [128, 4112], BF16); Pcb = [Buf() for _ in range(9)]
            PT = sb(es, "PTs", [128, 33, 128], BF16); PTcb = [Buf() for _ in range(9)]
            spp = Pool(es, nc, "spcs", [128, 512], F32, 3)
            negt = Pool(es, nc, "negts", [128, 1], F32, 2)
            Kcp = Pool(es, nc, "Kc", [128, 4, 512], F32, 4)
            Kbp = Pool(es, nc, "Kb", [128, 4, 512], BF16, 3)
            Vcp = Pool(es, nc, "Vc", [128, 4, 512], F32, 3)
            KTp = Pool(es, nc, "KTc", [128, 512], BF16, 4)
            fw.dma(sp, qs[:], qT_scr.rearrange("h p t -> p h t")[:, :, NPT:NT], reads=[d_qT], writes=[b_qs])
            fw.dma(sp, ks[:], kT_scr.rearrange("h p t -> p h t")[:, :, NPT:NT], reads=[d_kT], writes=[b_ks])
            fw.dma(sp, szs[:], sz_scr.rearrange("h p t -> p h t")[:, :, NPT:NT], reads=[d_sz], writes=[b_szs])
            fw.dma(sp, vnew[:], v_scr[NPT:NT, :].rearrange("(s j) f -> j s f", j=16), reads=[d_v], writes=[b_vnew])
            fw.op(pool, lambda e: e.memset(QP[:], 0.0), writes=[b_QP])
            fw.op(pool, lambda e: e.memset(zer[:], 0.0), writes=[b_zer])
            zi = [0]
            for pg in range(8):
                for i in range(8):
                    sl, hh = i // 4, i % 4
                    sq = 2 * pg + sl
                    fw.op(pool, lambda e: e.tensor_copy(out=QP[:, i, 16 * i:16 * i + 16], in_=qs[:, hh, sq * 16:(sq + 1) * 16]),
                          reads=[b_qs], writes=[b_QP])

                def z_chunk(c, s0, w):
                    zt, zb = psb[6 + zi[0] % 2]
                    zi[0] += 1
                    if c < 8:
                        kcs = []
                        for sl in range(2):
                            kc, kcb = Kcp.next()
                            fw.dma(sp, kc[:], cache_k[2 * pg + sl, c * 512:(c + 1) * 512, :].rearrange("(b p) f -> p b f", p=128),
                                   writes=[kcb])
                            kb16, kb16b = Kbp.next()
                            fw.op(act, lambda e: e.activation(out=kb16[:], in_=kc[:], func=AF.Copy), reads=[kcb], writes=[kb16b])
                            kcs.append((kb16, kb16b))
                        for i in range(8):
                            sl, hh = i // 4, i % 4
                            kc, kcb = kcs[sl]
                            pt, pb = ps_next()
                            ptb = pt.bitcast(BF16)
                            for b in range(4):
                                fw.op(pe, lambda e: e.transpose(ptb[:, b * 128:(b + 1) * 128], kc[:, b, hh * 128:(hh + 1) * 128], ident_b[:]),
                                      reads=[kcb, b_idb], writes=[pb])
                            kt_, ktb = KTp.next()
                            if i % 4 == 0:
                                fw.op(act, lambda e: e.activation(out=kt_[:], in_=ptb[:, 0:512], func=AF.Copy), reads=[pb], writes=[ktb])
                            else:
                                fw.op(dve, lambda e: e.tensor_copy(out=kt_[:], in_=ptb[:, 0:512]), reads=[pb], writes=[ktb])
                            fw.op(pe, lambda e: e.matmul(zt[:, 0:512], lhsT=QP[:, i, :], rhs=kt_[:], start=(i == 0), stop=(i == 7)),
                                  reads=[b_QP, ktb], writes=[zb])
                    else:
                        for i in range(8):
                            sl, hh = i // 4, i % 4
                            sq = 2 * pg + sl
                            fw.op(pe, lambda e: e.matmul(zt[:, 0:16], lhsT=QP[:, i, :], rhs=ks[:, hh, sq * 16:(sq + 1) * 16],
                                                         start=(i == 0), stop=(i == 7)), reads=[b_QP, b_ks], writes=[zb])
                    return zt, zb

                def mask_fn(E_, Eb__):
                    fw.op(dve, lambda e: e.tensor_tensor(out=E_[:, 4096:4112], in0=E_[:, 4096:4112], in1=m16[:], op=ALU.mult),
                          reads=[Eb__, b_m16], writes=[Eb__])
                attn_p1(E, Eb_, Fx, Fb_, 4112, spp, z_chunk, mask_fn)
                attn_p2(E, Eb_, Fx, Fb_, P, Pcb, PT, PTcb, 4112, spp, negt)
                po, pob = ps_next()
                fw.op(pe, lambda e: e.matmul(po[:, 0:128], lhsT=zer[:], rhs=zer[:], start=True, stop=False, skip_group_check=True),
                      reads=[b_zer], writes=[pob])
                for c in range(8):
                    vb, vbb = Vbp.next()
                    for sl in range(2):
                        vc, vcb = Vcp.next()
                        fw.dma(sp, vc[:], cache_v[2 * pg + sl, c * 512:(c + 1) * 512, :].rearrange("(b p) f -> p b f", p=128),
                               writes=[vcb])
                        fw.op(act, lambda e: e.activation(out=vb[:, :, sl, :], in_=vc[:], func=AF.Copy), reads=[vcb], writes=[vbb])
                    for i in range(8):
                        sl, hh = i // 4, i % 4
                        for b in range(4):
                            kb = c * 4 + b
                            fw.op(pe, lambda e: e.matmul(po[:, i * 16:(i + 1) * 16], lhsT=vb[:, b, sl, hh * 128:(hh + 1) * 128],
                                                         rhs=PT[:, kb, i * 16:(i + 1) * 16], start=False, stop=False,
                                                         skip_group_check=True),
                                  reads=[vbb, PTcb[c]], writes=[pob])
                for i in range(8):
                    sl, hh = i // 4, i % 4
                    sq = 2 * pg + sl
                    fw.op(pe, lambda e: e.matmul(po[:, i * 16:(i + 1) * 16], lhsT=vnew[0:16, sq, hh * 128:(hh + 1) * 128],
                                                 rhs=PT[0:16, 32, i * 16:(i + 1) * 16], start=False, stop=True,
                                                 skip_group_check=True),
                          reads=[b_vnew, PTcb[8]], writes=[pob])
                for sl in range(2):
                    sq = 2 * pg + sl
                    fw.op(dve, lambda e: e.tensor_tensor(out=y1sT[:, :, sq * 16:(sq + 1) * 16],
                                                         in0=po[:, sl * 64:(sl + 1) * 64].rearrange("p (h t) -> p h t", h=4),
                                                         in1=szs[:, :, sq * 16:(sq + 1) * 16], op=ALU.mult),
                          reads=[pob, b_szs], writes=[b_y1sT])
            fw.dma(pool, y1_src[8].rearrange("(h p) t -> p h t", p=128), y1sT[:], reads=[b_y1sT], writes=[d_y1s[8]])
            fw.barrier()
        ps_banks[0] = list(range(8))

    fw.collective("AllGather", y1_src[8][:, :], y1_all[8][:, :], reads=[d_y1s[8]], writes=[d_y1a[8]])
    with ExitStack() as es:
        wO2 = sb(es, "wO2", [128, 16, 512], BF16); b_wO2 = Buf()
        R2 = sb(es, "R2", [128, NTT, 512], F32); b_R2 = [Buf() for _ in range(NTT)]
        y1p = Pool(es, nc, "y1t", [128, 16, 512], BF16, 2)
        junk8 = sb(es, "junk8", [128, 512], BF16); b_j8 = Buf()
        x1p = Pool(es, nc, "x1t", [128, 512], F32, 3)
        for kq in range(4):
            fw.dma(pool, wO2[:, kq * 4:(kq + 1) * 4, :], w_out_b.rearrange("(kt p) f -> p kt f", p=128)[:, kq * 4:(kq + 1) * 4, :], writes=[b_wO2])
        for tb in range(9):
            n128 = 4 if tb < 8 else 2
            n = n128 * 128
            yt, ytb = y1p.next()
            fw.dma(sp, yt[:, :, 0:n], y1_all[tb].rearrange("(kt p) t -> p kt t", p=128), reads=[d_y1a[tb]], writes=[ytb])
            for j in range(n128):
                tile = tb * 4 + j
                pt, pb = ps_next()
                for kt in range(16):
                    fw.op(pe, lambda e: e.matmul(pt[:, :], lhsT=yt[:, kt, j * 128:(j + 1) * 128], rhs=wO2[:, kt, :],
                                                 start=(kt == 0), stop=(kt == 15)), reads=[ytb, b_wO2], writes=[pb])
                fw.op(act, lambda e: e.activation(out=R2[:, tile, :], in_=pt[:, :], func=AF.Copy), reads=[pb], writes=[b_R2[tile]])
                fw.op(act, lambda e: e.activation(out=junk8[:], in_=pt[:, :], func=AF.Square, accum_out=stat[:, 6, tile:tile + 1]),
                      reads=[pb], writes=[b_j8, b_stat[6]])
        exchange_stats(es, 6, 7, 2)
        for tile in range(NTT):
            rg = rg_of(tile)
            xs, xsb = x1p.next()
            fw.dma(sp, xs[:], x1_scr[tile * 128:(tile + 1) * 128, :], reads=[d_x1], writes=[xsb])
            fw.op(dve, lambda e: e.scalar_tensor_tensor(out=R2[:, tile, :], in0=R2[:, tile, :], scalar=stat[:, 7, tile:tile + 1],
                                                        in1=GG1bc[:, rg, :], op0=ALU.mult, op1=ALU.mult),
                  reads=[b_R2[tile], b_stat[7], b_GG1], writes=[b_R2[tile]])
            fw.op(pool, lambda e: e.tensor_tensor(out=R2[:, tile, :], in0=R2[:, tile, :], in1=xs[:], op=ALU.add),
                  reads=[b_R2[tile], xsb], writes=[b_R2[tile]])
            fw.dma(pool, y_sl[tile * 128:(tile + 1) * 128, :], R2[:, tile, :], reads=[b_R2[tile]], writes=[d_out[0]])
        fw.barrier()
    return nc


_NC_CACHE = {}


def _consts():
    ident = np.eye(128, dtype=np.float32)
    q = np.arange(128)
    tri = (q[None, :] < q[:, None]).astype(np.float32)
    m16 = ((np.arange(16)[None, :]) < (q[:, None] % 16)).astype(np.float32)
    sel = np.zeros((17, 3, 128), np.float32)
    sel[0, 0, :] = 1.0
    for p in range(128):
        sel[1 + p // 16, 1, p] = 1.0
        sel[9 + p // 16, 2, p] = 1.0
    return ident, tri, m16, sel


def make_in_maps(x_prompt, x_sample, c_prompt, c_sample, cache_k, cache_v, state_lru, state_conv,
                 g_pre, g_post, w_ada, b_ada, w_in_a, conv_w, conv_b, w_rgate, b_rgate, w_igate, b_igate,
                 lru_lambda, w_out_a, g_kv, w_kv, w_in_b, w_out_b):
    f = lambda a: np.ascontiguousarray(np.asarray(a, dtype=np.float32))
    ident, tri, m16, sel = _consts()
    in_maps = []
    for c in range(8):
        g, m = c // 4, c % 4
        fs = slice(512 * m, 512 * m + 512)
        cs = slice(1024 * m, 1024 * m + 1024)
        ss = slice(16 * g, 16 * g + 16)
        xg = np.concatenate([x_prompt[g], x_sample[ss].reshape(256, D)], axis=0)
        d = {
            "xg": f(xg),
            "xg_sl": f(xg[:, fs]),
            "cg": f(np.concatenate([c_prompt[g:g + 1], c_sample[ss]], axis=0)),
            "w_ada0": f(w_ada[0][:, 0:4096]),
            "b_ada0": f(b_ada[0][0:4096]),
            "w_adas": f(np.concatenate([w_ada[0][:, 4096 + 512 * m:4096 + 512 * m + 512],
                                        w_ada[1][:, 512 * m:512 * m + 512],
                                        w_ada[1][:, 2048 + 512 * m:2048 + 512 * m + 512],
                                        w_ada[1][:, 4096 + 512 * m:4096 + 512 * m + 512]], axis=1)),
            "b_adas": f(np.concatenate([b_ada[0][4096 + 512 * m:4096 + 512 * m + 512],
                                        b_ada[1][512 * m:512 * m + 512],
                                        b_ada[1][2048 + 512 * m:2048 + 512 * m + 512],
                                        b_ada[1][4096 + 512 * m:4096 + 512 * m + 512]])),
            "g_pre0": f(g_pre[0]),
            "g_sl": f(np.stack([g_post[0][fs], g_pre[1][fs], g_post[1][fs], g_kv[fs]])),
            "w_in_a": f(np.concatenate([w_in_a[0][:, cs], w_in_a[0][:, 4096 + 1024 * m:4096 + 1024 * m + 1024]], axis=1)),
            "conv_w": f(conv_w[0][:, cs]),
            "chp": f(np.stack([conv_b[0][cs], b_rgate[0][cs], b_igate[0][cs], lru_lambda[0][cs]])),
            "w_rg": f(w_rgate[0][4 * m:4 * m + 4]),
            "w_ig": f(w_igate[0][4 * m:4 * m + 4]),
            "st_in": f(np.concatenate([state_lru[0][ss][:, cs], state_conv[0][ss][:, :, cs].reshape(48, 1024)], axis=0)),
            "w_out_a": f(w_out_a[0][:, fs]),
            "w_kv": f(np.concatenate([w_kv[:, fs], w_kv[:, 2048 + 512 * m:2048 + 512 * m + 512]], axis=1)),
            "w_in_b": f(np.concatenate([w_in_b[0][:, fs], w_in_b[0][:, 2048 + 512 * m:2048 + 512 * m + 512]], axis=1)),
            "w_out_b": f(w_out_b[0][:, fs]),
            "cache_k": f(cache_k[ss][:, :, 4 * m:4 * m + 4, :].reshape(16, 4096, 512)),
            "cache_v": f(cache_v[ss][:, :, 4 * m:4 * m + 4, :].reshape(16, 4096, 512)),
            "ident": ident, "tri": tri, "m16": m16, "sel": sel,
        }
        in_maps.append(d)
    return in_maps


def assemble(results):
    y_prompt = np.zeros((2, 4096, D), np.float32)
    y_sample = np.zeros((32, 16, D), np.float32)
    k_prompt = np.zeros((2, 4096, 16, 128), np.float32)
    v_prompt = np.zeros((2, 4096, 16, 128), np.float32)
    k_sample = np.zeros((32, 16, 16, 128), np.float32)
    v_sample = np.zeros((32, 16, 16, 128), np.float32)
    lru_prompt = np.zeros((1, 2, 4096), np.float32)
    lru_sample = np.zeros((1, 32, 4096), np.float32)
    conv_prompt = np.zeros((1, 2, 3, 4096), np.float32)
    conv_sample = np.zeros((1, 32, 3, 4096), np.float32)
    for c in range(8):
        g, m = c // 4, c % 4
        r = results[c]
        fs = slice(512 * m, 512 * m + 512)
        cs = slice(1024 * m, 1024 * m + 1024)
        ss = slice(16 * g, 16 * g + 16)
        y_prompt[g][:, fs] = r["y_sl"][:4096]
        y_sample[ss].reshape(256, D)[:, fs] = r["y_sl"][4096:]
        k_prompt[g].reshape(4096, D)[:, fs] = r["k_sl"][:4096]
        v_prompt[g].reshape(4096, D)[:, fs] = r["v_sl"][:4096]
        k_sample[ss].reshape(256, D)[:, fs] = r["k_sl"][4096:]
        v_sample[ss].reshape(256, D)[:, fs] = r["v_sl"][4096:]
        lru_prompt[0, g, cs] = r["lru_sl"][0]
        lru_sample[0, ss, cs] = r["lru_sl"][1:17]
        cv = r["conv_sl"].reshape(17, 3, 1024)
        conv_prompt[0, g, :, cs] = cv[0]
        conv_sample[0, ss, :, cs] = cv[1:17]
    return (y_prompt, y_sample, k_prompt, v_prompt, k_sample, v_sample,
            lru_prompt, lru_sample, conv_prompt, conv_sample)


def kernel(**inputs):
    inputs = {k: np.asarray(v) for k, v in inputs.items()}
    in_maps = make_in_maps(**inputs)
    if "nc" not in _NC_CACHE:
        _NC_CACHE["nc"] = build_nc()
    names = _NC_CACHE["in_names"]
    in_maps = [{k: d[k] for k in names} for d in in_maps]
    res = run_bass_kernel_spmd(_NC_CACHE["nc"], in_maps, core_ids=list(range(8)))
    return assemble(res.results)
```

```python
import numpy as np
import concourse.bass as bass
import concourse.mybir as mybir
from concourse.bass_utils import run_bass_kernel_spmd

F32 = mybir.dt.float32
BF16 = mybir.dt.bfloat16
AF = mybir.ActivationFunctionType
ALU = mybir.AluOpType

D = 2048
NPT = 4096
NST = 256
NT = NPT + NST
NTT = NT // 128
EPS = 1e-6
GROUPS = [[0, 1, 2, 3], [4, 5, 6, 7]]
SCALE = 128 ** -0.5


class Buf:
    __slots__ = ("name", "w", "r", "excl")

    def __init__(self, name="", excl=False):
        self.name = name
        self.w = None
        self.r = {}
        self.excl = excl


class _Eng:
    def __init__(self, fw, name, eng, same_sync):
        self.fw = fw
        self.name = name
        self.eng = eng
        self.epoch = 0
        self.count = 0
        self.sem = fw.nc.alloc_semaphore(name=f"es_{name}_0")
        self.key = (name, 0)
        fw.sems[self.key] = self.sem
        self.seen = {}
        self.same_sync = same_sync
        self.dslots = []
        self.dnext = 0

    def tick(self, inst):
        if self.count >= 30000:
            self.epoch += 1
            self.count = 0
            self.sem = self.fw.nc.alloc_semaphore(name=f"es_{self.name}_{self.epoch}")
            self.key = (self.name, self.epoch)
            self.fw.sems[self.key] = self.sem
        self.count += 1
        inst.then_inc(self.sem, 1)
        return (self.key, self.count)


class FW:
    def __init__(self, nc, same_sync=True):
        self.nc = nc
        self.sems = {}
        self.pe = _Eng(self, "pe", nc.tensor, False)
        self.act = _Eng(self, "act", nc.scalar, same_sync)
        self.dve = _Eng(self, "dve", nc.vector, same_sync)
        self.pool = _Eng(self, "pool", nc.gpsimd, same_sync)
        self.sp = _Eng(self, "sp", nc.sync, False)
        self.engs = [self.pe, self.act, self.dve, self.pool, self.sp]
        for q, n in ((self.sp, 16), (self.pool, 12)):
            for i in range(n):
                key = ("d" + q.name, i)
                self.sems[key] = nc.alloc_semaphore(name=f"ds_{q.name}_{i}")
                q.dslots.append([key, 0, None])
        self.n_cc = 0
        self.cc_tickets = []

    def _wait(self, E, ticket):
        key, val = ticket
        if key[0] == E.name and not E.same_sync:
            return
        if E.seen.get(key, 0) >= val:
            return
        E.eng.wait_ge(self.sems[key], val)
        E.seen[key] = val

    def _deps(self, E, reads, writes):
        deps = {}

        def add(t):
            if t is None:
                return
            k, v = t
            if deps.get(k, 0) < v:
                deps[k] = v
        for b in reads:
            add(b.w)
            if b.excl:
                for k, v in b.r.items():
                    if k[0] != E.name:
                        add((k, v))
        for b in writes:
            add(b.w)
            for k, v in b.r.items():
                add((k, v))
        for k, v in deps.items():
            self._wait(E, (k, v))

    def _commit(self, ticket, reads, writes):
        k, v = ticket
        for b in reads:
            if b.r.get(k, 0) < v:
                b.r[k] = v
        for b in writes:
            b.w = ticket
            b.r = {}

    def op(self, E, emit, reads=(), writes=()):
        self._deps(E, reads, writes)
        inst = emit(E.eng)
        t = E.tick(inst)
        self._commit(t, reads, writes)
        return t

    def dma(self, Q, out, in_, reads=(), writes=(), **kw):
        slot = Q.dslots[Q.dnext % len(Q.dslots)]
        Q.dnext += 1
        if slot[2] is not None:
            self._wait(Q, slot[2])
        self._deps(Q, reads, writes)
        inst = Q.eng.dma_start(out=out, in_=in_, **kw)
        slot[1] += 1
        t = (slot[0], 16 * slot[1])
        inst.then_inc(self.sems[slot[0]], 16)
        slot[2] = t
        self._commit(t, reads, writes)
        return t

    def collective(self, kind, in_ap, out_ap, reads=(), writes=()):
        Q = self.pool
        self._deps(Q, reads, writes)
        key = ("cc", self.n_cc)
        self.n_cc += 1
        self.sems[key] = self.nc.alloc_semaphore(name=f"cc_{key[1]}")
        inst = Q.eng.collective_compute(kind, ALU.bypass, replica_groups=GROUPS,
                                        ins=[in_ap], outs=[out_ap])
        inst.then_inc(self.sems[key], 1)
        t = (key, 1)
        self.cc_tickets.append(t)
        self._commit(t, reads, writes)
        return t

    def barrier(self):
        tickets = []
        for E in self.engs:
            if E.count > 0:
                tickets.append((E.key, E.count))
            for s in E.dslots:
                if s[2] is not None:
                    tickets.append(s[2])
        tickets += self.cc_tickets
        for E in self.engs:
            for t in tickets:
                if t[0][0] == E.name:
                    continue
                self._wait(E, t)


class Pool:
    def __init__(self, es, nc, name, shape, dtype, n):
        self.tiles = []
        for i in range(n):
            t = es.enter_context(nc.sbuf_tensor(f"p_{name}_{i}", list(shape), dtype))
            self.tiles.append((t, Buf(f"{name}_{i}")))
        self.i = 0

    def next(self):
        t = self.tiles[self.i % len(self.tiles)]
        self.i += 1
        return t


from contextlib import ExitStack


def build_nc(stages=9):
    nc = bass.Bass("TRN2", target_bir_lowering=False)

    in_names = []
    _NC_CACHE["in_names"] = in_names

    def din(name, shape, dt=F32):
        in_names.append(name)
        return nc.dram_tensor(name, list(shape), dt, kind="ExternalInput").ap()

    def dout(name, shape, dt=F32):
        return nc.dram_tensor(name, list(shape), dt, kind="ExternalOutput").ap()

    def dscr(name, shape, dt, local=False):
        if local:
            return nc.dram_tensor(name, list(shape), dt, kind="Internal", addr_space="Local").ap()
        return nc.dram_tensor(name, list(shape), dt, kind="Internal").ap()

    xg = din("xg", [NT, D])
    xg_sl = din("xg_sl", [NT, 512])
    cg = din("cg", [17, D])
    w_ada0 = din("w_ada0", [D, 4096])
    b_ada0 = din("b_ada0", [4096])
    w_adas = din("w_adas", [D, 2048])
    b_adas = din("b_adas", [2048])
    g_pre0 = din("g_pre0", [D])
    g_sl = din("g_sl", [4, 512])
    w_in_a = din("w_in_a", [D, 2048])
    conv_w = din("conv_w", [4, 1024])
    chp = din("chp", [4, 1024])
    w_rg = din("w_rg", [4, 256, 256])
    w_ig = din("w_ig", [4, 256, 256])
    st_in = din("st_in", [64, 1024])
    w_out_a = din("w_out_a", [4096, 512])
    w_kv = din("w_kv", [D, 1024])
    w_in_b = din("w_in_b", [D, 1024])
    w_out_b = din("w_out_b", [D, 512])
    if stages >= 7:
        cache_k = din("cache_k", [16, 4096, 512])
        cache_v = din("cache_v", [16, 4096, 512])
    ident_in = din("ident", [128, 128])
    tri_in = din("tri", [128, 128])
    m16_in = din("m16", [128, 16])
    sel_in = din("sel", [17, 3, 128])

    y_sl = dout("y_sl", [NT, 512])
    k_sl = dout("k_sl", [NT, 512])
    v_sl = dout("v_sl", [NT, 512])
    lru_sl = dout("lru_sl", [17, 1024])
    conv_sl = dout("conv_sl", [51, 1024])

    hn0_scr = dscr("hn0_scr", [D, NT], BF16)
    TBN = [512] * 8 + [256]
    y0_src = [dscr(f"y0_src{tb}", [1024, TBN[tb]], BF16) for tb in range(9)]
    y0_all = [dscr(f"y0_all{tb}", [4096, TBN[tb]], BF16, local=True) for tb in range(9)]
    st_src = [dscr(f"st_src{i}", [128, NTT], F32) for i in range(3)]
    st_all = [dscr(f"st_all{i}", [512, NTT], F32, local=True) for i in range(3)]
    x1_scr = dscr("x1_scr", [NT, 512], F32)
    hn_src = [dscr(f"hn_src{tb}", [1024, TBN[tb]], BF16) for tb in range(9)]
    hn_all = [dscr(f"hn_all{tb}", [4096, TBN[tb]], BF16, local=True) for tb in range(9)]
    qT_scr = dscr("qT_scr", [4, 128, NT], BF16)
    kT_scr = dscr("kT_scr", [4, 128, NT], BF16)
    sz_scr = dscr("sz_scr", [4, 128, NT], BF16)
    v_scr = dscr("v_scr", [NT, 512], BF16)
    y1_src = [dscr(f"y1_src{tb}", [512, TBN[tb]], BF16) for tb in range(9)]
    y1_all = [dscr(f"y1_all{tb}", [2048, TBN[tb]], BF16, local=True) for tb in range(9)]

    fw = FW(nc)
    pe, act, dve, pool, sp = fw.pe, fw.act, fw.dve, fw.pool, fw.sp
    d_hn0 = Buf()
    d_y0s = [Buf() for _ in range(9)]
    d_y0a = [Buf() for _ in range(9)]
    d_sts = [Buf() for _ in range(3)]
    d_sta = [Buf() for _ in range(3)]
    d_x1 = Buf()
    d_hns = [Buf() for _ in range(9)]
    d_hna = [Buf() for _ in range(9)]
    d_y1s = [Buf() for _ in range(9)]
    d_y1a = [Buf() for _ in range(9)]
    d_qT, d_kT, d_sz, d_v = Buf(), Buf(), Buf(), Buf()
    d_out = [Buf() for _ in range(5)]

    glob = ExitStack()

    def sb(es, name, shape, dt):
        return es.enter_context(nc.sbuf_tensor("s_" + name, list(shape), dt))

    psb = [(nc.alloc_psum_tensor(f"ps{i}", [128, 512], F32), Buf(f"ps{i}", excl=True)) for i in range(8)]
    ps_i = [0]

    ps_banks = [list(range(8))]

    def ps_next():
        bl = ps_banks[0]
        t = psb[bl[ps_i[0] % len(bl)]]
        ps_i[0] += 1
        return t

    ident_f = sb(glob, "ident_f", [128, 128], F32); b_idf = Buf()
    ident_b = sb(glob, "ident_b", [128, 128], BF16); b_idb = Buf()
    tri = sb(glob, "tri", [128, 128], F32); b_tri = Buf()
    m16 = sb(glob, "m16", [128, 16], F32); b_m16 = Buf()
    sel = sb(glob, "sel", [17, 3, 128], F32); b_sel = Buf()
    ones_c = sb(glob, "ones_c", [128, 1], F32); b_ones = Buf()
    GG0bc = sb(glob, "GG0bc", [128, 3, 512], F32); b_GG0 = Buf()
    A1bc = sb(glob, "A1bc", [128, 3, 512], F32); b_A1 = Buf()
    B1bc = sb(glob, "B1bc", [128, 3, 512], F32); b_B1 = Buf()
    GG1bc = sb(glob, "GG1bc", [128, 3, 512], F32); b_GG1 = Buf()
    GKVbc = sb(glob, "GKVbc", [128, 512], F32); b_GKV = Buf()
    stat = sb(glob, "stat", [128, 8, NTT], F32)
    b_stat = [Buf() for _ in range(8)]

    fw.dma(sp, ident_f[:], ident_in[:, :], writes=[b_idf])
    fw.dma(sp, tri[:], tri_in[:, :], writes=[b_tri])
    fw.dma(sp, m16[:], m16_in[:, :], writes=[b_m16])
    fw.dma(sp, sel[:], sel_in[:, :, :], writes=[b_sel])
    fw.dma(sp, GKVbc[:], g_sl[3, :].partition_broadcast(128), writes=[b_GKV])
    fw.op(act, lambda e: e.activation(out=ident_b[:], in_=ident_f[:], func=AF.Copy), reads=[b_idf], writes=[b_idb])
    fw.op(pool, lambda e: e.memset(ones_c[:], 1.0), writes=[b_ones])

    def rg_of(tile):
        return 0 if tile < 32 else tile - 31

    def rstd_from(ss_i, r_i, ncols=NTT):
        fw.op(dve, lambda e: e.tensor_scalar(out=stat[:, r_i, 0:ncols], in0=stat[:, ss_i, 0:ncols], scalar1=1.0 / D,
                                             scalar2=EPS, op0=ALU.mult, op1=ALU.add),
              reads=[b_stat[ss_i]], writes=[b_stat[r_i]])
        fw.op(act, lambda e: e.activation(out=stat[:, r_i, 0:ncols], in_=stat[:, r_i, 0:ncols], func=AF.Sqrt),
              reads=[b_stat[r_i]], writes=[b_stat[r_i]])
        fw.op(dve, lambda e: e.reciprocal(out=stat[:, r_i, 0:ncols], in_=stat[:, r_i, 0:ncols]),
              reads=[b_stat[r_i]], writes=[b_stat[r_i]])

    with ExitStack() as es:
        A0bc = sb(es, "A0bc", [128, 3, D], F32); b_A0 = Buf()
        B0bc = sb(es, "B0bc", [128, 3, D], F32); b_B0 = Buf()
        with ExitStack() as es0:
            cgt = sb(es0, "cgt", [17, D], F32); b_cgt = Buf()
            scT = sb(es0, "scT", [128, 16, 17], BF16); b_scT = Buf()
            modrow = sb(es0, "modrow", [17, 6144], F32); b_mod = Buf()
            bbp = Pool(es0, nc, "bbc", [17, 512], F32, 2)
            g0bc = sb(es0, "g0bc", [17, D], F32); b_g0 = Buf()
            gslbc = sb(es0, "gslbc", [17, 3, 512], F32); b_gsl = Buf()
            rows = sb(es0, "rows", [17, 4, 512], F32); b_rows = Buf()
            A0row = sb(es0, "A0row", [17, D], F32); b_A0r = Buf()
            wap = Pool(es0, nc, "wap", [128, 16, 512], BF16, 2)

            fw.dma(sp, cgt[:], cg[:, :], writes=[b_cgt])
            fw.dma(sp, g0bc[:], g_pre0.partition_broadcast(17), writes=[b_g0])
            for i in range(3):
                fw.dma(sp, gslbc[:, i, :], g_sl[i, :].partition_broadcast(17), writes=[b_gsl])
            fw.op(act, lambda e: e.activation(out=cgt[:], in_=cgt[:], func=AF.Silu), reads=[b_cgt], writes=[b_cgt])
            pt, pb = ps_next()
            for kt in range(16):
                fw.op(pe, lambda e: e.transpose(pt[:, kt * 17:(kt + 1) * 17], cgt[0:17, kt * 128:(kt + 1) * 128],
                                                ident_f[0:17, 0:17]),
                      reads=[b_cgt, b_idf], writes=[pb])
            fw.op(act, lambda e: e.activation(out=scT[:].rearrange("p a b -> p (a b)"), in_=pt[:, 0:272], func=AF.Copy),
                  reads=[pb], writes=[b_scT])
            for cc in range(12):
                wt, wb = wap.next()
                src = (w_ada0 if cc < 8 else w_adas).rearrange("(kt p) c -> p kt c", p=128)
                c0 = (cc % 8) * 512 if cc < 8 else (cc - 8) * 512
                fw.dma(pool, wt[:], src[:, :, c0:c0 + 512], writes=[wb])
                bbc, b_bbc = bbp.next()
                fw.dma(sp, bbc[:], (b_ada0 if cc < 8 else b_adas)[c0:c0 + 512].partition_broadcast(17), writes=[b_bbc])
                pt, pb = ps_next()
                for kt in range(16):
                    fw.op(pe, lambda e: e.matmul(pt[0:17, :], lhsT=scT[:, kt, :], rhs=wt[:, kt, :],
                                                 start=(kt == 0), stop=(kt == 15)),
                          reads=[b_scT, wb], writes=[pb])
                fw.op(dve, lambda e: e.tensor_tensor(out=modrow[:, cc * 512:(cc + 1) * 512], in0=pt[0:17, :],
                                                     in1=bbc[:, :], op=ALU.add),
                      reads=[pb, b_bbc], writes=[b_mod])
            fw.op(dve, lambda e: e.scalar_tensor_tensor(out=A0row[:], in0=modrow[:, 2048:4096], scalar=1.0, in1=g0bc[:],
                                                        op0=ALU.add, op1=ALU.mult),
                  reads=[b_mod, b_g0], writes=[b_A0r])
            fw.op(dve, lambda e: e.tensor_tensor(out=rows[:, 0, :], in0=modrow[:, 4096:4608], in1=gslbc[:, 0, :], op=ALU.mult),
                  reads=[b_mod, b_gsl], writes=[b_rows])
            fw.op(dve, lambda e: e.scalar_tensor_tensor(out=rows[:, 1, :], in0=modrow[:, 5120:5632], scalar=1.0,
                                                        in1=gslbc[:, 1, :], op0=ALU.add, op1=ALU.mult),
                  reads=[b_mod, b_gsl], writes=[b_rows])
            fw.op(dve, lambda e: e.tensor_tensor(out=rows[:, 2, :], in0=modrow[:, 5632:6144], in1=gslbc[:, 2, :], op=ALU.mult),
                  reads=[b_mod, b_gsl], writes=[b_rows])
            jobs = []
            for rg in range(3):
                for c4 in range(4):
                    jobs.append((A0row[:, c4 * 512:(c4 + 1) * 512], b_A0r, A0bc[:, rg, c4 * 512:(c4 + 1) * 512], b_A0, rg))
                    jobs.append((modrow[:, c4 * 512:(c4 + 1) * 512], b_mod, B0bc[:, rg, c4 * 512:(c4 + 1) * 512], b_B0, rg))
                jobs.append((rows[:, 0, :], b_rows, GG0bc[:, rg, :], b_GG0, rg))
                jobs.append((rows[:, 1, :], b_rows, A1bc[:, rg, :], b_A1, rg))
                jobs.append((modrow[:, 4608:5120], b_mod, B1bc[:, rg, :], b_B1, rg))
                jobs.append((rows[:, 2, :], b_rows, GG1bc[:, rg, :], b_GG1, rg))
            for ji, (src_ap, src_b, dst_ap, dst_b, rg) in enumerate(jobs):
                pt, pb = ps_next()
                fw.op(pe, lambda e: e.matmul(pt[:, :], lhsT=sel[:, rg, :], rhs=src_ap, start=True, stop=True),
                      reads=[b_sel, src_b], writes=[pb])
                E = act if ji % 2 == 0 else dve
                if E is act:
                    fw.op(act, lambda e: e.activation(out=dst_ap, in_=pt[:, :], func=AF.Copy), reads=[pb], writes=[dst_b])
                else:
                    fw.op(dve, lambda e: e.tensor_copy(out=dst_ap, in_=pt[:, :]), reads=[pb], writes=[dst_b])
            fw.barrier()

        with ExitStack() as es1:
            xtp = Pool(es1, nc, "xt", [128, D], F32, 5)
            junk = sb(es1, "junk", [128, D], BF16); b_junk = Buf()
            hnp = Pool(es1, nc, "hn", [128, D], BF16, 4)
            stg = Pool(es1, nc, "hstg", [128, 16, 512], BF16, 3)
            tmpp = Pool(es1, nc, "tmp1", [128, 4], F32, 6)
            for tb in range(9):
                n128 = 4 if tb < 8 else 2
                st_t, st_b = stg.next()
                for j in range(n128):
                    tile = tb * 4 + j
                    rg = rg_of(tile)
                    xt, xb_ = xtp.next()
                    tmp1, b_tmp1 = tmpp.next()
                    fw.dma(sp, xt[:], xg[tile * 128:(tile + 1) * 128, :], writes=[xb_])
                    fw.op(act, lambda e: e.activation(out=junk[:], in_=xt[:], func=AF.Square, accum_out=tmp1[:, 0:1]),
                          reads=[xb_], writes=[b_junk, b_tmp1])
                    fw.op(dve, lambda e: e.tensor_scalar(out=tmp1[:, 1:2], in0=tmp1[:, 0:1], scalar1=1.0 / D, scalar2=EPS,
                                                         op0=ALU.mult, op1=ALU.add), reads=[b_tmp1], writes=[b_tmp1])
                    fw.op(act, lambda e: e.activation(out=tmp1[:, 2:3], in_=tmp1[:, 1:2], func=AF.Sqrt),
                          reads=[b_tmp1], writes=[b_tmp1])
                    fw.op(dve, lambda e: e.reciprocal(out=tmp1[:, 3:4], in_=tmp1[:, 2:3]), reads=[b_tmp1], writes=[b_tmp1])
                    fw.op(dve, lambda e: e.scalar_tensor_tensor(out=xt[:], in0=xt[:], scalar=tmp1[:, 3:4], in1=A0bc[:, rg, :],
                                                                op0=ALU.mult, op1=ALU.mult),
                          reads=[xb_, b_tmp1, b_A0], writes=[xb_])
                    hn, hb = hnp.next()
                    fw.op(pool, lambda e: e.tensor_tensor(out=hn[:], in0=xt[:], in1=B0bc[:, rg, :], op=ALU.add),
                          reads=[xb_, b_B0], writes=[hb])
                    for half in range(2):
                        pt, pb = ps_next()
                        ptb = pt.bitcast(BF16)
                        for k8 in range(8):
                            kt = half * 8 + k8
                            fw.op(pe, lambda e: e.transpose(ptb[:, k8 * 128:(k8 + 1) * 128], hn[:, kt * 128:(kt + 1) * 128],
                                                            ident_b[:]),
                                  reads=[hb, b_idb], writes=[pb])
                        dst = st_t[:, half * 8:(half + 1) * 8, j * 128:(j + 1) * 128]
                        srcp = ptb[:, :].rearrange("p (a b) -> p a b", a=8)
                        if half == 0:
                            fw.op(act, lambda e: e.activation(out=dst, in_=srcp, func=AF.Copy), reads=[pb], writes=[st_b])
                        else:
                            fw.op(dve, lambda e: e.tensor_copy(out=dst, in_=srcp), reads=[pb], writes=[st_b])
                n = n128 * 128
                fw.dma(pool, hn0_scr.rearrange("(kt p) t -> p kt t", p=128)[:, :, tb * 512:tb * 512 + n], st_t[:, :, 0:n],
                       reads=[st_b], writes=[d_hn0])
            fw.barrier()
    if stages <= 1:
        fw.barrier()
        return nc

    with ExitStack() as es:
        wA = sb(es, "wA", [128, 16, 2048], BF16); b_wA = Buf()
        wR = sb(es, "wR", [128, 8, 256], BF16); b_wR = Buf()
        wI = sb(es, "wI", [128, 8, 256], BF16); b_wI = Buf()
        cw = sb(es, "cw", [128, 4, 8], F32); b_cw = Buf()
        cp = sb(es, "cp", [128, 4, 8], F32); b_cp = Buf()
        negc = sb(es, "negc", [128, 2, 8], F32); b_negc = Buf()
        strow = sb(es, "strow", [64, 1024], F32); b_strow = Buf()
        stT = sb(es, "stT", [128, 8, 64], F32); b_stT = Buf()
        xe = sb(es, "xe", [128, 8, 515], F32); b_xe = [Buf() for _ in range(8)]
        b_xes = b_xe

        def xesv(ct_):
            return xe[:, ct_, 0:304].rearrange("p (s j) -> p s j", j=19)
        hc = sb(es, "hc", [128, 8], F32); b_hc = [Buf() for _ in range(8)]
        hfin = sb(es, "hfin", [128, 8, 17], F32); b_hfin = Buf()
        xlast = sb(es, "xlast", [128, 8, 17, 3], F32); b_xlast = Buf()
        hnT = Pool(es, nc, "hnT", [128, 16, 512], BF16, 2)
        xcp = Pool(es, nc, "xc", [128, 512], F32, 4)
        xcbp = Pool(es, nc, "xcb", [128, 512], BF16, 4)
        szp = Pool(es, nc, "sz", [128, 512], F32, 4)
        rp = Pool(es, nc, "r", [128, 512], F32, 2)
        ip = Pool(es, nc, "ig", [128, 512], F32, 2)
        ap_ = Pool(es, nc, "a", [128, 512], F32, 2)
        a2p = Pool(es, nc, "a2", [128, 512], F32, 2)
        up = Pool(es, nc, "u", [128, 512], F32, 2)
        hp = Pool(es, nc, "h", [128, 512], F32, 2)
        yp = Pool(es, nc, "y", [128, 512], BF16, 3)

        for kq in range(4):
            fw.dma(pool, wA[:, kq * 4:(kq + 1) * 4, :],
                   w_in_a.rearrange("(kt p) c -> p kt c", p=128)[:, kq * 4:(kq + 1) * 4, :], writes=[b_wA])
        fw.dma(pool, wR[:], w_rg.rearrange("b (kh p) j -> p (b kh) j", p=128), writes=[b_wR])
        fw.dma(pool, wI[:], w_ig.rearrange("b (kh p) j -> p (b kh) j", p=128), writes=[b_wI])
        with nc.allow_non_contiguous_dma(reason="tiny per-channel params"):
            fw.dma(sp, cw[:], conv_w.rearrange("i (ct p) -> p i ct", p=128), writes=[b_cw])
            fw.dma(sp, cp[:], chp.rearrange("i (ct p) -> p i ct", p=128), writes=[b_cp])
        fw.dma(sp, strow[:], st_in[:, :], writes=[b_strow])
        fw.op(act, lambda e: e.activation(out=negc[:, 0, :], in_=cp[:, 3, :], func=AF.Exp, scale=-1.0), reads=[b_cp], writes=[b_negc])
        fw.op(act, lambda e: e.activation(out=negc[:, 0, :], in_=negc[:, 0, :], func=AF.Ln, bias=1.0), reads=[b_negc], writes=[b_negc])
        fw.op(dve, lambda e: e.tensor_scalar(out=negc[:, 1, :], in0=negc[:, 0, :], scalar1=-16.0, scalar2=None, op0=ALU.mult),
              reads=[b_negc], writes=[b_negc])
        fw.op(dve, lambda e: e.tensor_scalar(out=negc[:, 0, :], in0=negc[:, 0, :], scalar1=-8.0, scalar2=None, op0=ALU.mult),
              reads=[b_negc], writes=[b_negc])
        pt, pb = ps_next()
        for ct in range(8):
            fw.op(pe, lambda e: e.transpose(pt[:, ct * 64:(ct + 1) * 64], strow[0:64, ct * 128:(ct + 1) * 128], ident_f[0:64, 0:64]),
                  reads=[b_strow, b_idf], writes=[pb])
        fw.op(act, lambda e: e.activation(out=stT[:].rearrange("p a b -> p (a b)"), in_=pt[:, :], func=AF.Copy), reads=[pb], writes=[b_stT])
        for ct in range(8):
            fw.op(pool, lambda e: e.memset(xe[:, ct, 0:3], 0.0), writes=[b_xe[ct]])
            fw.op(pool, lambda e: e.memset(hc[:, ct:ct + 1], 0.0), writes=[b_hc[ct]])

        for tb in range(9):
            samp = tb == 8
            n = 256 if samp else 512
            ht, hb_ = hnT.next()
            fw.dma(sp, ht[:, :, 0:n], hn0_scr.rearrange("(kt p) t -> p kt t", p=128)[:, :, tb * 512:tb * 512 + n],
                   reads=[d_hn0], writes=[hb_])
            if samp:
                for ct in range(8):
                    fw.op(pool, lambda e: e.tensor_copy(out=xesv(ct)[:, :, 0:3],
                                                        in_=stT[:, ct, 16:64].rearrange("p (s i) -> p s i", i=3)),
                          reads=[b_stT], writes=[b_xes[ct]])
            for blk in range(4):
                xcs, xcbs, szs = [], [], []
                for c2 in range(2):
                    ct = blk * 2 + c2
                    pt, pb = ps_next()
                    for kt in range(16):
                        fw.op(pe, lambda e: e.matmul(pt[:, 0:n], lhsT=wA[:, kt, ct * 128:(ct + 1) * 128], rhs=ht[:, kt, 0:n],
                                                     start=(kt == 0), stop=(kt == 15)), reads=[b_wA, hb_], writes=[pb])
                    if samp:
                        xv = xesv(ct)[:, :, 3:19]
                        bx = b_xes[ct]
                        fw.op(act, lambda e: e.activation(out=xv, in_=pt[:, 0:n].rearrange("p (s t) -> p s t", t=16), func=AF.Copy),
                              reads=[pb], writes=[bx])
                    else:
                        bx = b_xe[ct]
                        fw.op(act, lambda e: e.activation(out=xe[:, ct, 3:515], in_=pt[:, 0:n], func=AF.Copy), reads=[pb], writes=[bx])
                    pt2, pb2 = ps_next()
                    for kt in range(16):
                        fw.op(pe, lambda e: e.matmul(pt2[:, 0:n], lhsT=wA[:, kt, 1024 + ct * 128:1024 + (ct + 1) * 128],
                                                     rhs=ht[:, kt, 0:n], start=(kt == 0), stop=(kt == 15)),
                              reads=[b_wA, hb_], writes=[pb2])
                    sz, szb = szp.next()
                    fw.op(act, lambda e: e.activation(out=sz[:, 0:n], in_=pt2[:, 0:n], func=AF.Silu), reads=[pb2], writes=[szb])
                    szs.append((sz, szb))
                    xc, xcb_ = xcp.next()

                    def xin(i):
                        if samp:
                            return xesv(ct)[:, :, i:i + 16]
                        return xe[:, ct, i:i + 512]

                    def xo(t_):
                        if samp:
                            return t_[:, 0:n].rearrange("p (s t) -> p s t", t=16)
                        return t_[:, 0:n]
                    fw.op(dve, lambda e: e.tensor_scalar(out=xo(xc), in0=xin(3), scalar1=cw[:, 3, ct:ct + 1], scalar2=cp[:, 0, ct:ct + 1],
                                                         op0=ALU.mult, op1=ALU.add), reads=[bx, b_cw, b_cp], writes=[xcb_])
                    for i in (2, 1, 0):
                        fw.op(dve, lambda e: e.scalar_tensor_tensor(out=xo(xc), in0=xin(i), scalar=cw[:, i, ct:ct + 1], in1=xo(xc),
                                                                    op0=ALU.mult, op1=ALU.add), reads=[bx, b_cw, xcb_], writes=[xcb_])
                    xcb, xcbb = xcbp.next()
                    fw.op(pool, lambda e: e.tensor_copy(out=xcb[:, 0:n], in_=xc[:, 0:n]), reads=[xcb_], writes=[xcbb])
                    if samp:
                        fw.op(pool, lambda e: e.tensor_copy(out=xlast[:, ct, 1:17, :], in_=xesv(ct)[:, :, 16:19]),
                              reads=[bx], writes=[b_xlast])
                    else:
                        if tb == 7:
                            fw.op(pool, lambda e: e.tensor_copy(out=xlast[:, ct, 0, :], in_=xe[:, ct, 512:515]),
                                  reads=[bx], writes=[b_xlast])
                        fw.op(pool, lambda e: e.tensor_copy(out=xe[:, ct, 0:3], in_=xe[:, ct, 512:515]), reads=[bx], writes=[bx])
                    xcs.append((xc, xcb_))
                    xcbs.append((xcb, xcbb))
                for c2 in range(2):
                    ct = blk * 2 + c2
                    xc, xcb_ = xcs[c2]
                    sz, szb = szs[c2]
                    r, rb = rp.next()
                    ig, ib = ip.next()
                    for (wG, bwG, bias_i, dst, dstb) in ((wR, b_wR, 1, r, rb), (wI, b_wI, 2, ig, ib)):
                        pt, pb = ps_next()
                        for kh in range(2):
                            fw.op(pe, lambda e: e.matmul(pt[:, 0:n], lhsT=wG[:, blk * 2 + kh, c2 * 128:(c2 + 1) * 128],
                                                         rhs=xcbs[kh][0][:, 0:n], start=(kh == 0), stop=(kh == 1)),
                                  reads=[bwG, xcbs[kh][1]], writes=[pb])
                        fw.op(act, lambda e: e.activation(out=dst[:, 0:n], in_=pt[:, 0:n], func=AF.Sigmoid,
                                                          bias=cp[:, bias_i, ct:ct + 1]), reads=[pb, b_cp], writes=[dstb])
                    a, ab = ap_.next()
                    a2, a2b = a2p.next()
                    fw.op(act, lambda e: e.activation(out=a[:, 0:n], in_=r[:, 0:n], func=AF.Exp, scale=negc[:, 0, ct:ct + 1]),
                          reads=[rb, b_negc], writes=[ab])
                    fw.op(act, lambda e: e.activation(out=a2[:, 0:n], in_=r[:, 0:n], func=AF.Exp, scale=negc[:, 1, ct:ct + 1]),
                          reads=[rb, b_negc], writes=[a2b])
                    fw.op(act, lambda e: e.activation(out=a2[:, 0:n], in_=a2[:, 0:n], func=AF.Sqrt, scale=-1.0, bias=1.0),
                          reads=[a2b], writes=[a2b])
                    u, ub = up.next()
                    fw.op(pool, lambda e: e.tensor_tensor(out=u[:, 0:n], in0=ig[:, 0:n], in1=xc[:, 0:n], op=ALU.mult),
                          reads=[ib, xcb_], writes=[ub])
                    fw.op(dve, lambda e: e.tensor_tensor(out=u[:, 0:n], in0=u[:, 0:n], in1=a2[:, 0:n], op=ALU.mult),
                          reads=[ub, a2b], writes=[ub])
                    h, hb2 = hp.next()
                    if samp:
                        for s in range(16):
                            fw.op(dve, lambda e: e.tensor_tensor_scan(out=h[:, s * 16:(s + 1) * 16], data0=a[:, s * 16:(s + 1) * 16],
                                                                      data1=u[:, s * 16:(s + 1) * 16], initial=stT[:, ct, s:s + 1],
                                                                      op0=ALU.mult, op1=ALU.add),
                                  reads=[ab, ub, b_stT], writes=[hb2])
                        fw.op(pool, lambda e: e.tensor_copy(out=hfin[:, ct, 1:17],
                                                            in_=h[:, 0:n].rearrange("p (s t) -> p s t", t=16)[:, :, 15]),
                              reads=[hb2], writes=[b_hfin])
                    else:
                        fw.op(dve, lambda e: e.tensor_tensor_scan(out=h[:, 0:n], data0=a[:, 0:n], data1=u[:, 0:n],
                                                                  initial=hc[:, ct:ct + 1], op0=ALU.mult, op1=ALU.add),
                              reads=[ab, ub, b_hc[ct]], writes=[hb2])
                        fw.op(pool, lambda e: e.tensor_copy(out=hc[:, ct:ct + 1], in_=h[:, n - 1:n]), reads=[hb2], writes=[b_hc[ct]])
                        if tb == 7:
                            fw.op(pool, lambda e: e.tensor_copy(out=hfin[:, ct, 0:1], in_=h[:, n - 1:n]), reads=[hb2], writes=[b_hfin])
                    y, yb = yp.next()
                    fw.op(dve, lambda e: e.tensor_tensor(out=y[:, 0:n], in0=h[:, 0:n], in1=sz[:, 0:n], op=ALU.mult),
                          reads=[hb2, szb], writes=[yb])
                    fw.dma(pool, y0_src[tb][ct * 128:(ct + 1) * 128, :], y[:, 0:n], reads=[yb], writes=[d_y0s[tb]])
            fw.collective("AllGather", y0_src[tb][:, :], y0_all[tb][:, :], reads=[d_y0s[tb]], writes=[d_y0a[tb]])
        so = strow; b_so = b_strow
        for (srcT, bsrc, ncols, r0) in ((hfin, b_hfin, 17, 0), (xlast, b_xlast, 51, 17)):
            for half in range(2):
                pt, pb = ps_next()
                for c4 in range(4):
                    ct = half * 4 + c4
                    s_ap = srcT[:, ct, :] if ncols == 17 else srcT[:, ct, :, :].rearrange("p s i -> p (s i)")
                    fw.op(pe, lambda e: e.transpose(pt[0:ncols, c4 * 128:(c4 + 1) * 128], s_ap, ident_f[:]),
                          reads=[bsrc, b_idf], writes=[pb])
                stage = so[0:ncols, half * 512:(half + 1) * 512]
                fw.op(act, lambda e: e.activation(out=stage, in_=pt[0:ncols, :], func=AF.Copy), reads=[pb], writes=[b_so])
            if ncols == 17:
                fw.dma(pool, lru_sl[:, :], so[0:17, :], reads=[b_so], writes=[d_out[3]])
            else:
                fw.dma(pool, conv_sl[:, :], so[0:51, :], reads=[b_so], writes=[d_out[4]])
        fw.barrier()
    if stages <= 2:
        fw.barrier()
        return nc

    def exchange_stats(es_, i_ss, i_r, which):
        sg = sb(es_, f"sg{which}", [128, 4, NTT], F32); b_sg = Buf()
        fw.dma(pool, st_src[which][:, :], stat[:, i_ss, :], reads=[b_stat[i_ss]], writes=[d_sts[which]])
        fw.collective("AllGather", st_src[which][:, :], st_all[which][:, :], reads=[d_sts[which]], writes=[d_sta[which]])
        fw.dma(sp, sg[:], st_all[which].rearrange("(r p) t -> p r t", p=128), reads=[d_sta[which]], writes=[b_sg])
        fw.op(dve, lambda e: e.tensor_tensor(out=stat[:, i_ss, :], in0=sg[:, 0, :], in1=sg[:, 1, :], op=ALU.add),
              reads=[b_sg], writes=[b_stat[i_ss]])
        for r_ in (2, 3):
            fw.op(dve, lambda e: e.tensor_tensor(out=stat[:, i_ss, :], in0=stat[:, i_ss, :], in1=sg[:, r_, :], op=ALU.add),
                  reads=[b_sg, b_stat[i_ss]], writes=[b_stat[i_ss]])
        rstd_from(i_ss, i_r)

    with ExitStack() as esR:
        R1 = sb(esR, "R1", [128, NTT, 512], F32); b_R1 = [Buf() for _ in range(NTT)]
        with ExitStack() as es:
            wO = sb(es, "wO", [128, 32, 512], BF16); b_wO = Buf()
            y0p = Pool(es, nc, "y0t", [128, 32, 512], BF16, 2)
            junk3 = sb(es, "junk3", [128, 512], BF16); b_j3 = Buf()
            xsp = Pool(es, nc, "xsl", [128, 512], F32, 3)
            for kq in range(4):
                fw.dma(pool, wO[:, kq * 8:(kq + 1) * 8, :],
                       w_out_a.rearrange("(kt p) f -> p kt f", p=128)[:, kq * 8:(kq + 1) * 8, :], writes=[b_wO])
            for tb in range(9):
                n128 = 4 if tb < 8 else 2
                n = n128 * 128
                yt, ytb = y0p.next()
                fw.dma(sp, yt[:, :, 0:n], y0_all[tb].rearrange("(kt p) t -> p kt t", p=128),
                       reads=[d_y0a[tb]], writes=[ytb])
                for j in range(n128):
                    tile = tb * 4 + j
                    pt, pb = ps_next()
                    for kt in range(32):
                        fw.op(pe, lambda e: e.matmul(pt[:, :], lhsT=yt[:, kt, j * 128:(j + 1) * 128], rhs=wO[:, kt, :],
                                                     start=(kt == 0), stop=(kt == 31)), reads=[ytb, b_wO], writes=[pb])
                    fw.op(act, lambda e: e.activation(out=R1[:, tile, :], in_=pt[:, :], func=AF.Copy), reads=[pb], writes=[b_R1[tile]])
                    fw.op(act, lambda e: e.activation(out=junk3[:], in_=pt[:, :], func=AF.Square,
                                                      accum_out=stat[:, 2, tile:tile + 1]),
                          reads=[pb], writes=[b_j3, b_stat[2]])
            if stages == 3:
                fw.barrier()
                return nc
            exchange_stats(es, 2, 3, 0)
            if stages == 3.5:
                fw.barrier()
                return nc
            for tile in range(NTT):
                rg = rg_of(tile)
                xs, xsb = xsp.next()
                fw.dma(sp, xs[:], xg_sl[tile * 128:(tile + 1) * 128, :], writes=[xsb])
                fw.op(dve, lambda e: e.scalar_tensor_tensor(out=R1[:, tile, :], in0=R1[:, tile, :], scalar=stat[:, 3, tile:tile + 1],
                                                            in1=GG0bc[:, rg, :], op0=ALU.mult, op1=ALU.mult),
                      reads=[b_R1[tile], b_stat[3], b_GG0], writes=[b_R1[tile]])
                fw.op(pool, lambda e: e.tensor_tensor(out=R1[:, tile, :], in0=R1[:, tile, :], in1=xs[:], op=ALU.add),
                      reads=[b_R1[tile], xsb], writes=[b_R1[tile]])
                fw.op(act, lambda e: e.activation(out=junk3[:], in_=R1[:, tile, :], func=AF.Square,
                                                  accum_out=stat[:, 4, tile:tile + 1]),
                      reads=[b_R1[tile]], writes=[b_j3, b_stat[4]])
                fw.dma(pool, x1_scr[tile * 128:(tile + 1) * 128, :], R1[:, tile, :], reads=[b_R1[tile]], writes=[d_x1])
            exchange_stats(es, 4, 5, 1)
            fw.barrier()
        with ExitStack() as es:
            t4p = Pool(es, nc, "t4", [128, 512], F32, 2)
            h1p = Pool(es, nc, "h1", [128, 512], BF16, 2)
            hkp = Pool(es, nc, "hk", [128, 512], BF16, 2)
            stg4 = Pool(es, nc, "stg4", [128, 8, 512], BF16, 2)
            for tb in range(9):
                n128 = 4 if tb < 8 else 2
                n = n128 * 128
                st_t, st_b = stg4.next()
                for j in range(n128):
                    tile = tb * 4 + j
                    rg = rg_of(tile)
                    t4, t4b = t4p.next()
                    fw.op(dve, lambda e: e.scalar_tensor_tensor(out=t4[:], in0=R1[:, tile, :], scalar=stat[:, 5, tile:tile + 1],
                                                                in1=A1bc[:, rg, :], op0=ALU.mult, op1=ALU.mult),
                          reads=[b_R1[tile], b_stat[5], b_A1], writes=[t4b])
                    h1, h1b = h1p.next()
                    fw.op(pool, lambda e: e.tensor_tensor(out=h1[:], in0=t4[:], in1=B1bc[:, rg, :], op=ALU.add),
                          reads=[t4b, b_B1], writes=[h1b])
                    hk, hkb = hkp.next()
                    fw.op(dve, lambda e: e.scalar_tensor_tensor(out=hk[:], in0=R1[:, tile, :], scalar=stat[:, 5, tile:tile + 1],
                                                                in1=GKVbc[:, :], op0=ALU.mult, op1=ALU.mult),
                          reads=[b_R1[tile], b_stat[5], b_GKV], writes=[hkb])
                    pt, pb = ps_next()
                    ptb = pt.bitcast(BF16)
                    for f4 in range(4):
                        fw.op(pe, lambda e: e.transpose(ptb[:, f4 * 128:(f4 + 1) * 128], hk[:, f4 * 128:(f4 + 1) * 128], ident_b[:]),
                              reads=[hkb, b_idb], writes=[pb])
                    for f4 in range(4):
                        fw.op(pe, lambda e: e.transpose(ptb[:, (4 + f4) * 128:(5 + f4) * 128], h1[:, f4 * 128:(f4 + 1) * 128], ident_b[:]),
                              reads=[h1b, b_idb], writes=[pb])
                    fw.op(act, lambda e: e.activation(out=st_t[:, :, j * 128:(j + 1) * 128],
                                                      in_=ptb[:, :].rearrange("p (a b) -> p a b", a=8), func=AF.Copy),
                          reads=[pb], writes=[st_b])
                fw.dma(pool, hn_src[tb].rearrange("(k p) t -> p k t", p=128), st_t[:, :, 0:n],
                       reads=[st_b], writes=[d_hns[tb]])
                fw.collective("AllGather", hn_src[tb][:, :], hn_all[tb][:, :], reads=[d_hns[tb]], writes=[d_hna[tb]])
            fw.barrier()
    if stages <= 4:
        fw.barrier()
        return nc
    lvl = 9 if stages >= 5 else int(round((stages - 4) * 10))

    with ExitStack() as es:
        wQ = sb(es, "wQ", [128, 16, 1024], BF16); b_wQ = Buf()
        wKV = sb(es, "wKV", [128, 16, 1024], BF16); b_wKV = Buf()
        h1T = Pool(es, nc, "h1T", [128, 16, 512], BF16, 2)
        hkT = Pool(es, nc, "hkT", [128, 16, 512], BF16, 2)
        qst = Pool(es, nc, "qst", [128, 512], BF16, 3)
        kvf = Pool(es, nc, "kvf", [128, 512], F32, 3)
        kvb = Pool(es, nc, "kvb", [128, 512], BF16, 3)
        kts = Pool(es, nc, "kts", [128, 4, 128], BF16, 2)
        for kq in range(4):
            fw.dma(pool, wQ[:, kq * 4:(kq + 1) * 4, :], w_in_b.rearrange("(kt p) c -> p kt c", p=128)[:, kq * 4:(kq + 1) * 4, :], writes=[b_wQ])
            fw.dma(pool, wKV[:, kq * 4:(kq + 1) * 4, :], w_kv.rearrange("(kt p) c -> p kt c", p=128)[:, kq * 4:(kq + 1) * 4, :], writes=[b_wKV])
        for tb in range(9):
            n128 = 4 if tb < 8 else 2
            n = n128 * 128
            c0 = tb * 512
            a1, a1b = h1T.next()
            ak, akb = hkT.next()
            hview = hn_all[tb].rearrange("(r k f p) t -> k p r f t", r=4, k=2, f=4, p=128)
            for r_ in range(4):
                fw.dma(sp, ak[:, r_ * 4:(r_ + 1) * 4, 0:n], hview[0][:, r_, :, :], reads=[d_hna[tb]], writes=[akb])
                fw.dma(sp, a1[:, r_ * 4:(r_ + 1) * 4, 0:n], hview[1][:, r_, :, :], reads=[d_hna[tb]], writes=[a1b])
            for hh in range(4):
                for which in range(2):
                    pt, pb = ps_next()
                    for kt in range(16):
                        fw.op(pe, lambda e: e.matmul(pt[:, 0:n], lhsT=wQ[:, kt, which * 512 + hh * 128:which * 512 + (hh + 1) * 128],
                                                     rhs=a1[:, kt, 0:n], start=(kt == 0), stop=(kt == 15)),
                              reads=[b_wQ, a1b], writes=[pb])
                    st, stb = qst.next()
                    fw.op(act, lambda e: e.activation(out=st[:, 0:n], in_=pt[:, 0:n], func=(AF.Copy if which == 0 else AF.Silu)),
                          reads=[pb], writes=[stb])
                    if lvl < 4:
                        continue
                    if which == 0:
                        fw.dma(pool, qT_scr[hh, :, c0:c0 + n], st[:, 0:n], reads=[stb], writes=[d_qT])
                    else:
                        fw.dma(pool, sz_scr[hh, :, c0:c0 + n], st[:, 0:n], reads=[stb], writes=[d_sz])
            for j in range(n128 if lvl >= 6 else 0):
                tile = tb * 4 + j
                for half in range(2):
                    pt, pb = ps_next()
                    for kt in range(16):
                        fw.op(pe, lambda e: e.matmul(pt[:, :], lhsT=ak[:, kt, j * 128:(j + 1) * 128], rhs=wKV[:, kt, half * 512:(half + 1) * 512],
                                                     start=(kt == 0), stop=(kt == 15)), reads=[b_wKV, akb], writes=[pb])
                    f_, fb = kvf.next()
                    fw.op(act, lambda e: e.activation(out=f_[:], in_=pt[:, :], func=AF.Copy), reads=[pb], writes=[fb])
                    fw.dma(pool, (k_sl if half == 0 else v_sl)[tile * 128:(tile + 1) * 128, :], f_[:], reads=[fb],
                           writes=[d_out[1 + half]])
                    b_, bb = kvb.next()
                    fw.op(dve, lambda e: e.tensor_copy(out=b_[:], in_=pt[:, :]), reads=[pb], writes=[bb])
                    if half == 1:
                        fw.dma(pool, v_scr[tile * 128:(tile + 1) * 128, :], b_[:], reads=[bb], writes=[d_v])
                    elif lvl >= 8:
                        pt2, pb2 = ps_next()
                        ptb = pt2.bitcast(BF16)
                        for hh in range(4):
                            fw.op(pe, lambda e: e.transpose(ptb[:, hh * 128:(hh + 1) * 128], b_[:, hh * 128:(hh + 1) * 128], ident_b[:]),
                                  reads=[bb, b_idb], writes=[pb2])
                        kt_, ktb = kts.next()
                        fw.op(act, lambda e: e.activation(out=kt_[:], in_=ptb[:, 0:512].rearrange("p (a b) -> p a b", a=4), func=AF.Copy),
                              reads=[pb2], writes=[ktb])
                        fw.dma(pool, kT_scr.rearrange("h p t -> p h t")[:, :, tile * 128:(tile + 1) * 128], kt_[:],
                               reads=[ktb], writes=[d_kT])
        fw.barrier()
    if stages <= 5:
        fw.barrier()
        return nc

    def attn_p1(E, Eb_, Fx, Fb_, S, spp, z_chunk, mask_fn):
        nch = (S + 511) // 512
        fw.op(pool, lambda e: e.memset(Fx[:, 0:1], 0.0), writes=[Fb_])
        for c in range(nch):
            s0 = c * 512
            w = min(512, S - s0)
            pt, pb = z_chunk(c, s0, w)
            fw.op(act, lambda e: e.activation(out=E[:, s0:s0 + w], in_=pt[:, 0:w], func=AF.Exp, scale=SCALE),
                  reads=[pb], writes=[Eb_])
            if c == nch - 1:
                mask_fn(E, Eb_)
            spc, spb = spp.next()
            fw.op(act, lambda e: e.activation(out=spc[:, 0:w], in_=E[:, s0:s0 + w], func=AF.Ln, bias=1.0),
                  reads=[Eb_], writes=[spb])
            fw.op(dve, lambda e: e.tensor_tensor_scan(out=Fx[:, s0 + 1:s0 + w + 1], data0=ones_c[:, 0:1].to_broadcast([128, w]),
                                                      data1=spc[:, 0:w], initial=Fx[:, s0:s0 + 1], op0=ALU.mult, op1=ALU.add),
                  reads=[spb, b_ones, Fb_], writes=[Fb_])

    def attn_p2(E, Eb_, Fx, Fb_, P, Pcb, PT, PTcb, S, spp, negt):
        nch = (S + 511) // 512
        nt, ntb = negt.next()
        fw.op(dve, lambda e: e.tensor_scalar(out=nt[:], in0=Fx[:, S:S + 1], scalar1=-1.0, scalar2=None, op0=ALU.mult),
              reads=[Fb_], writes=[ntb])
        for c in range(nch):
            s0 = c * 512
            w = min(512, S - s0)
            wc, wcb = spp.next()
            fw.op(act, lambda e: e.activation(out=wc[:, 0:w], in_=Fx[:, s0:s0 + w], func=AF.Exp, bias=nt[:, 0:1]),
                  reads=[Fb_, ntb], writes=[wcb])
            fw.op(pool, lambda e: e.tensor_tensor(out=P[:, s0:s0 + w], in0=E[:, s0:s0 + w], in1=wc[:, 0:w], op=ALU.mult),
                  reads=[Eb_, wcb], writes=[Pcb[c]])
            pt, pb = ps_next()
            ptb = pt.bitcast(BF16)
            nb = (w + 127) // 128
            for b_ in range(nb):
                kb = c * 4 + b_
                wk = min(128, S - kb * 128)
                fw.op(pe, lambda e: e.transpose(ptb[0:wk, b_ * 128:(b_ + 1) * 128], P[:, kb * 128:kb * 128 + wk], ident_b[:]),
                      reads=[Pcb[c], b_idb], writes=[pb])
            full = w // 128
            if full > 0:
                srcv = ptb[:, 0:full * 128].rearrange("p (a b) -> p a b", a=full)
                fw.op(dve, lambda e: e.tensor_copy(out=PT[:, c * 4:c * 4 + full, :], in_=srcv), reads=[pb], writes=[PTcb[c]])
            if full < nb:
                wk = w - full * 128
                fw.op(dve, lambda e: e.tensor_copy(out=PT[0:wk, c * 4 + full, :], in_=ptb[0:wk, full * 128:(full + 1) * 128]),
                      reads=[pb], writes=[PTcb[c]])

    with ExitStack() as es:
        qTh = sb(es, "qTh", [128, NT], BF16); b_qTh = Buf()
        kTh = sb(es, "kTh", [128, NT], BF16); b_kTh = Buf()
        szh = sb(es, "szh", [128, NT], BF16); b_szh = Buf()
        Vh = sb(es, "Vh", [128, NTT, 128], BF16); b_Vh = Buf()
        y1h = sb(es, "y1h", [128, NPT], BF16); b_y1h = Buf()
        Ep = Pool(es, nc, "E", [128, 4096], F32, 2)
        Fp = Pool(es, nc, "Fx", [128, 4100], F32, 2)
        spp = Pool(es, nc, "spc", [128, 512], F32, 6)
        Pp = Pool(es, nc, "P", [128, 4096], BF16, 2)
        PTp = Pool(es, nc, "PT", [128, 32, 128], BF16, 2)
        negt = Pool(es, nc, "negt", [128, 1], F32, 4)
        Pcbs = [[Buf() for _ in range(9)] for _ in range(2)]
        PTcbs = [[Buf() for _ in range(9)] for _ in range(2)]
        n_heads = 4 if stages >= 6 else 1
        for hh in range(n_heads):
            fw.dma(sp, qTh[:], qT_scr[hh, :, :], reads=[d_qT], writes=[b_qTh])
            fw.dma(sp, kTh[:], kT_scr[hh, :, :], reads=[d_kT], writes=[b_kTh])
            fw.dma(sp, szh[:], sz_scr[hh, :, :], reads=[d_sz], writes=[b_szh])
            fw.dma(sp, Vh[:], v_scr.rearrange("(b p) (h d) -> p b h d", p=128, d=128)[:, :, hh, :], reads=[d_v], writes=[b_Vh])
            def finish(pend):
                qb_, S_, E_, Eb__, Fx_, Fb__ = pend
                P, _ = Pp.next()
                PT, _ = PTp.next()
                Pcb = Pcbs[(Pp.i - 1) % 2]
                PTcb = PTcbs[(PTp.i - 1) % 2]
                attn_p2(E_, Eb__, Fx_, Fb__, P, Pcb, PT, PTcb, S_, spp, negt)
                po, pob = ps_next()
                for kb in range(qb_ + 1):
                    fw.op(pe, lambda e: e.matmul(po[:, 0:128], lhsT=Vh[:, kb, :], rhs=PT[:, kb, :], start=(kb == 0), stop=(kb == qb_)),
                          reads=[b_Vh, PTcb[kb // 4]], writes=[pob])
                fw.op(dve, lambda e: e.tensor_tensor(out=y1h[:, qb_ * 128:(qb_ + 1) * 128], in0=po[:, 0:128],
                                                     in1=szh[:, qb_ * 128:(qb_ + 1) * 128], op=ALU.mult),
                      reads=[pob, b_szh], writes=[b_y1h])

            pend = None
            for qb in range(32):
                S = 128 * (qb + 1)
                E, Eb_ = Ep.next()
                Fx, Fb_ = Fp.next()

                def z_chunk(c, s0, w):
                    pt, pb = ps_next()
                    fw.op(pe, lambda e: e.matmul(pt[:, 0:w], lhsT=qTh[:, qb * 128:(qb + 1) * 128], rhs=kTh[:, s0:s0 + w],
                                                 start=True, stop=True), reads=[b_qTh, b_kTh], writes=[pb])
                    return pt, pb

                def mask_fn(E_, Eb__):
                    fw.op(dve, lambda e: e.tensor_tensor(out=E_[:, S - 128:S], in0=E_[:, S - 128:S], in1=tri[:], op=ALU.mult),
                          reads=[Eb__, b_tri], writes=[Eb__])
                attn_p1(E, Eb_, Fx, Fb_, S, spp, z_chunk, mask_fn)
                if pend is not None:
                    finish(pend)
                pend = (qb, S, E, Eb_, Fx, Fb_)
            finish(pend)
            for tb in range(8):
                fw.dma(pool, y1_src[tb][hh * 128:(hh + 1) * 128, :], y1h[:, tb * 512:(tb + 1) * 512], reads=[b_y1h], writes=[d_y1s[tb]])
        fw.barrier()
    if stages < 6:
        fw.barrier()
        return nc
    for tb in range(8):
        fw.collective("AllGather", y1_src[tb][:, :], y1_all[tb][:, :], reads=[d_y1s[tb]], writes=[d_y1a[tb]])

    if stages >= 7:
        ps_banks[0] = list(range(6))
        with ExitStack() as es:
            qs = sb(es, "qs", [128, 4, 256], BF16); b_qs = Buf()
            ks = sb(es, "ks", [128, 4, 256], BF16); b_ks = Buf()
            szs = sb(es, "szs", [128, 4, 256], BF16); b_szs = Buf()
            vnew = sb(es, "vnew", [16, 16, 512], BF16); b_vnew = Buf()
            y1sT = sb(es, "y1sT", [128, 4, 256], BF16); b_y1sT = Buf()
            QP = sb(es, "QP", [128, 8, 128], BF16); b_QP = Buf()
            Vbp = Pool(es, nc, "Vbc", [128, 4, 2, 512], BF16, 3)
            zer = sb(es, "zer", [128, 128], BF16); b_zer = Buf()
            E = sb(es, "Es", [128, 4112], F32); Eb_ = Buf()
            Fx = sb(es, "Fxs", [128, 4116], F32); Fb_ = Buf()
            P = sb(es, "Ps", [128, 4112], BF16); Pcb = [Buf() for _ in range(9)]
            PT = sb(es, "PTs", [128, 33, 128], BF16); PTcb = [Buf() for _ in range(9)]
            spp = Pool(es, nc, "spcs", [128, 512], F32, 3)
            negt = Pool(es, nc, "negts", [128, 1], F32, 2)
            Kcp = Pool(es, nc, "Kc", [128, 4, 512], F32, 4)
            Kbp = Pool(es, nc, "Kb", [128, 4, 512], BF16, 3)
            Vcp = Pool(es, nc, "Vc", [128, 4, 512], F32, 3)
            KTp = Pool(es, nc, "KTc", [128, 512], BF16, 4)
            fw.dma(sp, qs[:], qT_scr.rearrange("h p t -> p h t")[:, :, NPT:NT], reads=[d_qT], writes=[b_qs])
            fw.dma(sp, ks[:], kT_scr.rearrange("h p t -> p h t")[:, :, NPT:NT], reads=[d_kT], writes=[b_ks])
            fw.dma(sp, szs[:], sz_scr.rearrange("h p t -> p h t")[:, :, NPT:NT], reads=[d_sz], writes=[b_szs])
            fw.dma(sp, vnew[:], v_scr[NPT:NT, :].rearrange("(s j) f -> j s f", j=16), reads=[d_v], writes=[b_vnew])
            fw.op(pool, lambda e: e.memset(QP[:], 0.0), writes=[b_QP])
            fw.op(pool, lambda e: e.memset(zer[:], 0.0), writes=[b_zer])
            zi = [0]
            for pg in range(8):
                for i in range(8):
                    sl, hh = i // 4, i % 4
                    sq = 2 * pg + sl
                    fw.op(pool, lambda e: e.tensor_copy(out=QP[:, i, 16 * i:16 * i + 16], in_=qs[:, hh, sq * 16:(sq + 1) * 16]),
                          reads=[b_qs], writes=[b_QP])

                def z_chunk(c, s0, w):
                    zt, zb = psb[6 + zi[0] % 2]
                    zi[0] += 1
                    if c < 8:
                        kcs = []
                        for sl in range(2):
                            kc, kcb = Kcp.next()
                            fw.dma(sp, kc[:], cache_k[2 * pg + sl, c * 512:(c + 1) * 512, :].rearrange("(b p) f -> p b f", p=128),
                                   writes=[kcb])
                            kb16, kb16b = Kbp.next()
                            fw.op(act, lambda e: e.activation(out=kb16[:], in_=kc[:], func=AF.Copy), reads=[kcb], writes=[kb16b])
                            kcs.append((kb16, kb16b))
                        for i in range(8):
                            sl, hh = i // 4, i % 4
                            kc, kcb = kcs[sl]
                            pt, pb = ps_next()
                            ptb = pt.bitcast(BF16)
                            for b in range(4):
                                fw.op(pe, lambda e: e.transpose(ptb[:, b * 128:(b + 1) * 128], kc[:, b, hh * 128:(hh + 1) * 128], ident_b[:]),
                                      reads=[kcb, b_idb], writes=[pb])
                            kt_, ktb = KTp.next()
                            if i % 4 == 0:
                                fw.op(act, lambda e: e.activation(out=kt_[:], in_=ptb[:, 0:512], func=AF.Copy), reads=[pb], writes=[ktb])
                            else:
                                fw.op(dve, lambda e: e.tensor_copy(out=kt_[:], in_=ptb[:, 0:512]), reads=[pb], writes=[ktb])
                            fw.op(pe, lambda e: e.matmul(zt[:, 0:512], lhsT=QP[:, i, :], rhs=kt_[:], start=(i == 0), stop=(i == 7)),
                                  reads=[b_QP, ktb], writes=[zb])
                    else:
                        for i in range(8):
                            sl, hh = i // 4, i % 4
                            sq = 2 * pg + sl
                            fw.op(pe, lambda e: e.matmul(zt[:, 0:16], lhsT=QP[:, i, :], rhs=ks[:, hh, sq * 16:(sq + 1) * 16],
                                                         start=(i == 0), stop=(i == 7)), reads=[b_QP, b_ks], writes=[zb])
                    return zt, zb

                def mask_fn(E_, Eb__):
                    fw.op(dve, lambda e: e.tensor_tensor(out=E_[:, 4096:4112], in0=E_[:, 4096:4112], in1=m16[:], op=ALU.mult),
                          reads=[Eb__, b_m16], writes=[Eb__])
                attn_p1(E, Eb_, Fx, Fb_, 4112, spp, z_chunk, mask_fn)
                attn_p2(E, Eb_, Fx, Fb_, P, Pcb, PT, PTcb, 4112, spp, negt)
                po, pob = ps_next()
                fw.op(pe, lambda e: e.matmul(po[:, 0:128], lhsT=zer[:], rhs=zer[:], start=True, stop=False, skip_group_check=True),
                      reads=[b_zer], writes=[pob])
                for c in range(8):
                    vb, vbb = Vbp.next()
                    for sl in range(2):
                        vc, vcb = Vcp.next()
                        fw.dma(sp, vc[:], cache_v[2 * pg + sl, c * 512:(c + 1) * 512, :].rearrange("(b p) f -> p b f", p=128),
                               writes=[vcb])
                        fw.op(act, lambda e: e.activation(out=vb[:, :, sl, :], in_=vc[:], func=AF.Copy), reads=[vcb], writes=[vbb])
                    for i in range(8):
                        sl, hh = i // 4, i % 4
                        for b in range(4):
                            kb = c * 4 + b
                            fw.op(pe, lambda e: e.matmul(po[:, i * 16:(i + 1) * 16], lhsT=vb[:, b, sl, hh * 128:(hh + 1) * 128],
                                                         rhs=PT[:, kb, i * 16:(i + 1) * 16], start=False, stop=False,
                                                         skip_group_check=True),
                                  reads=[vbb, PTcb[c]], writes=[pob])
                for i in range(8):
                    sl, hh = i // 4, i % 4
                    sq = 2 * pg + sl
                    fw.op(pe, lambda e: e.matmul(po[:, i * 16:(i + 1) * 16], lhsT=vnew[0:16, sq, hh * 128:(hh + 1) * 128],
                                                 rhs=PT[0:16, 32, i * 16:(i + 1) * 16], start=False, stop=True,
                                                 skip_group_check=True),
                          reads=[b_vnew, PTcb[8]], writes=[pob])
                for sl in range(2):
                    sq = 2 * pg + sl
                    fw.op(dve, lambda e: e.tensor_tensor(out=y1sT[:, :, sq * 16:(sq + 1) * 16],
                                                         in0=po[:, sl * 64:(sl + 1) * 64].rearrange("p (h t) -> p h t", h=4),
                                                         in1=szs[:, :, sq * 16:(sq + 1) * 16], op=ALU.mult),
                          reads=[pob, b_szs], writes=[b_y1sT])
            fw.dma(pool, y1_src[8].rearrange("(h p) t -> p h t", p=128), y1sT[:], reads=[b_y1sT], writes=[d_y1s[8]])
            fw.barrier()
        ps_banks[0] = list(range(8))

    fw.collective("AllGather", y1_src[8][:, :], y1_all[8][:, :], reads=[d_y1s[8]], writes=[d_y1a[8]])
    with ExitStack() as es:
        wO2 = sb(es, "wO2", [128, 16, 512], BF16); b_wO2 = Buf()
        R2 = sb(es, "R2", [128, NTT, 512], F32); b_R2 = [Buf() for _ in range(NTT)]
        y1p = Pool(es, nc, "y1t", [128, 16, 512], BF16, 2)
        junk8 = sb(es, "junk8", [128, 512], BF16); b_j8 = Buf()
        x1p = Pool(es, nc, "x1t", [128, 512], F32, 3)
        for kq in range(4):
            fw.dma(pool, wO2[:, kq * 4:(kq + 1) * 4, :], w_out_b.rearrange("(kt p) f -> p kt f", p=128)[:, kq * 4:(kq + 1) * 4, :], writes=[b_wO2])
        for tb in range(9):
            n128 = 4 if tb < 8 else 2
            n = n128 * 128
            yt, ytb = y1p.next()
            fw.dma(sp, yt[:, :, 0:n], y1_all[tb].rearrange("(kt p) t -> p kt t", p=128), reads=[d_y1a[tb]], writes=[ytb])
            for j in range(n128):
                tile = tb * 4 + j
                pt, pb = ps_next()
                for kt in range(16):
                    fw.op(pe, lambda e: e.matmul(pt[:, :], lhsT=yt[:, kt, j * 128:(j + 1) * 128], rhs=wO2[:, kt, :],
                                                 start=(kt == 0), stop=(kt == 15)), reads=[ytb, b_wO2], writes=[pb])
                fw.op(act, lambda e: e.activation(out=R2[:, tile, :], in_=pt[:, :], func=AF.Copy), reads=[pb], writes=[b_R2[tile]])
                fw.op(act, lambda e: e.activation(out=junk8[:], in_=pt[:, :], func=AF.Square, accum_out=stat[:, 6, tile:tile + 1]),
                      reads=[pb], writes=[b_j8, b_stat[6]])
        exchange_stats(es, 6, 7, 2)
        for tile in range(NTT):
            rg = rg_of(tile)
            xs, xsb = x1p.next()
            fw.dma(sp, xs[:], x1_scr[tile * 128:(tile + 1) * 128, :], reads=[d_x1], writes=[xsb])
            fw.op(dve, lambda e: e.scalar_tensor_tensor(out=R2[:, tile, :], in0=R2[:, tile, :], scalar=stat[:, 7, tile:tile + 1],
                                                        in1=GG1bc[:, rg, :], op0=ALU.mult, op1=ALU.mult),
                  reads=[b_R2[tile], b_stat[7], b_GG1], writes=[b_R2[tile]])
            fw.op(pool, lambda e: e.tensor_tensor(out=R2[:, tile, :], in0=R2[:, tile, :], in1=xs[:], op=ALU.add),
                  reads=[b_R2[tile], xsb], writes=[b_R2[tile]])
            fw.dma(pool, y_sl[tile * 128:(tile + 1) * 128, :], R2[:, tile, :], reads=[b_R2[tile]], writes=[d_out[0]])
        fw.barrier()
    return nc


_NC_CACHE = {}


def _consts():
    ident = np.eye(128, dtype=np.float32)
    q = np.arange(128)
    tri = (q[None, :] < q[:, None]).astype(np.float32)
    m16 = ((np.arange(16)[None, :]) < (q[:, None] % 16)).astype(np.float32)
    sel = np.zeros((17, 3, 128), np.float32)
    sel[0, 0, :] = 1.0
    for p in range(128):
        sel[1 + p // 16, 1, p] = 1.0
        sel[9 + p // 16, 2, p] = 1.0
    return ident, tri, m16, sel


def make_in_maps(x_prompt, x_sample, c_prompt, c_sample, cache_k, cache_v, state_lru, state_conv,
                 g_pre, g_post, w_ada, b_ada, w_in_a, conv_w, conv_b, w_rgate, b_rgate, w_igate, b_igate,
                 lru_lambda, w_out_a, g_kv, w_kv, w_in_b, w_out_b):
    f = lambda a: np.ascontiguousarray(np.asarray(a, dtype=np.float32))
    ident, tri, m16, sel = _consts()
    in_maps = []
    for c in range(8):
        g, m = c // 4, c % 4
        fs = slice(512 * m, 512 * m + 512)
        cs = slice(1024 * m, 1024 * m + 1024)
        ss = slice(16 * g, 16 * g + 16)
        xg = np.concatenate([x_prompt[g], x_sample[ss].reshape(256, D)], axis=0)
        d = {
            "xg": f(xg),
            "xg_sl": f(xg[:, fs]),
            "cg": f(np.concatenate([c_prompt[g:g + 1], c_sample[ss]], axis=0)),
            "w_ada0": f(w_ada[0][:, 0:4096]),
            "b_ada0": f(b_ada[0][0:4096]),
            "w_adas": f(np.concatenate([w_ada[0][:, 4096 + 512 * m:4096 + 512 * m + 512],
                                        w_ada[1][:, 512 * m:512 * m + 512],
                                        w_ada[1][:, 2048 + 512 * m:2048 + 512 * m + 512],
                                        w_ada[1][:, 4096 + 512 * m:4096 + 512 * m + 512]], axis=1)),
            "b_adas": f(np.concatenate([b_ada[0][4096 + 512 * m:4096 + 512 * m + 512],
                                        b_ada[1][512 * m:512 * m + 512],
                                        b_ada[1][2048 + 512 * m:2048 + 512 * m + 512],
                                        b_ada[1][4096 + 512 * m:4096 + 512 * m + 512]])),
            "g_pre0": f(g_pre[0]),
            "g_sl": f(np.stack([g_post[0][fs], g_pre[1][fs], g_post[1][fs], g_kv[fs]])),
            "w_in_a": f(np.concatenate([w_in_a[0][:, cs], w_in_a[0][:, 4096 + 1024 * m:4096 + 1024 * m + 1024]], axis=1)),
            "conv_w": f(conv_w[0][:, cs]),
            "chp": f(np.stack([conv_b[0][cs], b_rgate[0][cs], b_igate[0][cs], lru_lambda[0][cs]])),
            "w_rg": f(w_rgate[0][4 * m:4 * m + 4]),
            "w_ig": f(w_igate[0][4 * m:4 * m + 4]),
            "st_in": f(np.concatenate([state_lru[0][ss][:, cs], state_conv[0][ss][:, :, cs].reshape(48, 1024)], axis=0)),
            "w_out_a": f(w_out_a[0][:, fs]),
            "w_kv": f(np.concatenate([w_kv[:, fs], w_kv[:, 2048 + 512 * m:2048 + 512 * m + 512]], axis=1)),
            "w_in_b": f(np.concatenate([w_in_b[0][:, fs], w_in_b[0][:, 2048 + 512 * m:2048 + 512 * m + 512]], axis=1)),
            "w_out_b": f(w_out_b[0][:, fs]),
            "cache_k": f(cache_k[ss][:, :, 4 * m:4 * m + 4, :].reshape(16, 4096, 512)),
            "cache_v": f(cache_v[ss][:, :, 4 * m:4 * m + 4, :].reshape(16, 4096, 512)),
            "ident": ident, "tri": tri, "m16": m16, "sel": sel,
        }
        in_maps.append(d)
    return in_maps


def assemble(results):
    y_prompt = np.zeros((2, 4096, D), np.float32)
    y_sample = np.zeros((32, 16, D), np.float32)
    k_prompt = np.zeros((2, 4096, 16, 128), np.float32)
    v_prompt = np.zeros((2, 4096, 16, 128), np.float32)
    k_sample = np.zeros((32, 16, 16, 128), np.float32)
    v_sample = np.zeros((32, 16, 16, 128), np.float32)
    lru_prompt = np.zeros((1, 2, 4096), np.float32)
    lru_sample = np.zeros((1, 32, 4096), np.float32)
    conv_prompt = np.zeros((1, 2, 3, 4096), np.float32)
    conv_sample = np.zeros((1, 32, 3, 4096), np.float32)
    for c in range(8):
        g, m = c // 4, c % 4
        r = results[c]
        fs = slice(512 * m, 512 * m + 512)
        cs = slice(1024 * m, 1024 * m + 1024)
        ss = slice(16 * g, 16 * g + 16)
        y_prompt[g][:, fs] = r["y_sl"][:4096]
        y_sample[ss].reshape(256, D)[:, fs] = r["y_sl"][4096:]
        k_prompt[g].reshape(4096, D)[:, fs] = r["k_sl"][:4096]
        v_prompt[g].reshape(4096, D)[:, fs] = r["v_sl"][:4096]
        k_sample[ss].reshape(256, D)[:, fs] = r["k_sl"][4096:]
        v_sample[ss].reshape(256, D)[:, fs] = r["v_sl"][4096:]
        lru_prompt[0, g, cs] = r["lru_sl"][0]
        lru_sample[0, ss, cs] = r["lru_sl"][1:17]
        cv = r["conv_sl"].reshape(17, 3, 1024)
        conv_prompt[0, g, :, cs] = cv[0]
        conv_sample[0, ss, :, cs] = cv[1:17]
    return (y_prompt, y_sample, k_prompt, v_prompt, k_sample, v_sample,
            lru_prompt, lru_sample, conv_prompt, conv_sample)


def kernel(**inputs):
    inputs = {k: np.asarray(v) for k, v in inputs.items()}
    in_maps = make_in_maps(**inputs)
    if "nc" not in _NC_CACHE:
        _NC_CACHE["nc"] = build_nc()
    names = _NC_CACHE["in_names"]
    in_maps = [{k: d[k] for k in names} for d in in_maps]
    res = run_bass_kernel_spmd(_NC_CACHE["nc"], in_maps, core_ids=list(range(8)))
    return assemble(res.results)
```
